# Optimizing a Trainium2 kernel written in Bass

```python
import math
import jax, jax.numpy as jnp
from jax import lax
import numpy as np

D_MODEL = 2048
BATCH = 8
SEQ = 2048
DEPTH = 2

N_MIXERS = 2
N_SSD_LAYERS = (DEPTH + 1) // 2
N_DSA_LAYERS = DEPTH // 2
DN_ALPHA = (2.0 * DEPTH) ** 0.25
DN_BETA = (8.0 * DEPTH) ** -0.25
LN_EPS = 1e-5
RMS_EPS = 1e-6

SSD_EXPAND = 2
SSD_D_INNER = SSD_EXPAND * D_MODEL
SSD_HEAD_DIM = 64
SSD_N_HEADS = SSD_D_INNER // SSD_HEAD_DIM
SSD_N_GROUPS = 8
SSD_HEADS_PER_GROUP = SSD_N_HEADS // SSD_N_GROUPS
SSD_D_STATE = 128
SSD_D_CONV = 4
SSD_CHUNK = 256
SSD_CONV_DIM = SSD_D_INNER + 2 * SSD_N_GROUPS * SSD_D_STATE
SSD_IN_DIM = SSD_D_INNER + SSD_CONV_DIM + SSD_N_HEADS
SSD_NORM_GROUP = SSD_D_INNER // SSD_N_GROUPS

DSA_N_HEADS = 32
DSA_HEAD_DIM = 128
DSA_WIDTH = DSA_N_HEADS * DSA_HEAD_DIM
DSA_Q_RANK = 512
DSA_KV_RANK = 256
IDX_N_HEADS = 16
IDX_HEAD_DIM = 64
IDX_TOPK = 256
Q_BLOCK = 128
DSA_IN_DIM = DSA_Q_RANK + DSA_KV_RANK + IDX_HEAD_DIM + IDX_N_HEADS + DSA_WIDTH

kernel_name = 'hybrid_ssd_dsa_deepnorm'


def layer_norm(x, g, b):
    xf = x.astype(jnp.float32)
    mu = jnp.mean(xf, -1, keepdims=True)
    var = jnp.mean(jnp.square(xf - mu), -1, keepdims=True)
    return ((xf - mu) * lax.rsqrt(var + LN_EPS) * g + b).astype(x.dtype)


def rms_norm(x, g):
    xf = x.astype(jnp.float32)
    y = xf * lax.rsqrt(jnp.mean(xf * xf, -1, keepdims=True) + RMS_EPS) * g
    return y.astype(x.dtype)


def causal_depthwise_conv(u, w, b):
    k, c = w.shape
    out = lax.conv_general_dilated(u, w[:, None, :], window_strides=(1,), padding=[(k - 1, 0)],
                                   dimension_numbers=('NWC', 'WIO', 'NWC'), feature_group_count=c)
    return out + b


def segsum(a):
    cs = jnp.cumsum(a, -1)
    t = a.shape[-1]
    diff = cs[..., :, None] - cs[..., None, :]
    mask = jnp.tril(jnp.ones((t, t), dtype=bool))
    return jnp.where(mask, diff, -jnp.inf)


def ssd_chunked(xs, dt, a, bm, cm, chunk):
    b, L, g, r, p = xs.shape
    n = bm.shape[-1]
    c = L // chunk
    xdt = (xs * dt[..., None]).reshape(b, c, chunk, g, r, p)
    bc = bm.reshape(b, c, chunk, g, n)
    cc = cm.reshape(b, c, chunk, g, n)
    adt = (dt * a).reshape(b, c, chunk, g, r).transpose(0, 3, 4, 1, 2)
    a_cs = jnp.cumsum(adt, -1)
    decay = jnp.exp(segsum(adt))
    cb = jnp.einsum('bclgn,bcsgn->bgcls', cc, bc)
    y_diag = jnp.einsum('bgcls,bgrcls,bcsgrp->bclgrp', cb, decay, xdt)
    decay_states = jnp.exp(a_cs[..., -1:] - a_cs)
    states = jnp.einsum('bclgn,bgrcl,bclgrp->bcgrpn', bc, decay_states, xdt)
    states = jnp.concatenate([jnp.zeros_like(states[:, :1]), states], 1)
    chunk_a = jnp.pad(a_cs[..., -1], ((0, 0), (0, 0), (0, 0), (1, 0)))
    chunk_decay = jnp.exp(segsum(chunk_a))
    new_states = jnp.einsum('bgrzc,bcgrpn->bzgrpn', chunk_decay, states)
    prev_states = new_states[:, :-1]
    y_off = jnp.einsum('bclgn,bcgrpn,bgrcl->bclgrp', cc, prev_states, jnp.exp(a_cs))
    return (y_diag + y_off).reshape(b, L, g, r, p)


def ssd_mixer(x, w_in, conv_w, conv_b, dt_bias, a_log, d_skip, norm_g, w_out):
    b, L, _ = x.shape
    proj = x @ w_in
    z, xbc, dt = jnp.split(proj, [SSD_D_INNER, SSD_D_INNER + SSD_CONV_DIM], -1)
    xbc = jax.nn.silu(causal_depthwise_conv(xbc, conv_w, conv_b))
    xs, bm, cm = jnp.split(xbc, [SSD_D_INNER, SSD_D_INNER + SSD_N_GROUPS * SSD_D_STATE], -1)
    g, r, p = SSD_N_GROUPS, SSD_HEADS_PER_GROUP, SSD_HEAD_DIM
    xs = xs.reshape(b, L, g, r, p)
    bm = bm.reshape(b, L, g, SSD_D_STATE)
    cm = cm.reshape(b, L, g, SSD_D_STATE)
    dt = jax.nn.softplus(dt.astype(jnp.float32) + dt_bias).reshape(b, L, g, r)
    a = -jnp.exp(a_log.astype(jnp.float32)).reshape(g, r)
    chunk = math.gcd(SSD_CHUNK, L)
    y = ssd_chunked(xs, dt, a, bm, cm, chunk)
    y = y + d_skip.reshape(g, r)[..., None] * xs
    y = y.reshape(b, L, SSD_D_INNER).astype(jnp.float32)
    yg = (y * jax.nn.silu(z.astype(jnp.float32))).reshape(b, L, g, SSD_NORM_GROUP)
    yg = yg * lax.rsqrt(jnp.mean(yg * yg, -1, keepdims=True) + RMS_EPS)
    yg = yg.reshape(b, L, SSD_D_INNER) * norm_g
    return yg.astype(x.dtype) @ w_out


def dsa_mixer(x, w_in, q_norm_g, kv_norm_g, w_uq, w_uk, w_uv, w_idx_q, w_out):
    b, L, _ = x.shape
    proj = x @ w_in
    s1 = DSA_Q_RANK
    s2 = s1 + DSA_KV_RANK
    s3 = s2 + IDX_HEAD_DIM
    s4 = s3 + IDX_N_HEADS
    c_q, c_kv, k_idx, w_idx, gate = jnp.split(proj, [s1, s2, s3, s4], -1)
    c_q = rms_norm(c_q, q_norm_g)
    c_kv = rms_norm(c_kv, kv_norm_g)
    q = (c_q @ w_uq).reshape(b, L, DSA_N_HEADS, DSA_HEAD_DIM)
    q_lat = jnp.einsum('bthd,hcd->bthc', q, w_uk)
    q_idx = (c_q @ w_idx_q).reshape(b, L, IDX_N_HEADS, IDX_HEAD_DIM)
    w_idx = w_idx.astype(jnp.float32) * (IDX_N_HEADS ** -0.5)
    k_idx_f = k_idx.astype(jnp.float32)
    k_top = min(IDX_TOPK, L // 4)
    nblk = L // Q_BLOCK
    key_pos = jnp.arange(L)
    idx_scale = IDX_HEAD_DIM ** -0.5
    attn_scale = DSA_HEAD_DIM ** -0.5

    def to_blocks(t):
        return t.reshape((b, nblk, Q_BLOCK) + t.shape[2:]).swapaxes(0, 1)

    def block(args):
        qi, wi, ql, tpos = args
        s = jnp.einsum('bthd,bsd->bths', qi.astype(jnp.float32), k_idx_f)
        score = jnp.einsum('bths,bth->bts', jax.nn.relu(s), wi) * idx_scale
        causal = key_pos[None, :] <= tpos[:, None]
        score = jnp.where(causal[None], score, -jnp.inf)
        _, sel = lax.top_k(score, k_top)
        kv_sel = jax.vmap(lambda c, i: c[i])(c_kv, sel)
        logits = jnp.einsum('bthc,btkc->bthk', ql, kv_sel).astype(jnp.float32) * attn_scale
        valid = sel <= tpos[None, :, None]
        logits = jnp.where(valid[:, :, None, :], logits, -jnp.inf)
        pr = jax.nn.softmax(logits, -1).astype(kv_sel.dtype)
        return jnp.einsum('bthk,btkc->bthc', pr, kv_sel)

    pos = jnp.arange(L).reshape(nblk, Q_BLOCK)
    o_lat = lax.map(block, (to_blocks(q_idx), to_blocks(w_idx), to_blocks(q_lat), pos))
    o_lat = o_lat.swapaxes(0, 1).reshape(b, L, DSA_N_HEADS, DSA_KV_RANK)
    o = jnp.einsum('bthc,hcv->bthv', o_lat, w_uv).reshape(b, L, DSA_WIDTH)
    return (o * jax.nn.silu(gate)) @ w_out


def setup_inputs(seed: int = 0) -> dict:
    key = jax.random.key(seed)
    ks = jax.random.split(key, 24)
    f32 = jnp.float32

    def nrm(k, shape, fan_in, scale=1.0):
        return jax.random.normal(k, shape, f32) * (scale * fan_in ** -0.5)

    x = jax.random.normal(ks[0], (BATCH, SEQ, D_MODEL), f32)
    ns, nd = N_SSD_LAYERS, N_DSA_LAYERS
    ssd_w_in = nrm(ks[1], (ns, D_MODEL, SSD_IN_DIM), D_MODEL)
    ssd_conv_w = nrm(ks[2], (ns, SSD_D_CONV, SSD_CONV_DIM), SSD_D_CONV)
    ssd_conv_b = 0.02 * jax.random.normal(ks[3], (ns, SSD_CONV_DIM), f32)
    u = jax.random.uniform(ks[4], (ns, SSD_N_HEADS), f32)
    dt0 = jnp.exp(u * (math.log(0.1) - math.log(0.001)) + math.log(0.001))
    ssd_dt_bias = dt0 + jnp.log(-jnp.expm1(-dt0))
    ssd_a_log = jnp.log(jax.random.uniform(ks[5], (ns, SSD_N_HEADS), f32, 1.0, 16.0))
    ssd_d_skip = 1.0 + 0.02 * jax.random.normal(ks[6], (ns, SSD_N_HEADS), f32)
    ssd_norm_g = 1.0 + 0.02 * jax.random.normal(ks[7], (ns, SSD_D_INNER), f32)
    ssd_w_out = nrm(ks[8], (ns, SSD_D_INNER, D_MODEL), SSD_D_INNER, DN_BETA)
    dsa_w_in = nrm(ks[9], (nd, D_MODEL, DSA_IN_DIM), D_MODEL)
    dsa_q_norm_g = 1.0 + 0.02 * jax.random.normal(ks[10], (nd, DSA_Q_RANK), f32)
    dsa_kv_norm_g = 1.0 + 0.02 * jax.random.normal(ks[11], (nd, DSA_KV_RANK), f32)
    dsa_w_uq = nrm(ks[12], (nd, DSA_Q_RANK, DSA_WIDTH), DSA_Q_RANK)
    dsa_w_uk = nrm(ks[13], (nd, DSA_N_HEADS, DSA_KV_RANK, DSA_HEAD_DIM), DSA_KV_RANK)
    dsa_w_uv = nrm(ks[14], (nd, DSA_N_HEADS, DSA_KV_RANK, DSA_HEAD_DIM), DSA_KV_RANK)
    dsa_w_idx_q = nrm(ks[15], (nd, DSA_Q_RANK, IDX_N_HEADS * IDX_HEAD_DIM), DSA_Q_RANK)
    dsa_w_out = nrm(ks[16], (nd, DSA_WIDTH, D_MODEL), DSA_WIDTH, DN_BETA)
    ln_g = 1.0 + 0.02 * jax.random.normal(ks[17], (DEPTH, D_MODEL), f32)
    ln_b = 0.02 * jax.random.normal(ks[18], (DEPTH, D_MODEL), f32)
    return {'x': x, 'ssd_w_in': ssd_w_in, 'ssd_conv_w': ssd_conv_w, 'ssd_conv_b': ssd_conv_b,
            'ssd_dt_bias': ssd_dt_bias, 'ssd_a_log': ssd_a_log, 'ssd_d_skip': ssd_d_skip,
            'ssd_norm_g': ssd_norm_g, 'ssd_w_out': ssd_w_out, 'dsa_w_in': dsa_w_in,
            'dsa_q_norm_g': dsa_q_norm_g, 'dsa_kv_norm_g': dsa_kv_norm_g, 'dsa_w_uq': dsa_w_uq,
            'dsa_w_uk': dsa_w_uk, 'dsa_w_uv': dsa_w_uv, 'dsa_w_idx_q': dsa_w_idx_q,
            'dsa_w_out': dsa_w_out, 'ln_g': ln_g, 'ln_b': ln_b}


def reference(x, ssd_w_in, ssd_conv_w, ssd_conv_b, ssd_dt_bias, ssd_a_log, ssd_d_skip,
              ssd_norm_g, ssd_w_out, dsa_w_in, dsa_q_norm_g, dsa_kv_norm_g, dsa_w_uq,
              dsa_w_uk, dsa_w_uv, dsa_w_idx_q, dsa_w_out, ln_g, ln_b):
    for i in range(DEPTH):
        j = i // N_MIXERS
        if i % N_MIXERS == 0:
            h = ssd_mixer(x, ssd_w_in[j], ssd_conv_w[j], ssd_conv_b[j], ssd_dt_bias[j],
                          ssd_a_log[j], ssd_d_skip[j], ssd_norm_g[j], ssd_w_out[j])
        else:
            h = dsa_mixer(x, dsa_w_in[j], dsa_q_norm_g[j], dsa_kv_norm_g[j], dsa_w_uq[j],
                          dsa_w_uk[j], dsa_w_uv[j], dsa_w_idx_q[j], dsa_w_out[j])
        x = layer_norm(DN_ALPHA * x + h, ln_g[i], ln_b[i])
    return x
```

```python
import numpy as np
from contextlib import ExitStack
import concourse.bass as bass
import concourse.mybir as mybir
from concourse.bass_utils import run_bass_kernel_spmd

F32 = mybir.dt.float32
BF16 = mybir.dt.bfloat16
ALU = mybir.AluOpType
AF = mybir.ActivationFunctionType

T = 2048
D = 2048
ALPHA = 4.0 ** 0.25
LN_EPS = 1e-5
RMS_EPS = 1e-6
NEG = -30000.0
USE_ACT_BISECT = True


class Buf:
    __slots__ = ("name", "w", "r")

    def __init__(self, name=""):
        self.name = name
        self.w = None
        self.r = []


class Op:
    __slots__ = ("eng", "fn", "deps", "signal", "sem", "val", "is_dma", "ndma", "prev")

    def __init__(self, eng, fn, is_dma=False, ndma=1):
        self.eng = eng
        self.fn = fn
        self.deps = []
        self.signal = False
        self.sem = None
        self.val = None
        self.is_dma = is_dma
        self.ndma = ndma
        self.prev = 0


class Sched:
    ENGS = ("pe", "act", "dve", "pool", "sp")

    def __init__(self, nc, n_dma_sems=16):
        self.nc = nc
        self.ops = {e: [] for e in self.ENGS}
        self.n_dma_sems = n_dma_sems
        self.out_ops = []
        self.dmas = []

    def _add(self, op, reads, writes):
        deps = set()
        for b in reads:
            if b.w is not None:
                deps.add(b.w)
        for b in writes:
            if b.w is not None:
                deps.add(b.w)
            for r in b.r:
                deps.add(r)
        for d in deps:
            if d is op:
                continue
            if (not op.is_dma) and (not d.is_dma) and d.eng == op.eng:
                if op.eng == "pe":
                    continue
            d.signal = True
            op.deps.append(d)
        for b in reads:
            b.r.append(op)
        for b in writes:
            b.w = op
            b.r = []
        self.ops[op.eng].append(op)
        return op

    def op(self, eng, fn, reads=(), writes=()):
        return self._add(Op(eng, fn), list(reads), list(writes))

    def dma(self, eng, fn, reads=(), writes=(), n=1):
        o = Op(eng, fn, is_dma=True, ndma=n)
        o.signal = True
        self.dmas.append(o)
        return self._add(o, list(reads), list(writes))

    def fence(self):
        front = []
        for e in self.ENGS:
            for o in reversed(self.ops[e]):
                if o.fn is not None and not o.is_dma:
                    front.append(o)
                    break
        front += self.dmas
        self.dmas = []
        for e in self.ENGS:
            p = Op(e, None)
            for d in front:
                if d.eng == e and e == "pe" and not d.is_dma:
                    continue
                d.signal = True
                p.deps.append(d)
            self.ops[e].append(p)

    def emit(self):
        nc = self.nc
        with ExitStack() as es:
            esem = {e: es.enter_context(nc.semaphore("s_" + e)) for e in self.ENGS}
            dsems = {e: [es.enter_context(nc.semaphore("d_%s%d" % (e, i))) for i in range(self.n_dma_sems)]
                     for e in ("sp", "pool", "act")}
            for e in self.ENGS:
                cnt = 0
                dcnt = [0] * self.n_dma_sems
                k = 0
                for o in self.ops[e]:
                    if o.is_dma:
                        o.sem = dsems[e][k]
                        o.prev = dcnt[k]
                        dcnt[k] += 16 * o.ndma
                        o.val = dcnt[k]
                        k = (k + 1) % self.n_dma_sems
                    elif o.signal:
                        cnt += 1
                        o.sem = esem[e]
                        o.val = cnt
            handles = {"pe": "tensor", "act": "scalar", "dve": "vector", "pool": "gpsimd", "sp": "sync"}
            block = es.enter_context(nc.Block())
            out_ops = self.out_ops
            for e in self.ENGS:
                ops = self.ops[e]

                def body(h, ops=ops, e=e):
                    seen = {}
                    for o in ops:
                        deps = o.deps
                        if o.fn is None:
                            best = {}
                            for d in deps:
                                if id(d.sem) not in best or best[id(d.sem)].val < d.val:
                                    best[id(d.sem)] = d
                            deps = list(best.values())
                        for d in deps:
                            key = id(d.sem)
                            if seen.get(key, 0) >= d.val:
                                continue
                            h.wait_ge(d.sem, d.val)
                            seen[key] = d.val
                        if o.is_dma:
                            if o.prev > 0 and seen.get(id(o.sem), 0) < o.prev:
                                h.wait_ge(o.sem, o.prev)
                                seen[id(o.sem)] = o.prev
                            r = o.fn(h)
                            if not isinstance(r, (list, tuple)):
                                r = [r]
                            assert len(r) == o.ndma
                            for ins in r:
                                ins.then_inc(o.sem, 16)
                        elif o.fn is not None:
                            ins = o.fn(h)
                            if o.signal:
                                ins.then_inc(o.sem, 1)
                    if e == "sp":
                        for o in out_ops:
                            h.wait_ge(o.sem, o.val)

                getattr(block, handles[e])(body)


class Ring:
    def __init__(self, tiles):
        self.tiles = tiles
        self.bufs = [Buf() for _ in tiles]
        self.i = 0

    def next(self):
        k = self.i % len(self.tiles)
        self.i += 1
        return self.tiles[k], self.bufs[k]


def build(upto="all"):
    nc = bass.Bass("TRN2", target_bir_lowering=False)
    S = Sched(nc)

    def din(name, shape):
        return nc.dram_tensor(name, shape, F32, kind="ExternalInput").ap()

    xT_d = din("xT", [D, T])
    w_in0 = din("w_in0", [D, 10304])
    convw_d = din("convw", [128, 48 * 4])
    convb_d = din("convb", [128, 48])
    dtb_d = din("dtb", [128, 64])
    alog_d = din("alog", [128, 64])
    dskip_d = din("dskip", [128, 32])
    normg_d = din("normg", [128, 32])
    w_out0 = din("w_out0", [4096, D])
    lng_d = din("lng", [128, 32])
    lnb_d = din("lnb", [128, 32])
    w_in1 = din("w_in1", [D, 4944])
    qng_d = din("qng", [128, 4])
    kvng_d = din("kvng", [128, 2])
    w_uq = din("w_uq", [512, 4096])
    w_ukT = din("w_ukT", [32 * 128, 256])
    w_uv = din("w_uv", [32 * 256, 128])
    w_idxq = din("w_idxq", [512, 1024])
    w_out1 = din("w_out1", [4096, D])
    outT_d = nc.dram_tensor("outT", [D, T], F32, kind="ExternalOutput").ap()

    ygT_d = nc.dram_tensor("ygT", [4096, T], BF16).ap()
    x1T_d = nc.dram_tensor("x1T", [D, T], F32).ap()
    gateT_d = nc.dram_tensor("gateT", [4096, T], BF16).ap()
    wsc_h = [nc.dram_tensor("wsc%d" % l, [16 * 128, 16 * 256], BF16) for l in range(2)]
    b_wsc = [Buf("wsc0"), Buf("wsc1")]
    csT_h = nc.dram_tensor("csTd", [64, T], F32)
    csT_d = csT_h.ap()
    b_csTd = Buf("csTd")
    b_ygT, b_x1T, b_gateT, b_outT = Buf("ygT"), Buf("x1T"), Buf("gateT"), Buf("outT")

    with ExitStack() as top:
        _cnt = [0]

        def sb(name, shape, dt=F32, es=top):
            _cnt[0] += 1
            return es.enter_context(nc.sbuf_tensor("sb%d_%s" % (_cnt[0], name), shape, dt))

        def mm(out, lhsT, rhs, start, stop, rd, wr):
            S.op("pe", lambda h: h.matmul(out, lhsT=lhsT, rhs=rhs, start=start, stop=stop), rd, wr)

        def tr(out, in_, idn, rd, wr):
            S.op("pe", lambda h: h.transpose(out=out, in_=in_, identity=idn), rd, wr)

        def act(out, in_, func, rd, wr, bias=None, scale=None, eng="act"):
            kw = {}
            if bias is not None:
                kw["bias"] = bias
            if scale is not None:
                kw["scale"] = scale
            S.op(eng, lambda h: h.activation(out=out, in_=in_, func=func, **kw), rd, wr)

        def amul(out, in_, c, rd, wr):
            S.op("act", lambda h: h.mul(out, in_, c), rd, wr)

        def acopy(out, in_, rd, wr):
            S.op("act", lambda h: h.copy(out, in_), rd, wr)

        def tt(out, in0, in1, op, rd, wr, eng="dve"):
            S.op(eng, lambda h: h.tensor_tensor(out=out, in0=in0, in1=in1, op=op), rd, wr)

        def ts(out, in0, s1, s2, op0, op1, rd, wr, accum=None, eng="dve"):
            kw = {}
            if op1 is not None:
                kw["op1"] = op1
            if accum is not None:
                kw["accum_out"] = accum
            S.op(eng, lambda h: h.tensor_scalar(out=out, in0=in0, scalar1=s1, scalar2=s2, op0=op0, **kw), rd, wr)

        def stt(out, in0, scalar, in1, op0, op1, rd, wr, eng="dve"):
            S.op(eng, lambda h: h.scalar_tensor_tensor(out=out, in0=in0, scalar=scalar, in1=in1, op0=op0, op1=op1), rd, wr)

        def vcopy(out, in_, rd, wr, eng="dve"):
            S.op(eng, lambda h: h.tensor_copy(out, in_), rd, wr)

        def recip(out, in_, rd, wr):
            S.op("dve", lambda h: h.reciprocal(out, in_), rd, wr)

        def memset(ap, val, wr, eng="pool"):
            S.op(eng, lambda h: h.memset(ap, val), (), wr)

        def dma(eng, out, in_, rd, wr):
            return S.dma(eng, lambda h: h.dma_start(out=out, in_=in_), rd, wr)

        psb = [top.enter_context(nc.psum_tensor("ps%d" % i, [128, 512], F32)) for i in range(8)]
        psbuf = [Buf("ps%d" % i) for i in range(8)]

        b_c = Buf("consts")
        identf = sb("identf", [128, 128])
        ident = sb("ident", [128, 128], BF16)
        U128 = sb("U128", [128, 128])
        onesf = sb("onesf", [128, 128])
        onesb = sb("onesb", [128, 128], BF16)
        memset(identf[:], 0.0, [b_c])
        S.op("pool", lambda h: h.affine_select(out=identf[:], in_=identf[:], pattern=[[-1, 128]], compare_op=ALU.not_equal,
                                                fill=1.0, base=0, channel_multiplier=1), [b_c], [b_c])
        memset(onesf[:], 1.0, [b_c])
        memset(onesb[:], 1.0, [b_c])
        memset(U128[:], 1.0, [b_c])
        S.op("pool", lambda h: h.affine_select(out=U128[:], in_=U128[:], pattern=[[1, 128]], compare_op=ALU.is_ge,
                                                fill=0.0, base=0, channel_multiplier=-1), [b_c], [b_c])
        vcopy(ident[:], identf[:], [b_c], [b_c], eng="pool")

        b_par = Buf("params")
        convw = sb("convw", [128, 48 * 4])
        convb = sb("convb", [128, 48])
        dtb = sb("dtb", [128, 64])
        alog = sb("alog", [128, 64])
        dskip = sb("dskip", [128, 32])
        normg = sb("normg", [128, 32])
        lng = sb("lng", [128, 32])
        lnb = sb("lnb", [128, 32])
        qng = sb("qng", [128, 4])
        kvng = sb("kvng", [128, 2])
        for t_, d_ in ((convw, convw_d), (convb, convb_d), (dtb, dtb_d), (alog, alog_d), (dskip, dskip_d),
                       (normg, normg_d), (lng, lng_d), (lnb, lnb_d), (qng, qng_d), (kvng, kvng_d)):
            dma("sp", t_[:], d_[:, :], [], [b_par])

        NW = 2
        wring = Ring([sb("wt%d" % i, [128, 16, 256], BF16) for i in range(NW)])

        xstk = ExitStack()
        xTb = sb("xTb", [128, 16, T], BF16, xstk)
        b_xTb = Buf("xTb")
        for c in range(16):
            dma("pool", xTb[:, c, :], xT_d[c * 128:(c + 1) * 128, :], [], [b_xTb])

        def load_w(src, kc, ncols):
            wt, wb = wring.next()
            dma("pool", wt[:, 0:kc, 0:ncols], src.rearrange("(c p) n -> p c n", p=128), [], [wb])
            return wt, wb

        precast_jobs = []
        for l_, wsrc in ((0, w_out0), (1, w_out1)):
            for dp in range(8):
                for kh in range(2):
                    precast_jobs.append((l_, wsrc, dp, kh))
        precast_pos = [0]

        def precast_step(layer_limit):
            if precast_pos[0] >= len(precast_jobs):
                return
            l_, wsrc, dp, kh = precast_jobs[precast_pos[0]]
            if l_ > layer_limit:
                return
            precast_pos[0] += 1
            ti = dp * 2 + kh
            dst = wsc_h[l_].ap()[ti * 128:(ti + 1) * 128, :].rearrange("p (c n) -> p c n", c=16)
            src = wsrc[kh * 2048:(kh + 1) * 2048, dp * 256:(dp + 1) * 256].rearrange("(c p) n -> p c n", p=128)
            dma("pool", dst, src, [], [b_wsc[l_]])

        def load_wsc(l_, dp, kh):
            wt, wb = wring.next()
            ti = dp * 2 + kh
            dma("pool", wt[:, :, :], wsc_h[l_].ap()[ti * 128:(ti + 1) * 128, :].rearrange("p (c n) -> p c n", c=16),
                [b_wsc[l_]], [wb])
            return wt, wb

        def outproj_ln(es, actT_d, b_act, w_d, res_d, b_res, layer, out_d, b_out, is_final, write_bf):
            nbuf = 2 if layer == 1 else 1
            Ar = Ring([sb("opA%d" % i, [128, 32, 512], BF16, es) for i in range(nbuf)])
            xrr = Ring([sb("opX%d" % i, [128, 16, 512], F32, es) for i in range(nbuf)])

            def preload(R):
                rs_ = slice(R * 512, (R + 1) * 512)
                A_, bA_ = Ar.next()
                for hh in range(2):
                    dma("sp", A_[:, hh * 16:(hh + 1) * 16, :],
                        actT_d[hh * 2048:(hh + 1) * 2048, rs_].rearrange("(c p) t -> p c t", p=128), [b_act], [bA_])
                x_, bx_ = xrr.next()
                dma("sp", x_[:, :, :], res_d[:, rs_].rearrange("(c p) t -> p c t", p=128), [b_res], [bx_])
                return A_, bA_, x_, bx_
            sqr = Ring([sb("opsq%d" % i, [128, 512], F32, es) for i in range(2)])
            o32r = Ring([sb("opo%d" % i, [128, 512], F32, es) for i in range(2)])
            xnr = Ring([sb("opxn%d" % i, [128, 512], F32, es) for i in range(2)])
            mean = sb("opmean", [128, 512], F32, es)
            msq = sb("opmsq", [128, 512], F32, es)
            var = sb("opvar", [128, 512], F32, es)
            rstd = sb("oprstd", [128, 512], F32, es)
            nb = sb("opnb", [128, 512], F32, es)
            b_st = Buf()
            for _ in range(32):
                precast_step(layer)
            pring = Ring([psb[0], psb[1]])
            pring.bufs = [psbuf[0], psbuf[1]]
            P1, P2 = psb[2], psb[3]
            nxt_ld = preload(0)
            for R in range(4):
                rs = slice(R * 512, (R + 1) * 512)
                A, b_A, xr, b_xr = nxt_ld
                if nbuf == 2 and R < 3:
                    nxt_ld = preload(R + 1)
                for dp in range(8):
                    w0, wb0 = load_wsc(layer, dp, 0)
                    w1, wb1 = load_wsc(layer, dp, 1)
                    for dq in range(2):
                        dch = dp * 2 + dq
                        ps, pb = pring.next()
                        for kk in range(32):
                            wt, wb = (w0, wb0) if kk < 16 else (w1, wb1)
                            mm(ps[:, :], wt[:, kk % 16, dq * 128:(dq + 1) * 128], A[:, kk, :], kk == 0, kk == 31,
                               [wb, b_A], [pb])
                        stt(xr[:, dch, :], xr[:, dch, :], ALPHA, ps[:, :], ALU.mult, ALU.add, [b_xr, pb], [b_xr])
                        sq, bsq = sqr.next()
                        act(sq[:], xr[:, dch, :], AF.Square, [b_xr], [bsq])
                        mm(P1[:, :], onesf[:], xr[:, dch, :], dch == 0, dch == 15, [b_c, b_xr], [psbuf[2]])
                        mm(P2[:, :], onesf[:], sq[:], dch == 0, dch == 15, [b_c, bsq], [psbuf[3]])
                amul(mean[:], P1[:, :], 1.0 / D, [psbuf[2]], [b_st])
                tt(msq[:], mean[:], mean[:], ALU.mult, [b_st], [b_st])
                stt(var[:], P2[:, :], 1.0 / D, msq[:], ALU.mult, ALU.subtract, [psbuf[3], b_st], [b_st])
                ts(var[:], var[:], LN_EPS, None, ALU.add, None, [b_st], [b_st])
                recip(var[:], var[:], [b_st], [b_st])
                act(rstd[:], var[:], AF.Sqrt, [b_st], [b_st])
                stt(nb[:], mean[:], -1.0, rstd[:], ALU.mult, ALU.mult, [b_st], [b_st])
                for dch in range(16):
                    xn, bxn = xnr.next()
                    tt(xn[:], xr[:, dch, :], rstd[:], ALU.mult, [b_xr, b_st], [bxn])
                    tt(xn[:], xn[:], nb[:], ALU.add, [bxn, b_st], [bxn])
                    o32, bo = o32r.next()
                    col = layer * 16 + dch
                    act(o32[:], xn[:], AF.Identity, [bxn, b_par], [bo], bias=lnb[:, col:col + 1], scale=lng[:, col:col + 1])
                    if write_bf:
                        act(xTb[:, dch, rs], xn[:], AF.Identity, [bxn, b_par], [b_xTb], bias=lnb[:, col:col + 1], scale=lng[:, col:col + 1])
                    o = dma("sp", out_d[dch * 128:(dch + 1) * 128, rs], o32[:], [bo], [b_out])
                    if is_final:
                        S.out_ops.append(o)
                if nbuf == 1 and R < 3:
                    nxt_ld = preload(R + 1)

        with ExitStack() as l0:
            def sb0(name, shape, dt=F32):
                return sb(name, shape, dt, l0)

            dt_tok = sb0("dt_tok", [128, 16, 64])
            negcs = sb0("negcs", [128, 16, 64])
            dtw = sb0("dtw", [128, 16, 64])
            etot = sb0("etot", [128, 8, 64])
            b_dt = Buf("dt")
            with ExitStack() as l0a:
                wdt = sb("wdt", [128, 16, 64], BF16, l0a)
                adt = sb("adt", [128, 16, 64], F32, l0a)
                cstr = Ring([sb("cst%d" % i, [64, 256], F32, l0a) for i in range(2)])
                b_wdt = Buf()
                tmpA = sb("tmpA", [128, 16, 64], F32, l0a)
                ea = sb("ea", [128, 64], F32, l0a)
                b_tmp = Buf()
                dma("pool", wdt[:, :, :], w_in0[:, 10240:10304].rearrange("(c p) n -> p c n", p=128), [], [b_wdt])
                for j in range(16):
                    bank = psb[j // 8]
                    jj = j % 8
                    for k in range(16):
                        mm(bank[:, jj * 64:(jj + 1) * 64], xTb[:, k, j * 128:(j + 1) * 128], wdt[:, k, :], k == 0, k == 15,
                           [b_xTb, b_wdt], [psbuf[j // 8]])
                for hf in range(2):
                    tt(tmpA[:, hf * 8:(hf + 1) * 8, :], psb[hf][:, :].rearrange("p (a b) -> p a b", a=8),
                       dtb[:].unsqueeze(1).to_broadcast([128, 8, 64]), ALU.add, [psbuf[hf], b_par], [b_tmp])
                act(tmpA[:], tmpA[:], AF.Exp, [b_tmp], [b_tmp])
                act(dt_tok[:], tmpA[:], AF.Ln, [b_tmp], [b_dt], bias=1.0)
                act(ea[:], alog[:], AF.Exp, [b_par], [b_tmp])
                stt(adt[:], dt_tok[:], -1.0, ea[:].unsqueeze(1).to_broadcast([128, 16, 64]), ALU.mult, ALU.mult,
                    [b_dt, b_tmp], [b_dt])
                for b in range(16):
                    bank = psb[2 + b // 8]
                    bb = b % 8
                    o_ = bank[:, bb * 64:(bb + 1) * 64]
                    if b % 2 == 0:
                        mm(o_, U128[:], adt[:, b, :], True, True, [b_c, b_dt], [psbuf[2 + b // 8]])
                    else:
                        mm(o_, onesf[:], adt[:, b - 1, :], True, False, [b_c, b_dt], [psbuf[2 + b // 8]])
                        mm(o_, U128[:], adt[:, b, :], False, True, [b_c, b_dt], [psbuf[2 + b // 8]])
                for c in range(8):
                    o_ = psb[4][:, c * 64:(c + 1) * 64]
                    mm(o_, onesf[:], adt[:, 2 * c, :], True, False, [b_c, b_dt], [psbuf[4]])
                    mm(o_, onesf[:], adt[:, 2 * c + 1, :], False, True, [b_c, b_dt], [psbuf[4]])
                for hf in range(2):
                    amul(negcs[:, hf * 8:(hf + 1) * 8, :], psb[2 + hf][:, :].rearrange("p (a b) -> p a b", a=8), -1.0,
                         [psbuf[2 + hf]], [b_dt])
                tt(tmpA[:].rearrange("p (c e) h -> p c e h", e=2), negcs[:].rearrange("p (c e) h -> p c e h", e=2),
                   psb[4][:, :].rearrange("p (c h) -> p c h", c=8).unsqueeze(2).to_broadcast([128, 8, 2, 64]),
                   ALU.add, [b_dt, psbuf[4]], [b_tmp])
                act(tmpA[:], tmpA[:], AF.Exp, [b_tmp], [b_tmp])
                tt(dtw[:], dt_tok[:], tmpA[:], ALU.mult, [b_dt, b_tmp], [b_dt])
                act(etot[:], psb[4][:, :].rearrange("p (c h) -> p c h", c=8), AF.Exp, [psbuf[4]], [b_dt])
                for c in range(8):
                    bk = 5 + c % 2
                    ps = psb[bk]
                    mm(ps[0:64, 0:128], adt[:, 2 * c, :], U128[:], True, True, [b_dt, b_c], [psbuf[bk]])
                    mm(ps[0:64, 128:256], adt[:, 2 * c, :], onesf[:], True, False, [b_dt, b_c], [psbuf[bk]])
                    mm(ps[0:64, 128:256], adt[:, 2 * c + 1, :], U128[:], False, True, [b_dt, b_c], [psbuf[bk]])
                    cst, bcst = cstr.next()
                    amul(cst[:], ps[0:64, 0:256], -1.0, [psbuf[bk]], [bcst])
                    dma("sp", csT_d[:, c * 256:(c + 1) * 256], cst[:], [bcst], [b_csTd])

            S.fence()
            with ExitStack() as l0b:
                def sbb(name, shape, dt=F32):
                    return sb(name, shape, dt, l0b)

                zs = sbb("zs", [128, 4, T], BF16)
                xsT = sbb("xsT", [128, 4, T], BF16)
                BT = sbb("BT", [128, T], BF16)
                CT = sbb("CT", [128, T], BF16)
                b_zs, b_xs, b_BT, b_CT = Buf(), Buf(), Buf(), Buf()
                stg = [sbb("stg%d" % i, [128, 515]) for i in range(2)]
                b_stg = [Buf(), Buf()]
                accr = Ring([sbb("cacc%d" % i, [128, 512]) for i in range(1)])
                xdt_pad = [sbb("xdtp%d" % i, [128, 2, 8, 128], BF16) for i in range(2)]
                b_xdt = [Buf(), Buf()]
                xdtw = [sbb("xdtw%d" % i, [128, 2, 512], BF16) for i in range(2)]
                b_xdtw = [Buf(), Buf()]
                Btok = [sbb("Btok%d" % i, [128, 256], BF16) for i in range(2)]
                b_Btok = [Buf(), Buf()]
                cbT = [sbb("cbT%d" % i, [128, 384]) for i in range(2)]
                b_cbT = [Buf(), Buf()]
                csbr = Ring([sbb("csb%d" % i, [128, 8, 256]) for i in range(2)])
                x3 = sbb("x3", [128, 8, 3, 128])
                b_x3 = Buf()
                ecs4 = sbb("ecs4", [128, 4, 256], BF16)
                b_ecs4 = Buf()
                diagD = sbb("diagD", [128, 8, 128], BF16)
                b_diagD = Buf()
                dhi_b = sbb("dhi_b", [128, 32], BF16)
                dhi = sbb("dhi", [128, 32])
                dlo = sbb("dlo", [128, 32])
                b_dsp = Buf()
                vcopy(dhi_b[:], dskip[:], [b_par], [b_dsp], eng="pool")
                vcopy(dhi[:], dhi_b[:], [b_dsp], [b_dsp], eng="pool")
                tt(dlo[:], dskip[:], dhi[:], ALU.subtract, [b_par, b_dsp], [b_dsp], eng="pool")
                GTr = Ring([sbb("GT%d" % i, [128, 384], BF16) for i in range(4)])
                yoffr = Ring([sbb("yoff%d" % i, [128, 256]) for i in range(1)])
                ysb = sbb("ysb", [128, 4, 256])
                b_y = [Buf() for _ in range(4)]
                sqr0 = Ring([sbb("sq0_%d" % i, [128, 256], BF16) for i in range(2)])
                rs0 = sbb("rs0", [128, 256])
                b_rs0 = Buf()
                ygr = Ring([sbb("yg%d" % i, [128, 4, 256], BF16) for i in range(1)])
                S32 = sbb("S32", [128, 512])
                Sbf = sbb("Sbf", [128, 512], BF16)
                b_S, b_Sbf = Buf(), Buf()
                for i in range(2):
                    memset(xdt_pad[i][:], 0.0, [b_xdt[i]])

                pring = Ring([psb[0], psb[1]])
                pring.bufs = [psbuf[0], psbuf[1]]
                pzr = Ring([psb[3][:, 0:256], psb[6][:, 0:256]])
                pzr.bufs = [psbuf[3], psbuf[6]]
                pyr = Ring([psb[4][:, 0:256], psb[5][:, 0:256]])
                pyr.bufs = [psbuf[4], psbuf[5]]
                P_ssq = psb[2]
                b_Pssq = psbuf[2]

                def conv_evac(cc, R, ps, pb, dest, b_dest):
                    st, bst = stg[R % 2], b_stg[R % 2]
                    if R == 0:
                        memset(st[:, 0:3], 0.0, [bst], eng="dve")
                    else:
                        acopy(st[:, 0:3], stg[(R - 1) % 2][:, 512:515], [b_stg[(R - 1) % 2]], [bst])
                    acopy(st[:, 3:515], ps, [pb], [bst])
                    acc, bacc = accr.next()
                    ts(acc[:], st[:, 0:512], convw[:, cc * 4:cc * 4 + 1], None, ALU.mult, None, [bst, b_par], [bacc])
                    for k in range(1, 4):
                        stt(acc[:], st[:, k:k + 512], convw[:, cc * 4 + k:cc * 4 + k + 1], acc[:], ALU.mult, ALU.add,
                            [bst, b_par, bacc], [bacc])
                    act(dest, acc[:], AF.Silu, [bacc, b_par], [b_dest], bias=convb[:, cc:cc + 1])

                def inproj_cols(col0, ncols, evac):
                    wt, wb = load_w(w_in0[:, col0:col0 + ncols], 16, ncols)
                    if g >= 1 and ncols == 256:
                        precast_step(0)
                    for qq in range(ncols // 128):
                        for R in range(4):
                            ps, pb = pring.next()
                            for k in range(16):
                                mm(ps[:, :], wt[:, k, qq * 128:(qq + 1) * 128], xTb[:, k, R * 512:(R + 1) * 512],
                                   k == 0, k == 15, [wb, b_xTb], [pb])
                            evac(qq, R, ps[:, :], pb)

                csb_next = None

                def load_csb(g_, c_):
                    t_, b_ = csbr.next()
                    src = bass.AP(csT_h, 8 * g_ * T + c_ * 256, [[0, 128], [T, 8], [1, 256]])
                    dma("sp", t_[:, :, :], src, [b_csTd], [b_])
                    return t_, b_

                for g in range(8):
                    for pr in range(4):
                        for hl, dsrc in ((0, dhi), (1, dlo)):
                            S.op("dve", lambda h, pr=pr, hl=hl, dsrc=dsrc, g=g: h.tensor_scalar(
                                out=diagD[:, 2 * pr + hl, :], in0=identf[:], scalar1=dsrc[:, 4 * g + pr:4 * g + pr + 1],
                                scalar2=None, op0=ALU.mult), [b_c, b_dsp], [b_diagD])
                    for half in range(2):
                        def ev_z(qq, R, ps, pb, half=half):
                            act(zs[:, half * 2 + qq, R * 512:(R + 1) * 512], ps, AF.Silu, [pb], [b_zs])
                        inproj_cols(g * 512 + half * 256, 256, ev_z)
                    for half in range(2):
                        def ev_x(qq, R, ps, pb, half=half):
                            q4 = half * 2 + qq
                            conv_evac(4 * g + q4, R, ps, pb, xsT[:, q4, R * 512:(R + 1) * 512], b_xs)
                        inproj_cols(4096 + g * 512 + half * 256, 256, ev_x)
                    inproj_cols(8192 + g * 128, 128,
                                lambda qq, R, ps, pb: conv_evac(32 + g, R, ps, pb, BT[:, R * 512:(R + 1) * 512], b_BT))
                    inproj_cols(9216 + g * 128, 128,
                                lambda qq, R, ps, pb: conv_evac(40 + g, R, ps, pb, CT[:, R * 512:(R + 1) * 512], b_CT))
                    hs = slice(8 * g, 8 * g + 8)

                    def prologue_pe(c):
                        nonlocal csb_next
                        t0 = c * 256
                        if g == 0 and c == 0:
                            csb_next = load_csb(0, 0)
                        csb, bcsb = csb_next
                        if c < 7:
                            csb_next = load_csb(g, c + 1)
                        elif g < 7:
                            csb_next = load_csb(g + 1, 0)
                        pxs_, b_Pxs = pring.next()
                        P_xstok = pxs_[:, :].bitcast(BF16)
                        pbt_, b_Pbt = pring.next()
                        P_btok = pbt_[:, 0:128].bitcast(BF16)
                        for j in range(2):
                            for q in range(4):
                                tr(P_xstok[:, j * 512 + q * 128: j * 512 + (q + 1) * 128],
                                   xsT[:, q, t0 + j * 128: t0 + (j + 1) * 128], ident[:], [b_xs, b_c], [b_Pxs])
                            tr(P_btok[:, j * 128:(j + 1) * 128], BT[:, t0 + j * 128: t0 + (j + 1) * 128], ident[:],
                               [b_BT, b_c], [b_Pbt])
                        return dict(csb=csb, bcsb=bcsb, P_xstok=P_xstok, b_Pxs=b_Pxs, P_btok=P_btok, b_Pbt=b_Pbt)

                    def prologue_rest1(c, h):
                        t0 = c * 256
                        pp = c % 2
                        P_xstok, b_Pxs, P_btok, b_Pbt = h["P_xstok"], h["b_Pxs"], h["P_btok"], h["b_Pbt"]
                        for j in range(2):
                            blk = P_xstok[:, j * 512:(j + 1) * 512]
                            xs4 = blk.rearrange("p (a e d) -> p a e d", a=4, e=2)
                            for par in range(2):
                                tt(xdt_pad[pp][:, j, par::2, par * 64:(par + 1) * 64], xs4[:, :, par, :],
                                   dt_tok[:, 2 * c + j, 8 * g + par:8 * g + 8:2].unsqueeze(2).to_broadcast([128, 4, 64]),
                                   ALU.mult, [b_Pxs, b_dt], [b_xdt[pp]])
                            tt(xdtw[pp][:, j, :].rearrange("p (a d) -> p a d", a=8), blk.rearrange("p (a d) -> p a d", a=8),
                               dtw[:, 2 * c + j, hs].unsqueeze(2).to_broadcast([128, 8, 64]), ALU.mult,
                               [b_Pxs, b_dt], [b_xdtw[pp]])

                    def prologue_rest1b(c, h):
                        t0 = c * 256
                        pp = c % 2
                        P_btok, b_Pbt = h["P_btok"], h["b_Pbt"]
                        acopy(Btok[pp][:], P_btok, [b_Pbt], [b_Btok[pp]])
                        pcb_, b_Pcb = pring.next()
                        P_cbT = pcb_[:, 0:384]
                        mm(P_cbT[:, 0:256], BT[:, t0:t0 + 128], CT[:, t0:t0 + 256], True, True, [b_BT, b_CT], [b_Pcb])
                        mm(P_cbT[:, 256:384], BT[:, t0 + 128:t0 + 256], CT[:, t0 + 128:t0 + 256], True, True,
                           [b_BT, b_CT], [b_Pcb])
                        tt(cbT[pp][:].rearrange("p (a b) -> p a b", a=3)[:, 0::2, :], P_cbT.rearrange("p (a b) -> p a b", a=3)[:, 0::2, :],
                           U128[:].unsqueeze(1).to_broadcast([128, 2, 128]), ALU.mult, [b_Pcb, b_c], [b_cbT[pp]])
                        acopy(cbT[pp][:, 128:256], P_cbT[:, 128:256], [b_Pcb], [b_cbT[pp]])
                        h["pst"] = h["pbst"] = None
                        if c < 7:
                            pst, pbst = psb[7], psbuf[7]
                            mm(pst[:, :], Btok[pp][:, 0:128], xdtw[pp][:, 0, :], True, False, [b_Btok[pp], b_xdtw[pp]], [pbst])
                            mm(pst[:, :], Btok[pp][:, 128:256], xdtw[pp][:, 1, :], False, True, [b_Btok[pp], b_xdtw[pp]], [pbst])
                            h["pst"], h["pbst"] = pst, pbst

                    def prologue_rest2(c, h):
                        csb, bcsb = h["csb"], h["bcsb"]
                        if c > 0:
                            for par in range(2):
                                act(ecs4[par * 64:(par + 1) * 64, :, :], csb[par * 64:(par + 1) * 64, par::2, :], AF.Exp,
                                    [bcsb], [b_ecs4], scale=-1.0)
                        nb0 = negcs[:, 2 * c, hs].unsqueeze(2).to_broadcast([128, 8, 128])
                        nb1 = negcs[:, 2 * c + 1, hs].unsqueeze(2).to_broadcast([128, 8, 128])
                        tt(x3[:, :, 0, :], csb[:, :, 0:128], nb0, ALU.max, [bcsb, b_dt], [b_x3])
                        tt(x3[:, :, 2, :], csb[:, :, 128:256], nb1, ALU.max, [bcsb, b_dt], [b_x3])
                        tt(x3[:, :, 1, :], csb[:, :, 128:256], nb0, ALU.subtract, [bcsb, b_dt], [b_x3])
                        tt(x3[:, :, 0, :], x3[:, :, 0, :], nb0, ALU.subtract, [b_x3, b_dt], [b_x3])
                        tt(x3[:, :, 2, :], x3[:, :, 2, :], nb1, ALU.subtract, [b_x3, b_dt], [b_x3])
                        act(x3[:], x3[:], AF.Exp, [b_x3], [b_x3], scale=-1.0)

                    def pairs(c, hooks, dq):
                        t0 = c * 256
                        pp = c % 2

                        def pair_heads(pr):
                            yps, byp = pyr.next()
                            pz = None
                            if c > 0:
                                pz = pzr.next()
                                mm(pz[0], Sbf[:, pr * 128:(pr + 1) * 128], CT[:, t0:t0 + 256], True, True, [b_Sbf, b_CT], [pz[1]])
                            for par in range(2):
                                r = 2 * pr + par
                                hh = 8 * g + r
                                GT, bGT = GTr.next()
                                tt(GT[:], x3[:, r, :, :].rearrange("p a b -> p (a b)"), cbT[pp][:], ALU.mult, [b_x3, b_cbT[pp]], [bGT])
                                mm(yps[:, 0:256], xdt_pad[pp][:, 0, r, :], GT[:, 0:256], par == 0, False,
                                   [b_xdt[pp], bGT], [byp])
                                mm(yps[:, 128:256], xdt_pad[pp][:, 1, r, :], GT[:, 256:384], False, False,
                                   [b_xdt[pp], bGT], [byp])
                            mm(yps[:, 0:256], diagD[:, 2 * pr, :], xsT[:, pr, t0:t0 + 256], False, False, [b_diagD, b_xs], [byp])
                            mm(yps[:, 0:256], diagD[:, 2 * pr + 1, :], xsT[:, pr, t0:t0 + 256], False, True, [b_diagD, b_xs], [byp])
                            return yps, byp, pz

                        def pair_tail(pr, yps, byp, pz):
                            yv = ysb[:, pr, :]
                            if c > 0:
                                yoff, byo = yoffr.next()
                                tt(yoff[:], pz[0], ecs4[:, pr, :], ALU.mult, [pz[1], b_ecs4], [byo])
                                tt(yv, yps, yoff[:], ALU.add, [byp, byo], [b_y[pr]])
                            else:
                                acopy(yv, yps, [byp], [b_y[pr]])
                            tt(yv, yv, zs[:, pr, t0:t0 + 256], ALU.mult, [b_y[pr], b_zs], [b_y[pr]], eng="pool")
                            sq, bsq = sqr0.next()
                            tt(sq[:], yv, yv, ALU.mult, [b_y[pr]], [bsq], eng="pool")
                            mm(P_ssq[:, 0:256], onesb[:], sq[:], pr == 0, pr == 3, [b_c, bsq], [b_Pssq])

                        cur = pair_heads(0)
                        for pr in range(4):
                            nxtp = pair_heads(pr + 1) if pr < 3 else None
                            if pr in hooks:
                                hooks[pr]()
                            if dq:
                                dq.pop(0)()
                            pair_tail(pr, *cur)
                            cur = nxtp
                        while dq:
                            dq.pop(0)()

                    def state_update(c, pst, pbst):
                        if c < 7:
                            if c == 0:
                                vcopy(S32[:], pst[:, :], [pbst], [b_S])
                            else:
                                tt(S32[:].rearrange("p (a d) -> p a d", a=8), S32[:].rearrange("p (a d) -> p a d", a=8),
                                   etot[:, c, hs].unsqueeze(2).to_broadcast([128, 8, 64]), ALU.mult, [b_S, b_dt], [b_S])
                                tt(S32[:], S32[:], pst[:, :], ALU.add, [b_S, pbst], [b_S])
                            acopy(Sbf[:], S32[:], [b_S], [b_Sbf])

                    def epilogue_a(c):
                        ts(rs0[:], P_ssq[:, 0:256], 1.0 / 512, RMS_EPS, ALU.mult, ALU.add, [b_Pssq], [b_rs0])
                        act(rs0[:], rs0[:], AF.Ln, [b_rs0], [b_rs0])
                        act(rs0[:], rs0[:], AF.Exp, [b_rs0], [b_rs0], scale=-0.5)

                    def epilogue_b(c):
                        t0 = c * 256
                        yg, byg = ygr.next()
                        ops = []
                        for pr in range(4):
                            ops.append(lambda pr=pr: stt(yg[:, pr, :], ysb[:, pr, :], normg[:, 4 * g + pr:4 * g + pr + 1], rs0[:],
                                                         ALU.mult, ALU.mult, [b_y[pr], b_par, b_rs0], [byg]))
                        ops.append(lambda: dma("sp", ygT_d[g * 512:(g + 1) * 512, t0:t0 + 256].rearrange("(q p) t -> p q t", p=128),
                                               yg[:], [byg], [b_ygT]))
                        return ops

                    pro = prologue_pe(0)
                    prologue_rest1(0, pro)
                    prologue_rest1b(0, pro)
                    prologue_rest2(0, pro)
                    dq = []
                    for c in range(8):
                        box = {}
                        hooks = {}
                        if c < 7:
                            hooks[1] = lambda c=c, box=box: box.update(p=prologue_pe(c + 1))
                            hooks[3] = lambda c=c, box=box: prologue_rest1(c + 1, box["p"])
                        pairs(c, hooks, dq)
                        state_update(c, pro["pst"], pro["pbst"])
                        if c < 7:
                            prologue_rest1b(c + 1, box["p"])
                        epilogue_a(c)
                        if c < 7:
                            prologue_rest2(c + 1, box["p"])
                        dq = epilogue_b(c)
                        if c < 7:
                            pro = box["p"]
                    while dq:
                        dq.pop(0)()

            S.fence()
            with ExitStack() as l0c:
                if upto == "l0":
                    outproj_ln(l0c, ygT_d, b_ygT, w_out0, xT_d, Buf(), 0, outT_d, b_outT, True, False)
                else:
                    outproj_ln(l0c, ygT_d, b_ygT, w_out0, xT_d, Buf(), 0, x1T_d, b_x1T, False, True)


        if upto != "l0":
            cqnT_d = nc.dram_tensor("cqnT", [512, T], BF16).ap()
            ckvT_d = nc.dram_tensor("ckvT", [256, T], BF16).ap()
            kdup_d = nc.dram_tensor("kdup", [128, T], BF16).ap()
            widx_d = nc.dram_tensor("widx", [128, 256], F32).ap()
            b_cqnT_d, b_ckvT_d, b_kdup_d, b_widx_d = Buf(), Buf(), Buf(), Buf()
            S.fence()
            with ExitStack() as l1a:
                def sba(name, shape, dt=F32):
                    return sb(name, shape, dt, l1a)
                cq32 = sba("cq32", [128, 4, T])
                ckv32 = sba("ckv32", [128, 2, T])
                b_cq32, b_ckv32 = Buf(), Buf()
                kst = sba("kst", [128, T], BF16)
                b_kst = Buf()
                wst = sba("wst", [128, 256])
                b_wst = Buf()
                gstr = Ring([sba("gst%d" % i, [128, 512], BF16) for i in range(3)])
                sq1r = Ring([sba("sq1_%d" % i, [128, 512]) for i in range(2)])
                rs1 = sba("rs1", [128, 512])
                b_rs1 = Buf()
                nrmr = Ring([sba("nrm%d" % i, [128, 512], BF16) for i in range(2)])
                pring = Ring([psb[0], psb[1], psb[2]])
                pring.bufs = [psbuf[0], psbuf[1], psbuf[2]]

                def inproj1(wt, wb, nsub, evac):
                    for qq in range(nsub):
                        for R in range(4):
                            ps, pb = pring.next()
                            for k in range(16):
                                mm(ps[:, :], wt[:, k, qq * 128:(qq + 1) * 128], xTb[:, k, R * 512:(R + 1) * 512],
                                   k == 0, k == 15, [wb, b_xTb], [pb])
                            evac(qq, R, ps[:, :], pb)

                for half in range(2):
                    wt, wb = load_w(w_in1[:, half * 256:(half + 1) * 256], 16, 256)
                    inproj1(wt, wb, 2, lambda qq, R, ps, pb, half=half:
                            acopy(cq32[:, half * 2 + qq, R * 512:(R + 1) * 512], ps, [pb], [b_cq32]))
                wt, wb = load_w(w_in1[:, 512:768], 16, 256)
                inproj1(wt, wb, 2, lambda qq, R, ps, pb: acopy(ckv32[:, qq, R * 512:(R + 1) * 512], ps, [pb], [b_ckv32]))
                wt, wb = wring.next()
                ksrc = w_in1[:, 768:832].rearrange("(c p) n -> p c n", p=128)
                S.dma("pool", lambda h, wt=wt: [h.dma_start(out=wt[:, 0:16, 0:64], in_=ksrc),
                                                h.dma_start(out=wt[:, 0:16, 64:128], in_=ksrc)], [], [wb], n=2)
                inproj1(wt, wb, 1, lambda qq, R, ps, pb: acopy(kst[:, R * 512:(R + 1) * 512], ps, [pb], [b_kst]))
                dma("sp", kdup_d[:, :], kst[:], [b_kst], [b_kdup_d])
                wt, wb = load_w(w_in1[:, 832:848], 16, 16)
                ps, pb = pring.next()
                for i in range(16):
                    for k in range(16):
                        mm(ps[:, i * 16:(i + 1) * 16], xTb[:, k, i * 128:(i + 1) * 128], wt[:, k, 0:16], k == 0, k == 15,
                           [wb, b_xTb], [pb])
                amul(wst[:], ps[:, 0:256], 0.25 * 0.125, [pb], [b_wst])
                dma("sp", widx_d[:, :], wst[:], [b_wst], [b_widx_d])
                for gt in range(16):
                    wt, wb = load_w(w_in1[:, 848 + gt * 256: 848 + (gt + 1) * 256], 16, 256)
                    precast_step(1)

                    def ev_g(qq, R, ps, pb, gt=gt):
                        g_, bg_ = gstr.next()
                        act(g_[:], ps, AF.Silu, [pb], [bg_])
                        fc = gt * 2 + qq
                        dma("sp", gateT_d[fc * 128:(fc + 1) * 128, R * 512:(R + 1) * 512], g_[:], [bg_], [b_gateT])
                    inproj1(wt, wb, 2, ev_g)
                for (src, bsrc, nch, gtile, dst_d, bdst) in ((cq32, b_cq32, 4, qng, cqnT_d, b_cqnT_d),
                                                            (ckv32, b_ckv32, 2, kvng, ckvT_d, b_ckvT_d)):
                    for R in range(4):
                        rs = slice(R * 512, (R + 1) * 512)
                        ps, pb = pring.next()
                        for q in range(nch):
                            sq, bsq = sq1r.next()
                            act(sq[:], src[:, q, rs], AF.Square, [bsrc], [bsq])
                            mm(ps[:, :], onesf[:], sq[:], q == 0, q == nch - 1, [b_c, bsq], [pb])
                        ts(rs1[:], ps[:, :], 1.0 / (128 * nch), RMS_EPS, ALU.mult, ALU.add, [pb], [b_rs1])
                        recip(rs1[:], rs1[:], [b_rs1], [b_rs1])
                        act(rs1[:], rs1[:], AF.Sqrt, [b_rs1], [b_rs1])
                        for q in range(nch):
                            nr, bnr = nrmr.next()
                            stt(nr[:], src[:, q, rs], gtile[:, q:q + 1], rs1[:], ALU.mult, ALU.mult, [bsrc, b_par, b_rs1], [bnr])
                            dma("sp", dst_d[q * 128:(q + 1) * 128, rs], nr[:], [bnr], [bdst])
            xstk.close()

            S.fence()
            with ExitStack() as l1:
                def sb1(name, shape, dt=F32):
                    return sb(name, shape, dt, l1)
                cqn = sb1("cqn", [128, 4, T], BF16)
                ckvT = sb1("ckvT", [128, 2, T], BF16)
                ckv_tok = sb1("ckv_tok", [128, 16, 256], BF16)
                kdup = sb1("kdup", [128, T], BF16)
                widx = sb1("widx", [128, 256])
                maskT = sb1("maskT", [128, 16, T], BF16)
                b_cqn, b_ckvT, b_ckvtok, b_kdup, b_widx, b_maskT = Buf(), Buf(), Buf(), Buf(), Buf(), Buf()
                dma("sp", cqn[:, :, :], cqnT_d[:, :].rearrange("(c p) t -> p c t", p=128), [b_cqnT_d], [b_cqn])
                dma("sp", ckvT[:, :, :], ckvT_d[:, :].rearrange("(c p) t -> p c t", p=128), [b_ckvT_d], [b_ckvT])
                dma("sp", kdup[:, :], kdup_d[:, :], [b_kdup_d], [b_kdup])
                dma("sp", widx[:, :], widx_d[:, :], [b_widx_d], [b_widx])
                pring = Ring([psb[0], psb[1], psb[7]])
                pring.bufs = [psbuf[0], psbuf[1], psbuf[7]]
                for j in range(16):
                    ps, pb = pring.next()
                    pv = ps[:, 0:128].bitcast(BF16)
                    for cc in range(2):
                        tr(pv[:, cc * 128:(cc + 1) * 128], ckvT[:, cc, j * 128:(j + 1) * 128], ident[:], [b_ckvT, b_c], [pb])
                    acopy(ckv_tok[:, j, :], pv, [pb], [b_ckvtok])

                with ExitStack() as l1b:
                    def sbi(name, shape, dt=F32):
                        return sb(name, shape, dt, l1b)
                    qidxT = sbi("qidxT", [128, 8, T], BF16)
                    b_qidx = Buf()
                    scr = Ring([sbi("sc%d" % i, [128, T]) for i in range(3)])
                    rlr = Ring([sbi("rl%d" % i, [128, 512]) for i in range(2)])
                    mkr = Ring([sbi("mk%d" % i, [128, T], BF16) for i in range(2)])
                    junk = sbi("junk", [128, T], BF16)
                    b_junk = Buf()
                    cneg = sbi("cneg", [128, 128])
                    b_cneg = Buf()
                    NIT = 26
                    bis = Ring([sbi("bis%d" % i, [128, 72]) for i in range(2)])
                    memset(cneg[:], 0.0, [b_cneg])
                    S.op("pool", lambda h: h.affine_select(out=cneg[:], in_=cneg[:], pattern=[[-1, 128]], compare_op=ALU.is_ge,
                                                            fill=-1e30, base=0, channel_multiplier=1), [b_cneg], [b_cneg])
                    for cp in range(4):
                        wt, wb = load_w(w_idxq[:, cp * 256:(cp + 1) * 256], 4, 256)
                        for qq in range(2):
                            for R in range(4):
                                ps, pb = pring.next()
                                for k in range(4):
                                    mm(ps[:, :], wt[:, k, qq * 128:(qq + 1) * 128], cqn[:, k, R * 512:(R + 1) * 512],
                                       k == 0, k == 3, [wb, b_cqn], [pb])
                                acopy(qidxT[:, cp * 2 + qq, R * 512:(R + 1) * 512], ps[:, :], [pb], [b_qidx])
                    bconst = sbi("bconst", [128, 64])
                    b_bconst = Buf()
                    for k in range(NIT):
                        memset(bconst[:, k:k + 1], -(2.0 ** -(k + 2)), [b_bconst])
                    for i in range(16):
                        memset(bconst[:, 32 + i:33 + i], 0.5 - (512.0 - (i + 1) * 128), [b_bconst])
                    ftab = sbi("ftab", [128, 32])
                    for k in range(NIT + 1):
                        memset(ftab[:, k:k + 1], 2.0 ** -(k + 1), [b_bconst])
                    junk2 = sbi("junk2", [128, T], BF16)
                    b_junk2 = Buf()

                    def scores(i):
                        Wi = (i + 1) * 128
                        sc, bsc = scr.next()
                        for Q in range((Wi + 511) // 512):
                            wq = min(512, Wi - Q * 512)
                            qs = slice(Q * 512, Q * 512 + wq)
                            for hd in range(16):
                                ip, hf = hd // 2, hd % 2
                                ps, pb = pring.next()
                                mm(ps[:, 0:wq], qidxT[hf * 64:(hf + 1) * 64, ip, i * 128:(i + 1) * 128],
                                   kdup[hf * 64:(hf + 1) * 64, qs], True, True, [b_qidx, b_kdup], [pb])
                                if hd == 0:
                                    ts(sc[:, qs], ps[:, 0:wq], 0.0, widx[:, i * 16:i * 16 + 1], ALU.max, ALU.mult, [pb, b_widx], [bsc])
                                else:
                                    rl, brl = rlr.next()
                                    ts(rl[:, 0:wq], ps[:, 0:wq], 0.0, widx[:, i * 16 + hd:i * 16 + hd + 1], ALU.max, ALU.mult,
                                       [pb, b_widx], [brl])
                                    tt(sc[:, qs], sc[:, qs], rl[:, 0:wq], ALU.add, [brl, bsc], [bsc])
                        tt(sc[:, i * 128:Wi], sc[:, i * 128:Wi], cneg[:], ALU.add, [bsc, b_cneg], [bsc])
                        return sc, bsc

                    def bisect(i, sc, bsc, use_act):
                        Wi = (i + 1) * 128
                        bt, bbt = bis.next()
                        lo = bt[:, 0:1]
                        if i < 2:
                            memset(lo, -1e29, [bbt], eng="dve")
                            return lambda: (lo, bbt)
                        hi = bt[:, 1:2]
                        w0 = bt[:, 2:3]
                        mid = bt[:, 3:4]
                        stp = bt[:, 4:5]
                        tab = bt[:, 40:40 + NIT + 1]
                        S.op("dve", lambda h: h.tensor_reduce(out=hi, in_=sc[:, 0:Wi], axis=mybir.AxisListType.X, op=ALU.max), [bsc], [bbt])
                        S.op("dve", lambda h: h.tensor_reduce(out=lo, in_=sc[:, 0:i * 128], axis=mybir.AxisListType.X, op=ALU.min), [bsc], [bbt])
                        tt(w0, hi, lo, ALU.subtract, [bbt], [bbt])
                        memset(bt[:, 8:8 + NIT], 0.0, [bbt], eng="dve")
                        if use_act:
                            ts(tab, ftab[:, 0:NIT + 1], w0, -1.0, ALU.mult, ALU.mult, [bbt, b_bconst], [bbt])
                            stt(mid, lo, -1.0, tab[:, 0:1], ALU.mult, ALU.add, [bbt], [bbt])
                        else:
                            ts(tab, ftab[:, 0:NIT + 1], w0, None, ALU.mult, None, [bbt, b_bconst], [bbt])
                            tt(mid, lo, tab[:, 0:1], ALU.add, [bbt], [bbt])
                        return lambda: bisect_loop(i, sc, bsc, use_act, bt, bbt)

                    def bisect_loop(i, sc, bsc, use_act, bt, bbt):
                        Wi = (i + 1) * 128
                        lo = bt[:, 0:1]
                        mid = bt[:, 3:4]
                        stp = bt[:, 4:5]
                        tab = bt[:, 40:40 + NIT + 1]
                        if not use_act:
                            for k in range(NIT):
                                ts(junk[:, 0:Wi], sc[:, 0:Wi], mid, 0.0, ALU.is_ge, ALU.add, [bsc, bbt], [b_junk, bbt],
                                   accum=bt[:, 8 + k:9 + k])
                                ts(stp, bt[:, 8 + k:9 + k], 256.0, 0.5, ALU.is_ge, ALU.subtract, [bbt], [bbt])
                                stt(mid, stp, tab[:, k:k + 1], mid, ALU.mult, ALU.add, [bbt], [bbt])
                            tt(lo, mid, tab[:, NIT:NIT + 1], ALU.subtract, [bbt], [bbt])
                        else:
                            for k in range(NIT):
                                S.op("act", lambda h, k=k: h.activation(out=junk2[:, 0:Wi], in_=sc[:, 0:Wi], func=AF.Sign, bias=mid, scale=1.0,
                                                                       accum_out=bt[:, 8 + k:9 + k]), [bsc, bbt], [b_junk2, bbt])
                                act(stp, bt[:, 8 + k:9 + k], AF.Sign, [bbt, b_bconst], [bbt], bias=bconst[:, 32 + i:33 + i])
                                act(mid, stp, AF.Identity, [bbt], [bbt], bias=mid, scale=tab[:, k + 1:k + 2])
                            stt(lo, mid, -1.0, tab[:, NIT:NIT + 1], ALU.mult, ALU.add, [bbt], [bbt])
                        return lo, bbt

                    def finish(i, sc, bsc, lo, bbt):
                        Wi = (i + 1) * 128
                        mk, bmk = mkr.next()
                        ts(mk[:, 0:Wi], sc[:, 0:Wi], lo, None, ALU.is_ge, None, [bsc, bbt], [bmk])
                        for j0 in range(0, i + 1, 4):
                            n = min(4, i + 1 - j0)
                            ps, pb = pring.next()
                            pv = ps[:, 0:256].bitcast(BF16)
                            for jj in range(n):
                                tr(pv[:, jj * 128:(jj + 1) * 128], mk[:, (j0 + jj) * 128:(j0 + jj + 1) * 128], ident[:], [bmk, b_c], [pb])
                            acopy(maskT[:, j0:j0 + n, i * 128:(i + 1) * 128], pv[:, 0:n * 128].rearrange("p (a b) -> p a b", a=n),
                                  [pb], [b_maskT])

                    scq = {0: scores(0), 1: scores(1)}
                    for i in range(16):
                        sa = scq.pop(i)
                        fa = bisect(i, sa[0], sa[1], True)
                        if i + 2 < 16:
                            scq[i + 2] = scores(i + 2)
                        la = fa()
                        finish(i, sa[0], sa[1], la[0], la[1])

                S.fence()
                with ExitStack() as l1c:
                    def sbc(name, shape, dt=F32):
                        return sb(name, shape, dt, l1c)
                    wqr = Ring([sbc("wq%d" % i, [128, 4, 128], BF16) for i in range(2)])
                    wkr = Ring([sbc("wk%d" % i, [128, 256], BF16) for i in range(2)])
                    wvr = Ring([sbc("wv%d" % i, [128, 2, 128], BF16) for i in range(2)])
                    qTr = Ring([sbc("qT%d" % i, [128, T], BF16) for i in range(2)])
                    qlr = Ring([sbc("ql%d" % i, [128, 2, T], BF16) for i in range(2)])
                    er = Ring([sbc("e%d" % i, [128, 512], BF16) for i in range(4)])
                    pTr = Ring([sbc("pT%d" % i, [128, 512], BF16) for i in range(4)])
                    rden = sbc("rden", [128, 512])
                    evr = Ring([sbc("ev%d" % i, [128, 3, 512]) for i in range(2)])
                    b_rden = Buf()
                    olr = Ring([sbc("oln%d" % i, [128, 2, 512], BF16) for i in range(2)])
                    gtr = Ring([sbc("gt%d" % i, [128, 512], BF16) for i in range(2)])
                    ogr = Ring([sbc("og%d" % i, [128, 512], BF16) for i in range(2)])
                    pring = Ring([psb[0], psb[1]])
                    pring.bufs = [psbuf[0], psbuf[1]]
                    plog = Ring([psb[2], psb[3], psb[7]])
                    plog.bufs = [psbuf[2], psbuf[3], psbuf[7]]
                    PA = [psb[4], psb[5]]
                    PD = psb[6]
                    SCALE = 128.0 ** -0.5
                    LAG = 2

                    def head_proj(hd):
                        wq_, bwq = wqr.next()
                        dma("pool", wq_[:, :, :], w_uq[:, hd * 128:(hd + 1) * 128].rearrange("(c p) n -> p c n", p=128), [], [bwq])
                        wk_, bwk = wkr.next()
                        dma("pool", wk_[:, :], w_ukT[hd * 128:(hd + 1) * 128, :], [], [bwk])
                        wv_, bwv = wvr.next()
                        dma("pool", wv_[:, :, :], w_uv[hd * 256:(hd + 1) * 256, :].rearrange("(c p) n -> p c n", p=128), [], [bwv])
                        qT, bqT = qTr.next()
                        ql, bql = qlr.next()
                        res = (ql, bql, wv_, bwv)
                        yield res
                        for R in range(4):
                            ps, pb = pring.next()
                            for k in range(4):
                                mm(ps[:, :], wq_[:, k, :], cqn[:, k, R * 512:(R + 1) * 512], k == 0, k == 3, [bwq, b_cqn], [pb])
                            acopy(qT[:, R * 512:(R + 1) * 512], ps[:, :], [pb], [bqT])
                            yield res
                        for cc in range(2):
                            for R in range(4):
                                ps, pb = pring.next()
                                mm(ps[:, :], wk_[:, cc * 128:(cc + 1) * 128], qT[:, R * 512:(R + 1) * 512], True, True, [bwk, bqT], [pb])
                                vcopy(ql[:, cc, R * 512:(R + 1) * 512], ps[:, :], [pb], [bql])
                                yield res

                    tuples = [(R, j) for R in range(4) for j in range(4 * R + 4)]
                    n = len(tuples)
                    nxt = None
                    deferred = []
                    for nxt in head_proj(0):
                        pass
                    for hd in range(32):
                        ql, bql, wv_, bwv = nxt
                        gen = head_proj(hd + 1) if hd + 1 < 32 else None
                        pts = {}

                        def stage_qk(t, ql=ql, bql=bql):
                            R, j = tuples[t]
                            r1 = (R + 1) * 512
                            tlo = max(j * 128, R * 512)
                            off = tlo - R * 512
                            pl, bpl = plog.next()
                            mm(pl[:, off:512], ckvT[:, 0, j * 128:(j + 1) * 128], ql[:, 0, tlo:r1], True, False, [b_ckvT, bql], [bpl])
                            mm(pl[:, off:512], ckvT[:, 1, j * 128:(j + 1) * 128], ql[:, 1, tlo:r1], False, True, [b_ckvT, bql], [bpl])
                            e_, be = er.next()
                            act(e_[:, off:512], pl[:, off:512], AF.Exp, [bpl], [be], scale=SCALE)
                            pT, bpT = pTr.next()
                            tt(pT[:, off:512], e_[:, off:512], maskT[:, j, tlo:r1], ALU.mult, [be, b_maskT], [bpT])
                            pts[t] = (pT, bpT, off)

                        def stage_pv(t, wv_=wv_, bwv=bwv, hd=hd):
                            R, j = tuples[t]
                            r1 = (R + 1) * 512
                            nj = 4 * R + 4
                            pT, bpT, off = pts.pop(t)
                            for cc in range(2):
                                mm(PA[cc][:, off:512], ckv_tok[:, j, cc * 128:(cc + 1) * 128], pT[:, off:512], j == 0, j == nj - 1,
                                   [b_ckvtok, bpT], [psbuf[4 + cc]])
                            mm(PD[:, off:512], onesb[:], pT[:, off:512], j == 0, j == nj - 1, [b_c, bpT], [psbuf[6]])
                            if j == nj - 1:
                                ev, bev = evr.next()
                                acopy(ev[:, 2, :], PD[:, :], [psbuf[6]], [bev])
                                vcopy(ev[:, 0, :], PA[0][:, :], [psbuf[4]], [bev])
                                acopy(ev[:, 1, :], PA[1][:, :], [psbuf[5]], [bev])
                                ol, bol = olr.next()
                                st_ = {}

                                def d_ln(ev=ev, bev=bev):
                                    act(rden[:], ev[:, 2, :], AF.Ln, [bev], [b_rden])

                                def d_exp():
                                    act(rden[:], rden[:], AF.Exp, [b_rden], [b_rden], scale=-1.0)

                                def d_ol(cc, ev=ev, bev=bev, ol=ol, bol=bol):
                                    tt(ol[:, cc, :], ev[:, cc, :], rden[:], ALU.mult, [bev, b_rden], [bol])

                                def d_oproj(ol=ol, bol=bol, wv_=wv_, bwv=bwv, hd=hd, R=R, r1=r1, st_=st_):
                                    ps, pb = pring.next()
                                    for cc in range(2):
                                        mm(ps[:, :], wv_[:, cc, :], ol[:, cc, :], cc == 0, cc == 1, [bwv, bol], [pb])
                                    gt_, bgt = gtr.next()
                                    dma("sp", gt_[:], gateT_d[hd * 128:(hd + 1) * 128, R * 512:r1], [b_gateT], [bgt])
                                    st_["v"] = (ps, pb, gt_, bgt)

                                def d_og(hd=hd, R=R, r1=r1, st_=st_):
                                    ps, pb, gt_, bgt = st_["v"]
                                    og, bog = ogr.next()
                                    tt(og[:], ps[:, :], gt_[:], ALU.mult, [pb, bgt], [bog])
                                    dma("sp", ygT_d[hd * 128:(hd + 1) * 128, R * 512:r1], og[:], [bog], [b_ygT])

                                deferred.extend([d_ln, d_exp, lambda: d_ol(0), lambda: d_ol(1), d_oproj, d_og])

                        for step in range(n + LAG):
                            if step < n:
                                stage_qk(step)
                            if step - LAG >= 0:
                                stage_pv(step - LAG)
                            if deferred:
                                deferred.pop(0)()
                            if gen is not None and step >= 6 and step % 2 == 0:
                                nxt = next(gen, nxt)
                        if gen is not None:
                            for nxt in gen:
                                pass
                    while deferred:
                        deferred.pop(0)()

            S.fence()
            with ExitStack() as l1d:
                outproj_ln(l1d, ygT_d, b_ygT, w_out1, x1T_d, b_x1T, 1, outT_d, b_outT, True, False)
        else:
            xstk.close()

        S.emit()
    return nc


_CACHE = {}


def _feat(v, nchunk):
    return np.ascontiguousarray(np.asarray(v, np.float32).reshape(nchunk, 128).T)


def kernel(x, ssd_w_in, ssd_conv_w, ssd_conv_b, ssd_dt_bias, ssd_a_log, ssd_d_skip, ssd_norm_g, ssd_w_out,
           dsa_w_in, dsa_q_norm_g, dsa_kv_norm_g, dsa_w_uq, dsa_w_uk, dsa_w_uv, dsa_w_idx_q, dsa_w_out,
           ln_g, ln_b, _upto="all"):
    f = lambda a: np.ascontiguousarray(np.asarray(a, np.float32))
    x = f(x)
    nb = x.shape[0]
    shared = {
        "w_in0": f(ssd_w_in[0]),
        "convw": np.ascontiguousarray(f(ssd_conv_w[0]).T.reshape(48, 128, 4).transpose(1, 0, 2).reshape(128, 192)),
        "convb": _feat(ssd_conv_b[0], 48),
        "dtb": np.ascontiguousarray(np.tile(f(ssd_dt_bias[0])[None, :], (128, 1))),
        "alog": np.ascontiguousarray(np.tile(f(ssd_a_log[0])[None, :], (128, 1))),
        "dskip": _feat(np.repeat(f(ssd_d_skip[0]), 64), 32),
        "normg": _feat(ssd_norm_g[0], 32),
        "w_out0": f(ssd_w_out[0]),
        "lng": np.ascontiguousarray(np.concatenate([_feat(ln_g[0], 16), _feat(ln_g[1], 16)], 1)),
        "lnb": np.ascontiguousarray(np.concatenate([_feat(ln_b[0], 16), _feat(ln_b[1], 16)], 1)),
        "w_in1": f(dsa_w_in[0]),
        "qng": _feat(dsa_q_norm_g[0], 4),
        "kvng": _feat(dsa_kv_norm_g[0], 2),
        "w_uq": f(dsa_w_uq[0]),
        "w_ukT": np.ascontiguousarray(f(dsa_w_uk[0]).transpose(0, 2, 1).reshape(32 * 128, 256)),
        "w_uv": np.ascontiguousarray(f(dsa_w_uv[0]).reshape(32 * 256, 128)),
        "w_idxq": f(dsa_w_idx_q[0]),
        "w_out1": f(dsa_w_out[0]),
    }
    if _upto not in _CACHE:
        _CACHE[_upto] = build(_upto)
    nc = _CACHE[_upto]
    in_maps = []
    for b in range(nb):
        m = dict(shared)
        m["xT"] = np.ascontiguousarray(x[b].T)
        in_maps.append(m)
    res = run_bass_kernel_spmd(nc, in_maps, core_ids=list(range(nb)))
    out = np.stack([np.ascontiguousarray(np.asarray(r["outT"], np.float32).T) for r in res.results], 0)
    return out
```

```python
import numpy as np
from contextlib import ExitStack
import concourse.bass as bass
import concourse.mybir as mybir
from concourse.bass_utils import run_bass_kernel_spmd

F32 = mybir.dt.float32
BF16 = mybir.dt.bfloat16
ALU = mybir.AluOpType
AF = mybir.ActivationFunctionType

T = 2048
D = 2048
ALPHA = 4.0 ** 0.25
LN_EPS = 1e-5
RMS_EPS = 1e-6
NEG = -30000.0
USE_ACT_BISECT = True


class Buf:
    __slots__ = ("name", "w", "r")

    def __init__(self, name=""):
        self.name = name
        self.w = None
        self.r = []


class Op:
    __slots__ = ("eng", "fn", "deps", "signal", "sem", "val", "is_dma", "ndma", "prev")

    def __init__(self, eng, fn, is_dma=False, ndma=1):
        self.eng = eng
        self.fn = fn
        self.deps = []
        self.signal = False
        self.sem = None
        self.val = None
        self.is_dma = is_dma
        self.ndma = ndma
        self.prev = 0


class Sched:
    ENGS = ("pe", "act", "dve", "pool", "sp")

    def __init__(self, nc, n_dma_sems=16):
        self.nc = nc
        self.ops = {e: [] for e in self.ENGS}
        self.n_dma_sems = n_dma_sems
        self.out_ops = []
        self.dmas = []

    def _add(self, op, reads, writes):
        deps = set()
        for b in reads:
            if b.w is not None:
                deps.add(b.w)
        for b in writes:
            if b.w is not None:
                deps.add(b.w)
            for r in b.r:
                deps.add(r)
        for d in deps:
            if d is op:
                continue
            if (not op.is_dma) and (not d.is_dma) and d.eng == op.eng:
                if op.eng == "pe":
                    continue
            d.signal = True
            op.deps.append(d)
        for b in reads:
            b.r.append(op)
        for b in writes:
            b.w = op
            b.r = []
        self.ops[op.eng].append(op)
        return op

    def op(self, eng, fn, reads=(), writes=()):
        return self._add(Op(eng, fn), list(reads), list(writes))

    def dma(self, eng, fn, reads=(), writes=(), n=1):
        o = Op(eng, fn, is_dma=True, ndma=n)
        o.signal = True
        self.dmas.append(o)
        return self._add(o, list(reads), list(writes))

    def fence(self):
        front = []
        for e in self.ENGS:
            for o in reversed(self.ops[e]):
                if o.fn is not None and not o.is_dma:
                    front.append(o)
                    break
        front += self.dmas
        self.dmas = []
        for e in self.ENGS:
            p = Op(e, None)
            for d in front:
                if d.eng == e and e == "pe" and not d.is_dma:
                    continue
                d.signal = True
                p.deps.append(d)
            self.ops[e].append(p)

    def emit(self):
        nc = self.nc
        with ExitStack() as es:
            esem = {e: es.enter_context(nc.semaphore("s_" + e)) for e in self.ENGS}
            dsems = {e: [es.enter_context(nc.semaphore("d_%s%d" % (e, i))) for i in range(self.n_dma_sems)]
                     for e in ("sp", "pool", "act")}
            for e in self.ENGS:
                cnt = 0
                dcnt = [0] * self.n_dma_sems
                k = 0
                for o in self.ops[e]:
                    if o.is_dma:
                        o.sem = dsems[e][k]
                        o.prev = dcnt[k]
                        dcnt[k] += 16 * o.ndma
                        o.val = dcnt[k]
                        k = (k + 1) % self.n_dma_sems
                    elif o.signal:
                        cnt += 1
                        o.sem = esem[e]
                        o.val = cnt
            handles = {"pe": "tensor", "act": "scalar", "dve": "vector", "pool": "gpsimd", "sp": "sync"}
            block = es.enter_context(nc.Block())
            out_ops = self.out_ops
            for e in self.ENGS:
                ops = self.ops[e]

                def body(h, ops=ops, e=e):
                    seen = {}
                    for o in ops:
                        deps = o.deps
                        if o.fn is None:
                            best = {}
                            for d in deps:
                                if id(d.sem) not in best or best[id(d.sem)].val < d.val:
                                    best[id(d.sem)] = d
                            deps = list(best.values())
                        for d in deps:
                            key = id(d.sem)
                            if seen.get(key, 0) >= d.val:
                                continue
                            h.wait_ge(d.sem, d.val)
                            seen[key] = d.val
                        if o.is_dma:
                            if o.prev > 0 and seen.get(id(o.sem), 0) < o.prev:
                                h.wait_ge(o.sem, o.prev)
                                seen[id(o.sem)] = o.prev
                            r = o.fn(h)
                            if not isinstance(r, (list, tuple)):
                                r = [r]
                            assert len(r) == o.ndma
                            for ins in r:
                                ins.then_inc(o.sem, 16)
                        elif o.fn is not None:
                            ins = o.fn(h)
                            if o.signal:
                                ins.then_inc(o.sem, 1)
                    if e == "sp":
                        for o in out_ops:
                            h.wait_ge(o.sem, o.val)

                getattr(block, handles[e])(body)


class Ring:
    def __init__(self, tiles):
        self.tiles = tiles
        self.bufs = [Buf() for _ in tiles]
        self.i = 0

    def next(self):
        k = self.i % len(self.tiles)
        self.i += 1
        return self.tiles[k], self.bufs[k]


def build(upto="all"):
    nc = bass.Bass("TRN2", target_bir_lowering=False)
    S = Sched(nc)

    def din(name, shape):
        return nc.dram_tensor(name, shape, F32, kind="ExternalInput").ap()

    xT_d = din("xT", [D, T])
    w_in0 = din("w_in0", [D, 10304])
    convw_d = din("convw", [128, 48 * 4])
    convb_d = din("convb", [128, 48])
    dtb_d = din("dtb", [128, 64])
    alog_d = din("alog", [128, 64])
    dskip_d = din("dskip", [128, 32])
    normg_d = din("normg", [128, 32])
    w_out0 = din("w_out0", [4096, D])
    lng_d = din("lng", [128, 32])
    lnb_d = din("lnb", [128, 32])
    w_in1 = din("w_in1", [D, 4944])
    qng_d = din("qng", [128, 4])
    kvng_d = din("kvng", [128, 2])
    w_uq = din("w_uq", [512, 4096])
    w_ukT = din("w_ukT", [32 * 128, 256])
    w_uv = din("w_uv", [32 * 256, 128])
    w_idxq = din("w_idxq", [512, 1024])
    w_out1 = din("w_out1", [4096, D])
    outT_d = nc.dram_tensor("outT", [D, T], F32, kind="ExternalOutput").ap()

    ygT_d = nc.dram_tensor("ygT", [4096, T], BF16).ap()
    x1T_d = nc.dram_tensor("x1T", [D, T], F32).ap()
    gateT_d = nc.dram_tensor("gateT", [4096, T], BF16).ap()
    wsc_h = [nc.dram_tensor("wsc%d" % l, [16 * 128, 16 * 256], BF16) for l in range(2)]
    b_wsc = [Buf("wsc0"), Buf("wsc1")]
    csT_h = nc.dram_tensor("csTd", [64, T], F32)
    csT_d = csT_h.ap()
    b_csTd = Buf("csTd")
    b_ygT, b_x1T, b_gateT, b_outT = Buf("ygT"), Buf("x1T"), Buf("gateT"), Buf("outT")

    with ExitStack() as top:
        _cnt = [0]

        def sb(name, shape, dt=F32, es=top):
            _cnt[0] += 1
            return es.enter_context(nc.sbuf_tensor("sb%d_%s" % (_cnt[0], name), shape, dt))

        def mm(out, lhsT, rhs, start, stop, rd, wr):
            S.op("pe", lambda h: h.matmul(out, lhsT=lhsT, rhs=rhs, start=start, stop=stop), rd, wr)

        def tr(out, in_, idn, rd, wr):
            S.op("pe", lambda h: h.transpose(out=out, in_=in_, identity=idn), rd, wr)

        def act(out, in_, func, rd, wr, bias=None, scale=None, eng="act"):
            kw = {}
            if bias is not None:
                kw["bias"] = bias
            if scale is not None:
                kw["scale"] = scale
            S.op(eng, lambda h: h.activation(out=out, in_=in_, func=func, **kw), rd, wr)

        def amul(out, in_, c, rd, wr):
            S.op("act", lambda h: h.mul(out, in_, c), rd, wr)

        def acopy(out, in_, rd, wr):
            S.op("act", lambda h: h.copy(out, in_), rd, wr)

        def tt(out, in0, in1, op, rd, wr, eng="dve"):
            S.op(eng, lambda h: h.tensor_tensor(out=out, in0=in0, in1=in1, op=op), rd, wr)

        def ts(out, in0, s1, s2, op0, op1, rd, wr, accum=None, eng="dve"):
            kw = {}
            if op1 is not None:
                kw["op1"] = op1
            if accum is not None:
                kw["accum_out"] = accum
            S.op(eng, lambda h: h.tensor_scalar(out=out, in0=in0, scalar1=s1, scalar2=s2, op0=op0, **kw), rd, wr)

        def stt(out, in0, scalar, in1, op0, op1, rd, wr, eng="dve"):
            S.op(eng, lambda h: h.scalar_tensor_tensor(out=out, in0=in0, scalar=scalar, in1=in1, op0=op0, op1=op1), rd, wr)

        def vcopy(out, in_, rd, wr, eng="dve"):
            S.op(eng, lambda h: h.tensor_copy(out, in_), rd, wr)

        def recip(out, in_, rd, wr):
            S.op("dve", lambda h: h.reciprocal(out, in_), rd, wr)

        def memset(ap, val, wr, eng="pool"):
            S.op(eng, lambda h: h.memset(ap, val), (), wr)

        def dma(eng, out, in_, rd, wr):
            return S.dma(eng, lambda h: h.dma_start(out=out, in_=in_), rd, wr)

        psb = [top.enter_context(nc.psum_tensor("ps%d" % i, [128, 512], F32)) for i in range(8)]
        psbuf = [Buf("ps%d" % i) for i in range(8)]

        b_c = Buf("consts")
        identf = sb("identf", [128, 128])
        ident = sb("ident", [128, 128], BF16)
        U128 = sb("U128", [128, 128])
        onesf = sb("onesf", [128, 128])
        onesb = sb("onesb", [128, 128], BF16)
        memset(identf[:], 0.0, [b_c])
        S.op("pool", lambda h: h.affine_select(out=identf[:], in_=identf[:], pattern=[[-1, 128]], compare_op=ALU.not_equal,
                                                fill=1.0, base=0, channel_multiplier=1), [b_c], [b_c])
        memset(onesf[:], 1.0, [b_c])
        memset(onesb[:], 1.0, [b_c])
        memset(U128[:], 1.0, [b_c])
        S.op("pool", lambda h: h.affine_select(out=U128[:], in_=U128[:], pattern=[[1, 128]], compare_op=ALU.is_ge,
                                                fill=0.0, base=0, channel_multiplier=-1), [b_c], [b_c])
        vcopy(ident[:], identf[:], [b_c], [b_c], eng="pool")

        b_par = Buf("params")
        convw = sb("convw", [128, 48 * 4])
        convb = sb("convb", [128, 48])
        dtb = sb("dtb", [128, 64])
        alog = sb("alog", [128, 64])
        dskip = sb("dskip", [128, 32])
        normg = sb("normg", [128, 32])
        lng = sb("lng", [128, 32])
        lnb = sb("lnb", [128, 32])
        qng = sb("qng", [128, 4])
        kvng = sb("kvng", [128, 2])
        for t_, d_ in ((convw, convw_d), (convb, convb_d), (dtb, dtb_d), (alog, alog_d), (dskip, dskip_d),
                       (normg, normg_d), (lng, lng_d), (lnb, lnb_d), (qng, qng_d), (kvng, kvng_d)):
            dma("sp", t_[:], d_[:, :], [], [b_par])

        NW = 2
        wring = Ring([sb("wt%d" % i, [128, 16, 256], BF16) for i in range(NW)])

        xstk = ExitStack()
        xTb = sb("xTb", [128, 16, T], BF16, xstk)
        b_xTb = Buf("xTb")
        for c in range(16):
            dma("pool", xTb[:, c, :], xT_d[c * 128:(c + 1) * 128, :], [], [b_xTb])

        def load_w(src, kc, ncols):
            wt, wb = wring.next()
            dma("pool", wt[:, 0:kc, 0:ncols], src.rearrange("(c p) n -> p c n", p=128), [], [wb])
            return wt, wb

        precast_jobs = []
        for l_, wsrc in ((0, w_out0), (1, w_out1)):
            for dp in range(8):
                for kh in range(2):
                    precast_jobs.append((l_, wsrc, dp, kh))
        precast_pos = [0]

        def precast_step(layer_limit):
            if precast_pos[0] >= len(precast_jobs):
                return
            l_, wsrc, dp, kh = precast_jobs[precast_pos[0]]
            if l_ > layer_limit:
                return
            precast_pos[0] += 1
            ti = dp * 2 + kh
            dst = wsc_h[l_].ap()[ti * 128:(ti + 1) * 128, :].rearrange("p (c n) -> p c n", c=16)
            src = wsrc[kh * 2048:(kh + 1) * 2048, dp * 256:(dp + 1) * 256].rearrange("(c p) n -> p c n", p=128)
            dma("pool", dst, src, [], [b_wsc[l_]])

        def load_wsc(l_, dp, kh):
            wt, wb = wring.next()
            ti = dp * 2 + kh
            dma("pool", wt[:, :, :], wsc_h[l_].ap()[ti * 128:(ti + 1) * 128, :].rearrange("p (c n) -> p c n", c=16),
                [b_wsc[l_]], [wb])
            return wt, wb

        def outproj_ln(es, actT_d, b_act, w_d, res_d, b_res, layer, out_d, b_out, is_final, write_bf):
            nbuf = 2 if layer == 1 else 1
            Ar = Ring([sb("opA%d" % i, [128, 32, 512], BF16, es) for i in range(nbuf)])
            xrr = Ring([sb("opX%d" % i, [128, 16, 512], F32, es) for i in range(nbuf)])

            def preload(R):
                rs_ = slice(R * 512, (R + 1) * 512)
                A_, bA_ = Ar.next()
                for hh in range(2):
                    dma("sp", A_[:, hh * 16:(hh + 1) * 16, :],
                        actT_d[hh * 2048:(hh + 1) * 2048, rs_].rearrange("(c p) t -> p c t", p=128), [b_act], [bA_])
                x_, bx_ = xrr.next()
                dma("sp", x_[:, :, :], res_d[:, rs_].rearrange("(c p) t -> p c t", p=128), [b_res], [bx_])
                return A_, bA_, x_, bx_
            sqr = Ring([sb("opsq%d" % i, [128, 512], F32, es) for i in range(2)])
            o32r = Ring([sb("opo%d" % i, [128, 512], F32, es) for i in range(2)])
            xnr = Ring([sb("opxn%d" % i, [128, 512], F32, es) for i in range(2)])
            mean = sb("opmean", [128, 512], F32, es)
            msq = sb("opmsq", [128, 512], F32, es)
            var = sb("opvar", [128, 512], F32, es)
            rstd = sb("oprstd", [128, 512], F32, es)
            nb = sb("opnb", [128, 512], F32, es)
            b_st = Buf()
            for _ in range(32):
                precast_step(layer)
            pring = Ring([psb[0], psb[1]])
            pring.bufs = [psbuf[0], psbuf[1]]
            P1, P2 = psb[2], psb[3]
            nxt_ld = preload(0)
            for R in range(4):
                rs = slice(R * 512, (R + 1) * 512)
                A, b_A, xr, b_xr = nxt_ld
                if nbuf == 2 and R < 3:
                    nxt_ld = preload(R + 1)
                for dp in range(8):
                    w0, wb0 = load_wsc(layer, dp, 0)
                    w1, wb1 = load_wsc(layer, dp, 1)
                    for dq in range(2):
                        dch = dp * 2 + dq
                        ps, pb = pring.next()
                        for kk in range(32):
                            wt, wb = (w0, wb0) if kk < 16 else (w1, wb1)
                            mm(ps[:, :], wt[:, kk % 16, dq * 128:(dq + 1) * 128], A[:, kk, :], kk == 0, kk == 31,
                               [wb, b_A], [pb])
                        stt(xr[:, dch, :], xr[:, dch, :], ALPHA, ps[:, :], ALU.mult, ALU.add, [b_xr, pb], [b_xr])
                        sq, bsq = sqr.next()
                        act(sq[:], xr[:, dch, :], AF.Square, [b_xr], [bsq])
                        mm(P1[:, :], onesf[:], xr[:, dch, :], dch == 0, dch == 15, [b_c, b_xr], [psbuf[2]])
                        mm(P2[:, :], onesf[:], sq[:], dch == 0, dch == 15, [b_c, bsq], [psbuf[3]])
                amul(mean[:], P1[:, :], 1.0 / D, [psbuf[2]], [b_st])
                tt(msq[:], mean[:], mean[:], ALU.mult, [b_st], [b_st])
                stt(var[:], P2[:, :], 1.0 / D, msq[:], ALU.mult, ALU.subtract, [psbuf[3], b_st], [b_st])
                ts(var[:], var[:], LN_EPS, None, ALU.add, None, [b_st], [b_st])
                recip(var[:], var[:], [b_st], [b_st])
                act(rstd[:], var[:], AF.Sqrt, [b_st], [b_st])
                stt(nb[:], mean[:], -1.0, rstd[:], ALU.mult, ALU.mult, [b_st], [b_st])
                for dch in range(16):
                    xn, bxn = xnr.next()
                    tt(xn[:], xr[:, dch, :], rstd[:], ALU.mult, [b_xr, b_st], [bxn])
                    tt(xn[:], xn[:], nb[:], ALU.add, [bxn, b_st], [bxn])
                    o32, bo = o32r.next()
                    col = layer * 16 + dch
                    act(o32[:], xn[:], AF.Identity, [bxn, b_par], [bo], bias=lnb[:, col:col + 1], scale=lng[:, col:col + 1])
                    if write_bf:
                        act(xTb[:, dch, rs], xn[:], AF.Identity, [bxn, b_par], [b_xTb], bias=lnb[:, col:col + 1], scale=lng[:, col:col + 1])
                    o = dma("sp", out_d[dch * 128:(dch + 1) * 128, rs], o32[:], [bo], [b_out])
                    if is_final:
                        S.out_ops.append(o)
                if nbuf == 1 and R < 3:
                    nxt_ld = preload(R + 1)

        with ExitStack() as l0:
            def sb0(name, shape, dt=F32):
                return sb(name, shape, dt, l0)

            dt_tok = sb0("dt_tok", [128, 16, 64])
            negcs = sb0("negcs", [128, 16, 64])
            dtw = sb0("dtw", [128, 16, 64])
            etot = sb0("etot", [128, 8, 64])
            b_dt = Buf("dt")
            with ExitStack() as l0a:
                wdt = sb("wdt", [128, 16, 64], BF16, l0a)
                adt = sb("adt", [128, 16, 64], F32, l0a)
                cstr = Ring([sb("cst%d" % i, [64, 256], F32, l0a) for i in range(2)])
                b_wdt = Buf()
                tmpA = sb("tmpA", [128, 16, 64], F32, l0a)
                ea = sb("ea", [128, 64], F32, l0a)
                b_tmp = Buf()
                dma("pool", wdt[:, :, :], w_in0[:, 10240:10304].rearrange("(c p) n -> p c n", p=128), [], [b_wdt])
                for j in range(16):
                    bank = psb[j // 8]
                    jj = j % 8
                    for k in range(16):
                        mm(bank[:, jj * 64:(jj + 1) * 64], xTb[:, k, j * 128:(j + 1) * 128], wdt[:, k, :], k == 0, k == 15,
                           [b_xTb, b_wdt], [psbuf[j // 8]])
                for hf in range(2):
                    tt(tmpA[:, hf * 8:(hf + 1) * 8, :], psb[hf][:, :].rearrange("p (a b) -> p a b", a=8),
                       dtb[:].unsqueeze(1).to_broadcast([128, 8, 64]), ALU.add, [psbuf[hf], b_par], [b_tmp])
                act(tmpA[:], tmpA[:], AF.Exp, [b_tmp], [b_tmp])
                act(dt_tok[:], tmpA[:], AF.Ln, [b_tmp], [b_dt], bias=1.0)
                act(ea[:], alog[:], AF.Exp, [b_par], [b_tmp])
                stt(adt[:], dt_tok[:], -1.0, ea[:].unsqueeze(1).to_broadcast([128, 16, 64]), ALU.mult, ALU.mult,
                    [b_dt, b_tmp], [b_dt])
                for b in range(16):
                    bank = psb[2 + b // 8]
                    bb = b % 8
                    o_ = bank[:, bb * 64:(bb + 1) * 64]
                    if b % 2 == 0:
                        mm(o_, U128[:], adt[:, b, :], True, True, [b_c, b_dt], [psbuf[2 + b // 8]])
                    else:
                        mm(o_, onesf[:], adt[:, b - 1, :], True, False, [b_c, b_dt], [psbuf[2 + b // 8]])
                        mm(o_, U128[:], adt[:, b, :], False, True, [b_c, b_dt], [psbuf[2 + b // 8]])
                for c in range(8):
                    o_ = psb[4][:, c * 64:(c + 1) * 64]
                    mm(o_, onesf[:], adt[:, 2 * c, :], True, False, [b_c, b_dt], [psbuf[4]])
                    mm(o_, onesf[:], adt[:, 2 * c + 1, :], False, True, [b_c, b_dt], [psbuf[4]])
                for hf in range(2):
                    amul(negcs[:, hf * 8:(hf + 1) * 8, :], psb[2 + hf][:, :].rearrange("p (a b) -> p a b", a=8), -1.0,
                         [psbuf[2 + hf]], [b_dt])
                tt(tmpA[:].rearrange("p (c e) h -> p c e h", e=2), negcs[:].rearrange("p (c e) h -> p c e h", e=2),
                   psb[4][:, :].rearrange("p (c h) -> p c h", c=8).unsqueeze(2).to_broadcast([128, 8, 2, 64]),
                   ALU.add, [b_dt, psbuf[4]], [b_tmp])
                act(tmpA[:], tmpA[:], AF.Exp, [b_tmp], [b_tmp])
                tt(dtw[:], dt_tok[:], tmpA[:], ALU.mult, [b_dt, b_tmp], [b_dt])
                act(etot[:], psb[4][:, :].rearrange("p (c h) -> p c h", c=8), AF.Exp, [psbuf[4]], [b_dt])
                for c in range(8):
                    bk = 5 + c % 2
                    ps = psb[bk]
                    mm(ps[0:64, 0:128], adt[:, 2 * c, :], U128[:], True, True, [b_dt, b_c], [psbuf[bk]])
                    mm(ps[0:64, 128:256], adt[:, 2 * c, :], onesf[:], True, False, [b_dt, b_c], [psbuf[bk]])
                    mm(ps[0:64, 128:256], adt[:, 2 * c + 1, :], U128[:], False, True, [b_dt, b_c], [psbuf[bk]])
                    cst, bcst = cstr.next()
                    amul(cst[:], ps[0:64, 0:256], -1.0, [psbuf[bk]], [bcst])
                    dma("sp", csT_d[:, c * 256:(c + 1) * 256], cst[:], [bcst], [b_csTd])

            S.fence()
            with ExitStack() as l0b:
                def sbb(name, shape, dt=F32):
                    return sb(name, shape, dt, l0b)

                zs = sbb("zs", [128, 4, T], BF16)
                xsT = sbb("xsT", [128, 4, T], BF16)
                BT = sbb("BT", [128, T], BF16)
                CT = sbb("CT", [128, T], BF16)
                b_zs, b_xs, b_BT, b_CT = Buf(), Buf(), Buf(), Buf()
                stg = [sbb("stg%d" % i, [128, 515]) for i in range(2)]
                b_stg = [Buf(), Buf()]
                accr = Ring([sbb("cacc%d" % i, [128, 512]) for i in range(1)])
                xdt_pad = [sbb("xdtp%d" % i, [128, 2, 8, 128], BF16) for i in range(2)]
                b_xdt = [Buf(), Buf()]
                xdtw = [sbb("xdtw%d" % i, [128, 2, 512], BF16) for i in range(2)]
                b_xdtw = [Buf(), Buf()]
                Btok = [sbb("Btok%d" % i, [128, 256], BF16) for i in range(2)]
                b_Btok = [Buf(), Buf()]
                cbT = [sbb("cbT%d" % i, [128, 384]) for i in range(2)]
                b_cbT = [Buf(), Buf()]
                csbr = Ring([sbb("csb%d" % i, [128, 8, 256]) for i in range(2)])
                x3 = sbb("x3", [128, 8, 3, 128])
                b_x3 = Buf()
                ecs4 = sbb("ecs4", [128, 4, 256], BF16)
                b_ecs4 = Buf()
                diagD = sbb("diagD", [128, 8, 128], BF16)
                b_diagD = Buf()
                dhi_b = sbb("dhi_b", [128, 32], BF16)
                dhi = sbb("dhi", [128, 32])
                dlo = sbb("dlo", [128, 32])
                b_dsp = Buf()
                vcopy(dhi_b[:], dskip[:], [b_par], [b_dsp], eng="pool")
                vcopy(dhi[:], dhi_b[:], [b_dsp], [b_dsp], eng="pool")
                tt(dlo[:], dskip[:], dhi[:], ALU.subtract, [b_par, b_dsp], [b_dsp], eng="pool")
                GTr = Ring([sbb("GT%d" % i, [128, 384], BF16) for i in range(4)])
                yoffr = Ring([sbb("yoff%d" % i, [128, 256]) for i in range(1)])
                ysb = sbb("ysb", [128, 4, 256])
                b_y = [Buf() for _ in range(4)]
                sqr0 = Ring([sbb("sq0_%d" % i, [128, 256], BF16) for i in range(2)])
                rs0 = sbb("rs0", [128, 256])
                b_rs0 = Buf()
                ygr = Ring([sbb("yg%d" % i, [128, 4, 256], BF16) for i in range(1)])
                S32 = sbb("S32", [128, 512])
                Sbf = sbb("Sbf", [128, 512], BF16)
                b_S, b_Sbf = Buf(), Buf()
                for i in range(2):
                    memset(xdt_pad[i][:], 0.0, [b_xdt[i]])

                pring = Ring([psb[0], psb[1]])
                pring.bufs = [psbuf[0], psbuf[1]]
                pzr = Ring([psb[3][:, 0:256], psb[6][:, 0:256]])
                pzr.bufs = [psbuf[3], psbuf[6]]
                pyr = Ring([psb[4][:, 0:256], psb[5][:, 0:256]])
                pyr.bufs = [psbuf[4], psbuf[5]]
                P_ssq = psb[2]
                b_Pssq = psbuf[2]

                def conv_evac(cc, R, ps, pb, dest, b_dest):
                    st, bst = stg[R % 2], b_stg[R % 2]
                    if R == 0:
                        memset(st[:, 0:3], 0.0, [bst], eng="dve")
                    else:
                        acopy(st[:, 0:3], stg[(R - 1) % 2][:, 512:515], [b_stg[(R - 1) % 2]], [bst])
                    acopy(st[:, 3:515], ps, [pb], [bst])
                    acc, bacc = accr.next()
                    ts(acc[:], st[:, 0:512], convw[:, cc * 4:cc * 4 + 1], None, ALU.mult, None, [bst, b_par], [bacc])
                    for k in range(1, 4):
                        stt(acc[:], st[:, k:k + 512], convw[:, cc * 4 + k:cc * 4 + k + 1], acc[:], ALU.mult, ALU.add,
                            [bst, b_par, bacc], [bacc])
                    act(dest, acc[:], AF.Silu, [bacc, b_par], [b_dest], bias=convb[:, cc:cc + 1])

                def inproj_cols(col0, ncols, evac):
                    wt, wb = load_w(w_in0[:, col0:col0 + ncols], 16, ncols)
                    if g >= 1 and ncols == 256:
                        precast_step(0)
                    for qq in range(ncols // 128):
                        for R in range(4):
                            ps, pb = pring.next()
                            for k in range(16):
                                mm(ps[:, :], wt[:, k, qq * 128:(qq + 1) * 128], xTb[:, k, R * 512:(R + 1) * 512],
                                   k == 0, k == 15, [wb, b_xTb], [pb])
                            evac(qq, R, ps[:, :], pb)

                csb_next = None

                def load_csb(g_, c_):
                    t_, b_ = csbr.next()
                    src = bass.AP(csT_h, 8 * g_ * T + c_ * 256, [[0, 128], [T, 8], [1, 256]])
                    dma("sp", t_[:, :, :], src, [b_csTd], [b_])
                    return t_, b_

                for g in range(8):
                    for pr in range(4):
                        for hl, dsrc in ((0, dhi), (1, dlo)):
                            S.op("dve", lambda h, pr=pr, hl=hl, dsrc=dsrc, g=g: h.tensor_scalar(
                                out=diagD[:, 2 * pr + hl, :], in0=identf[:], scalar1=dsrc[:, 4 * g + pr:4 * g + pr + 1],
                                scalar2=None, op0=ALU.mult), [b_c, b_dsp], [b_diagD])
                    for half in range(2):
                        def ev_z(qq, R, ps, pb, half=half):
                            act(zs[:, half * 2 + qq, R * 512:(R + 1) * 512], ps, AF.Silu, [pb], [b_zs])
                        inproj_cols(g * 512 + half * 256, 256, ev_z)
                    for half in range(2):
                        def ev_x(qq, R, ps, pb, half=half):
                            q4 = half * 2 + qq
                            conv_evac(4 * g + q4, R, ps, pb, xsT[:, q4, R * 512:(R + 1) * 512], b_xs)
                        inproj_cols(4096 + g * 512 + half * 256, 256, ev_x)
                    inproj_cols(8192 + g * 128, 128,
                                lambda qq, R, ps, pb: conv_evac(32 + g, R, ps, pb, BT[:, R * 512:(R + 1) * 512], b_BT))
                    inproj_cols(9216 + g * 128, 128,
                                lambda qq, R, ps, pb: conv_evac(40 + g, R, ps, pb, CT[:, R * 512:(R + 1) * 512], b_CT))
                    hs = slice(8 * g, 8 * g + 8)

                    def prologue_pe(c):
                        nonlocal csb_next
                        t0 = c * 256
                        if g == 0 and c == 0:
                            csb_next = load_csb(0, 0)
                        csb, bcsb = csb_next
                        if c < 7:
                            csb_next = load_csb(g, c + 1)
                        elif g < 7:
                            csb_next = load_csb(g + 1, 0)
                        pxs_, b_Pxs = pring.next()
                        P_xstok = pxs_[:, :].bitcast(BF16)
                        pbt_, b_Pbt = pring.next()
                        P_btok = pbt_[:, 0:128].bitcast(BF16)
                        for j in range(2):
                            for q in range(4):
                                tr(P_xstok[:, j * 512 + q * 128: j * 512 + (q + 1) * 128],
                                   xsT[:, q, t0 + j * 128: t0 + (j + 1) * 128], ident[:], [b_xs, b_c], [b_Pxs])
                            tr(P_btok[:, j * 128:(j + 1) * 128], BT[:, t0 + j * 128: t0 + (j + 1) * 128], ident[:],
                               [b_BT, b_c], [b_Pbt])
                        return dict(csb=csb, bcsb=bcsb, P_xstok=P_xstok, b_Pxs=b_Pxs, P_btok=P_btok, b_Pbt=b_Pbt)

                    def prologue_rest1(c, h):
                        t0 = c * 256
                        pp = c % 2
                        P_xstok, b_Pxs, P_btok, b_Pbt = h["P_xstok"], h["b_Pxs"], h["P_btok"], h["b_Pbt"]
                        for j in range(2):
                            blk = P_xstok[:, j * 512:(j + 1) * 512]
                            xs4 = blk.rearrange("p (a e d) -> p a e d", a=4, e=2)
                            for par in range(2):
                                tt(xdt_pad[pp][:, j, par::2, par * 64:(par + 1) * 64], xs4[:, :, par, :],
                                   dt_tok[:, 2 * c + j, 8 * g + par:8 * g + 8:2].unsqueeze(2).to_broadcast([128, 4, 64]),
                                   ALU.mult, [b_Pxs, b_dt], [b_xdt[pp]])
                            tt(xdtw[pp][:, j, :].rearrange("p (a d) -> p a d", a=8), blk.rearrange("p (a d) -> p a d", a=8),
                               dtw[:, 2 * c + j, hs].unsqueeze(2).to_broadcast([128, 8, 64]), ALU.mult,
                               [b_Pxs, b_dt], [b_xdtw[pp]])
                        acopy(Btok[pp][:], P_btok, [b_Pbt], [b_Btok[pp]])
                        pcb_, b_Pcb = pring.next()
                        P_cbT = pcb_[:, 0:384]
                        mm(P_cbT[:, 0:256], BT[:, t0:t0 + 128], CT[:, t0:t0 + 256], True, True, [b_BT, b_CT], [b_Pcb])
                        mm(P_cbT[:, 256:384], BT[:, t0 + 128:t0 + 256], CT[:, t0 + 128:t0 + 256], True, True,
                           [b_BT, b_CT], [b_Pcb])
                        tt(cbT[pp][:].rearrange("p (a b) -> p a b", a=3)[:, 0::2, :], P_cbT.rearrange("p (a b) -> p a b", a=3)[:, 0::2, :],
                           U128[:].unsqueeze(1).to_broadcast([128, 2, 128]), ALU.mult, [b_Pcb, b_c], [b_cbT[pp]])
                        acopy(cbT[pp][:, 128:256], P_cbT[:, 128:256], [b_Pcb], [b_cbT[pp]])
                        h["pst"] = h["pbst"] = None
                        if c < 7:
                            pst, pbst = psb[7], psbuf[7]
                            mm(pst[:, :], Btok[pp][:, 0:128], xdtw[pp][:, 0, :], True, False, [b_Btok[pp], b_xdtw[pp]], [pbst])
                            mm(pst[:, :], Btok[pp][:, 128:256], xdtw[pp][:, 1, :], False, True, [b_Btok[pp], b_xdtw[pp]], [pbst])
                            h["pst"], h["pbst"] = pst, pbst

                    def prologue_rest2(c, h):
                        csb, bcsb = h["csb"], h["bcsb"]
                        if c > 0:
                            for par in range(2):
                                act(ecs4[par * 64:(par + 1) * 64, :, :], csb[par * 64:(par + 1) * 64, par::2, :], AF.Exp,
                                    [bcsb], [b_ecs4], scale=-1.0)
                        nb0 = negcs[:, 2 * c, hs].unsqueeze(2).to_broadcast([128, 8, 128])
                        nb1 = negcs[:, 2 * c + 1, hs].unsqueeze(2).to_broadcast([128, 8, 128])
                        tt(x3[:, :, 0, :], csb[:, :, 0:128], nb0, ALU.max, [bcsb, b_dt], [b_x3])
                        tt(x3[:, :, 2, :], csb[:, :, 128:256], nb1, ALU.max, [bcsb, b_dt], [b_x3])
                        tt(x3[:, :, 1, :], csb[:, :, 128:256], nb0, ALU.subtract, [bcsb, b_dt], [b_x3])
                        tt(x3[:, :, 0, :], x3[:, :, 0, :], nb0, ALU.subtract, [b_x3, b_dt], [b_x3])
                        tt(x3[:, :, 2, :], x3[:, :, 2, :], nb1, ALU.subtract, [b_x3, b_dt], [b_x3])
                        act(x3[:], x3[:], AF.Exp, [b_x3], [b_x3], scale=-1.0)

                    def pairs(c, hook):
                        t0 = c * 256
                        pp = c % 2

                        def pair_heads(pr):
                            yps, byp = pyr.next()
                            pz = None
                            if c > 0:
                                pz = pzr.next()
                                mm(pz[0], Sbf[:, pr * 128:(pr + 1) * 128], CT[:, t0:t0 + 256], True, True, [b_Sbf, b_CT], [pz[1]])
                            for par in range(2):
                                r = 2 * pr + par
                                hh = 8 * g + r
                                GT, bGT = GTr.next()
                                tt(GT[:], x3[:, r, :, :].rearrange("p a b -> p (a b)"), cbT[pp][:], ALU.mult, [b_x3, b_cbT[pp]], [bGT])
                                mm(yps[:, 0:256], xdt_pad[pp][:, 0, r, :], GT[:, 0:256], par == 0, False,
                                   [b_xdt[pp], bGT], [byp])
                                mm(yps[:, 128:256], xdt_pad[pp][:, 1, r, :], GT[:, 256:384], False, False,
                                   [b_xdt[pp], bGT], [byp])
                            mm(yps[:, 0:256], diagD[:, 2 * pr, :], xsT[:, pr, t0:t0 + 256], False, False, [b_diagD, b_xs], [byp])
                            mm(yps[:, 0:256], diagD[:, 2 * pr + 1, :], xsT[:, pr, t0:t0 + 256], False, True, [b_diagD, b_xs], [byp])
                            return yps, byp, pz

                        def pair_tail(pr, yps, byp, pz):
                            yv = ysb[:, pr, :]
                            if c > 0:
                                yoff, byo = yoffr.next()
                                tt(yoff[:], pz[0], ecs4[:, pr, :], ALU.mult, [pz[1], b_ecs4], [byo])
                                tt(yv, yps, yoff[:], ALU.add, [byp, byo], [b_y[pr]])
                            else:
                                acopy(yv, yps, [byp], [b_y[pr]])
                            tt(yv, yv, zs[:, pr, t0:t0 + 256], ALU.mult, [b_y[pr], b_zs], [b_y[pr]], eng="pool")
                            sq, bsq = sqr0.next()
                            tt(sq[:], yv, yv, ALU.mult, [b_y[pr]], [bsq], eng="pool")
                            mm(P_ssq[:, 0:256], onesb[:], sq[:], pr == 0, pr == 3, [b_c, bsq], [b_Pssq])

                        cur = pair_heads(0)
                        res = None
                        for pr in range(4):
                            nxtp = pair_heads(pr + 1) if pr < 3 else None
                            if pr == 3:
                                res = hook()
                            pair_tail(pr, *cur)
                            cur = nxtp
                        return res

                    def state_update(c, pst, pbst):
                        if c < 7:
                            if c == 0:
                                vcopy(S32[:], pst[:, :], [pbst], [b_S])
                            else:
                                tt(S32[:].rearrange("p (a d) -> p a d", a=8), S32[:].rearrange("p (a d) -> p a d", a=8),
                                   etot[:, c, hs].unsqueeze(2).to_broadcast([128, 8, 64]), ALU.mult, [b_S, b_dt], [b_S])
                                tt(S32[:], S32[:], pst[:, :], ALU.add, [b_S, pbst], [b_S])
                            acopy(Sbf[:], S32[:], [b_S], [b_Sbf])

                    def epilogue_a(c):
                        ts(rs0[:], P_ssq[:, 0:256], 1.0 / 512, RMS_EPS, ALU.mult, ALU.add, [b_Pssq], [b_rs0])
                        act(rs0[:], rs0[:], AF.Ln, [b_rs0], [b_rs0])
                        act(rs0[:], rs0[:], AF.Exp, [b_rs0], [b_rs0], scale=-0.5)

                    def epilogue_b(c):
                        t0 = c * 256
                        yg, byg = ygr.next()
                        for pr in range(4):
                            stt(yg[:, pr, :], ysb[:, pr, :], normg[:, 4 * g + pr:4 * g + pr + 1], rs0[:], ALU.mult, ALU.mult,
                                [b_y[pr], b_par, b_rs0], [byg])
                        dma("sp", ygT_d[g * 512:(g + 1) * 512, t0:t0 + 256].rearrange("(q p) t -> p q t", p=128), yg[:],
                            [byg], [b_ygT])

                    pro = prologue_pe(0)
                    prologue_rest1(0, pro)
                    prologue_rest2(0, pro)
                    for c in range(8):
                        nxt_pro = pairs(c, (lambda c=c: prologue_pe(c + 1)) if c < 7 else (lambda: None))
                        state_update(c, pro["pst"], pro["pbst"])
                        if c < 7:
                            prologue_rest1(c + 1, nxt_pro)
                        epilogue_a(c)
                        if c < 7:
                            prologue_rest2(c + 1, nxt_pro)
                        epilogue_b(c)
                        pro = nxt_pro

            S.fence()
            with ExitStack() as l0c:
                if upto == "l0":
                    outproj_ln(l0c, ygT_d, b_ygT, w_out0, xT_d, Buf(), 0, outT_d, b_outT, True, False)
                else:
                    outproj_ln(l0c, ygT_d, b_ygT, w_out0, xT_d, Buf(), 0, x1T_d, b_x1T, False, True)


        if upto != "l0":
            cqnT_d = nc.dram_tensor("cqnT", [512, T], BF16).ap()
            ckvT_d = nc.dram_tensor("ckvT", [256, T], BF16).ap()
            kdup_d = nc.dram_tensor("kdup", [128, T], BF16).ap()
            widx_d = nc.dram_tensor("widx", [128, 256], F32).ap()
            b_cqnT_d, b_ckvT_d, b_kdup_d, b_widx_d = Buf(), Buf(), Buf(), Buf()
            S.fence()
            with ExitStack() as l1a:
                def sba(name, shape, dt=F32):
                    return sb(name, shape, dt, l1a)
                cq32 = sba("cq32", [128, 4, T])
                ckv32 = sba("ckv32", [128, 2, T])
                b_cq32, b_ckv32 = Buf(), Buf()
                kst = sba("kst", [128, T], BF16)
                b_kst = Buf()
                wst = sba("wst", [128, 256])
                b_wst = Buf()
                gstr = Ring([sba("gst%d" % i, [128, 512], BF16) for i in range(3)])
                sq1r = Ring([sba("sq1_%d" % i, [128, 512]) for i in range(2)])
                rs1 = sba("rs1", [128, 512])
                b_rs1 = Buf()
                nrmr = Ring([sba("nrm%d" % i, [128, 512], BF16) for i in range(2)])
                pring = Ring([psb[0], psb[1], psb[2]])
                pring.bufs = [psbuf[0], psbuf[1], psbuf[2]]

                def inproj1(wt, wb, nsub, evac):
                    for qq in range(nsub):
                        for R in range(4):
                            ps, pb = pring.next()
                            for k in range(16):
                                mm(ps[:, :], wt[:, k, qq * 128:(qq + 1) * 128], xTb[:, k, R * 512:(R + 1) * 512],
                                   k == 0, k == 15, [wb, b_xTb], [pb])
                            evac(qq, R, ps[:, :], pb)

                for half in range(2):
                    wt, wb = load_w(w_in1[:, half * 256:(half + 1) * 256], 16, 256)
                    inproj1(wt, wb, 2, lambda qq, R, ps, pb, half=half:
                            acopy(cq32[:, half * 2 + qq, R * 512:(R + 1) * 512], ps, [pb], [b_cq32]))
                wt, wb = load_w(w_in1[:, 512:768], 16, 256)
                inproj1(wt, wb, 2, lambda qq, R, ps, pb: acopy(ckv32[:, qq, R * 512:(R + 1) * 512], ps, [pb], [b_ckv32]))
                wt, wb = wring.next()
                ksrc = w_in1[:, 768:832].rearrange("(c p) n -> p c n", p=128)
                S.dma("pool", lambda h, wt=wt: [h.dma_start(out=wt[:, 0:16, 0:64], in_=ksrc),
                                                h.dma_start(out=wt[:, 0:16, 64:128], in_=ksrc)], [], [wb], n=2)
                inproj1(wt, wb, 1, lambda qq, R, ps, pb: acopy(kst[:, R * 512:(R + 1) * 512], ps, [pb], [b_kst]))
                dma("sp", kdup_d[:, :], kst[:], [b_kst], [b_kdup_d])
                wt, wb = load_w(w_in1[:, 832:848], 16, 16)
                ps, pb = pring.next()
                for i in range(16):
                    for k in range(16):
                        mm(ps[:, i * 16:(i + 1) * 16], xTb[:, k, i * 128:(i + 1) * 128], wt[:, k, 0:16], k == 0, k == 15,
                           [wb, b_xTb], [pb])
                amul(wst[:], ps[:, 0:256], 0.25 * 0.125, [pb], [b_wst])
                dma("sp", widx_d[:, :], wst[:], [b_wst], [b_widx_d])
                for gt in range(16):
                    wt, wb = load_w(w_in1[:, 848 + gt * 256: 848 + (gt + 1) * 256], 16, 256)
                    precast_step(1)

                    def ev_g(qq, R, ps, pb, gt=gt):
                        g_, bg_ = gstr.next()
                        act(g_[:], ps, AF.Silu, [pb], [bg_])
                        fc = gt * 2 + qq
                        dma("sp", gateT_d[fc * 128:(fc + 1) * 128, R * 512:(R + 1) * 512], g_[:], [bg_], [b_gateT])
                    inproj1(wt, wb, 2, ev_g)
                for (src, bsrc, nch, gtile, dst_d, bdst) in ((cq32, b_cq32, 4, qng, cqnT_d, b_cqnT_d),
                                                            (ckv32, b_ckv32, 2, kvng, ckvT_d, b_ckvT_d)):
                    for R in range(4):
                        rs = slice(R * 512, (R + 1) * 512)
                        ps, pb = pring.next()
                        for q in range(nch):
                            sq, bsq = sq1r.next()
                            act(sq[:], src[:, q, rs], AF.Square, [bsrc], [bsq])
                            mm(ps[:, :], onesf[:], sq[:], q == 0, q == nch - 1, [b_c, bsq], [pb])
                        ts(rs1[:], ps[:, :], 1.0 / (128 * nch), RMS_EPS, ALU.mult, ALU.add, [pb], [b_rs1])
                        recip(rs1[:], rs1[:], [b_rs1], [b_rs1])
                        act(rs1[:], rs1[:], AF.Sqrt, [b_rs1], [b_rs1])
                        for q in range(nch):
                            nr, bnr = nrmr.next()
                            stt(nr[:], src[:, q, rs], gtile[:, q:q + 1], rs1[:], ALU.mult, ALU.mult, [bsrc, b_par, b_rs1], [bnr])
                            dma("sp", dst_d[q * 128:(q + 1) * 128, rs], nr[:], [bnr], [bdst])
            xstk.close()

            S.fence()
            with ExitStack() as l1:
                def sb1(name, shape, dt=F32):
                    return sb(name, shape, dt, l1)
                cqn = sb1("cqn", [128, 4, T], BF16)
                ckvT = sb1("ckvT", [128, 2, T], BF16)
                ckv_tok = sb1("ckv_tok", [128, 16, 256], BF16)
                kdup = sb1("kdup", [128, T], BF16)
                widx = sb1("widx", [128, 256])
                maskT = sb1("maskT", [128, 16, T], BF16)
                b_cqn, b_ckvT, b_ckvtok, b_kdup, b_widx, b_maskT = Buf(), Buf(), Buf(), Buf(), Buf(), Buf()
                dma("sp", cqn[:, :, :], cqnT_d[:, :].rearrange("(c p) t -> p c t", p=128), [b_cqnT_d], [b_cqn])
                dma("sp", ckvT[:, :, :], ckvT_d[:, :].rearrange("(c p) t -> p c t", p=128), [b_ckvT_d], [b_ckvT])
                dma("sp", kdup[:, :], kdup_d[:, :], [b_kdup_d], [b_kdup])
                dma("sp", widx[:, :], widx_d[:, :], [b_widx_d], [b_widx])
                pring = Ring([psb[0], psb[1], psb[7]])
                pring.bufs = [psbuf[0], psbuf[1], psbuf[7]]
                for j in range(16):
                    ps, pb = pring.next()
                    pv = ps[:, 0:128].bitcast(BF16)
                    for cc in range(2):
                        tr(pv[:, cc * 128:(cc + 1) * 128], ckvT[:, cc, j * 128:(j + 1) * 128], ident[:], [b_ckvT, b_c], [pb])
                    acopy(ckv_tok[:, j, :], pv, [pb], [b_ckvtok])

                with ExitStack() as l1b:
                    def sbi(name, shape, dt=F32):
                        return sb(name, shape, dt, l1b)
                    qidxT = sbi("qidxT", [128, 8, T], BF16)
                    b_qidx = Buf()
                    scr = Ring([sbi("sc%d" % i, [128, T]) for i in range(3)])
                    rlr = Ring([sbi("rl%d" % i, [128, 512]) for i in range(2)])
                    mkr = Ring([sbi("mk%d" % i, [128, T], BF16) for i in range(2)])
                    junk = sbi("junk", [128, T], BF16)
                    b_junk = Buf()
                    cneg = sbi("cneg", [128, 128])
                    b_cneg = Buf()
                    NIT = 26
                    bis = Ring([sbi("bis%d" % i, [128, 72]) for i in range(2)])
                    memset(cneg[:], 0.0, [b_cneg])
                    S.op("pool", lambda h: h.affine_select(out=cneg[:], in_=cneg[:], pattern=[[-1, 128]], compare_op=ALU.is_ge,
                                                            fill=-1e30, base=0, channel_multiplier=1), [b_cneg], [b_cneg])
                    for cp in range(4):
                        wt, wb = load_w(w_idxq[:, cp * 256:(cp + 1) * 256], 4, 256)
                        for qq in range(2):
                            for R in range(4):
                                ps, pb = pring.next()
                                for k in range(4):
                                    mm(ps[:, :], wt[:, k, qq * 128:(qq + 1) * 128], cqn[:, k, R * 512:(R + 1) * 512],
                                       k == 0, k == 3, [wb, b_cqn], [pb])
                                acopy(qidxT[:, cp * 2 + qq, R * 512:(R + 1) * 512], ps[:, :], [pb], [b_qidx])
                    bconst = sbi("bconst", [128, 64])
                    b_bconst = Buf()
                    for k in range(NIT):
                        memset(bconst[:, k:k + 1], -(2.0 ** -(k + 2)), [b_bconst])
                    for i in range(16):
                        memset(bconst[:, 32 + i:33 + i], 0.5 - (512.0 - (i + 1) * 128), [b_bconst])
                    ftab = sbi("ftab", [128, 32])
                    for k in range(NIT + 1):
                        memset(ftab[:, k:k + 1], 2.0 ** -(k + 1), [b_bconst])
                    junk2 = sbi("junk2", [128, T], BF16)
                    b_junk2 = Buf()

                    def scores(i):
                        Wi = (i + 1) * 128
                        sc, bsc = scr.next()
                        for Q in range((Wi + 511) // 512):
                            wq = min(512, Wi - Q * 512)
                            qs = slice(Q * 512, Q * 512 + wq)
                            for hd in range(16):
                                ip, hf = hd // 2, hd % 2
                                ps, pb = pring.next()
                                mm(ps[:, 0:wq], qidxT[hf * 64:(hf + 1) * 64, ip, i * 128:(i + 1) * 128],
                                   kdup[hf * 64:(hf + 1) * 64, qs], True, True, [b_qidx, b_kdup], [pb])
                                if hd == 0:
                                    ts(sc[:, qs], ps[:, 0:wq], 0.0, widx[:, i * 16:i * 16 + 1], ALU.max, ALU.mult, [pb, b_widx], [bsc])
                                else:
                                    rl, brl = rlr.next()
                                    ts(rl[:, 0:wq], ps[:, 0:wq], 0.0, widx[:, i * 16 + hd:i * 16 + hd + 1], ALU.max, ALU.mult,
                                       [pb, b_widx], [brl])
                                    tt(sc[:, qs], sc[:, qs], rl[:, 0:wq], ALU.add, [brl, bsc], [bsc])
                        tt(sc[:, i * 128:Wi], sc[:, i * 128:Wi], cneg[:], ALU.add, [bsc, b_cneg], [bsc])
                        return sc, bsc

                    def bisect(i, sc, bsc, use_act):
                        Wi = (i + 1) * 128
                        bt, bbt = bis.next()
                        lo = bt[:, 0:1]
                        if i < 2:
                            memset(lo, -1e29, [bbt], eng="dve")
                            return lambda: (lo, bbt)
                        hi = bt[:, 1:2]
                        w0 = bt[:, 2:3]
                        mid = bt[:, 3:4]
                        stp = bt[:, 4:5]
                        tab = bt[:, 40:40 + NIT + 1]
                        S.op("dve", lambda h: h.tensor_reduce(out=hi, in_=sc[:, 0:Wi], axis=mybir.AxisListType.X, op=ALU.max), [bsc], [bbt])
                        S.op("dve", lambda h: h.tensor_reduce(out=lo, in_=sc[:, 0:i * 128], axis=mybir.AxisListType.X, op=ALU.min), [bsc], [bbt])
                        tt(w0, hi, lo, ALU.subtract, [bbt], [bbt])
                        memset(bt[:, 8:8 + NIT], 0.0, [bbt], eng="dve")
                        if use_act:
                            ts(tab, ftab[:, 0:NIT + 1], w0, -1.0, ALU.mult, ALU.mult, [bbt, b_bconst], [bbt])
                            stt(mid, lo, -1.0, tab[:, 0:1], ALU.mult, ALU.add, [bbt], [bbt])
                        else:
                            ts(tab, ftab[:, 0:NIT + 1], w0, None, ALU.mult, None, [bbt, b_bconst], [bbt])
                            tt(mid, lo, tab[:, 0:1], ALU.add, [bbt], [bbt])
                        return lambda: bisect_loop(i, sc, bsc, use_act, bt, bbt)

                    def bisect_loop(i, sc, bsc, use_act, bt, bbt):
                        Wi = (i + 1) * 128
                        lo = bt[:, 0:1]
                        mid = bt[:, 3:4]
                        stp = bt[:, 4:5]
                        tab = bt[:, 40:40 + NIT + 1]
                        if not use_act:
                            for k in range(NIT):
                                ts(junk[:, 0:Wi], sc[:, 0:Wi], mid, 0.0, ALU.is_ge, ALU.add, [bsc, bbt], [b_junk, bbt],
                                   accum=bt[:, 8 + k:9 + k])
                                ts(stp, bt[:, 8 + k:9 + k], 256.0, 0.5, ALU.is_ge, ALU.subtract, [bbt], [bbt])
                                stt(mid, stp, tab[:, k:k + 1], mid, ALU.mult, ALU.add, [bbt], [bbt])
                            tt(lo, mid, tab[:, NIT:NIT + 1], ALU.subtract, [bbt], [bbt])
                        else:
                            for k in range(NIT):
                                S.op("act", lambda h, k=k: h.activation(out=junk2[:, 0:Wi], in_=sc[:, 0:Wi], func=AF.Sign, bias=mid, scale=1.0,
                                                                       accum_out=bt[:, 8 + k:9 + k]), [bsc, bbt], [b_junk2, bbt])
                                act(stp, bt[:, 8 + k:9 + k], AF.Sign, [bbt, b_bconst], [bbt], bias=bconst[:, 32 + i:33 + i])
                                act(mid, stp, AF.Identity, [bbt], [bbt], bias=mid, scale=tab[:, k + 1:k + 2])
                            stt(lo, mid, -1.0, tab[:, NIT:NIT + 1], ALU.mult, ALU.add, [bbt], [bbt])
                        return lo, bbt

                    def finish(i, sc, bsc, lo, bbt):
                        Wi = (i + 1) * 128
                        mk, bmk = mkr.next()
                        ts(mk[:, 0:Wi], sc[:, 0:Wi], lo, None, ALU.is_ge, None, [bsc, bbt], [bmk])
                        for j0 in range(0, i + 1, 4):
                            n = min(4, i + 1 - j0)
                            ps, pb = pring.next()
                            pv = ps[:, 0:256].bitcast(BF16)
                            for jj in range(n):
                                tr(pv[:, jj * 128:(jj + 1) * 128], mk[:, (j0 + jj) * 128:(j0 + jj + 1) * 128], ident[:], [bmk, b_c], [pb])
                            acopy(maskT[:, j0:j0 + n, i * 128:(i + 1) * 128], pv[:, 0:n * 128].rearrange("p (a b) -> p a b", a=n),
                                  [pb], [b_maskT])

                    scq = {0: scores(0), 1: scores(1)}
                    for i in range(16):
                        sa = scq.pop(i)
                        fa = bisect(i, sa[0], sa[1], True)
                        if i + 2 < 16:
                            scq[i + 2] = scores(i + 2)
                        la = fa()
                        finish(i, sa[0], sa[1], la[0], la[1])

                S.fence()
                with ExitStack() as l1c:
                    def sbc(name, shape, dt=F32):
                        return sb(name, shape, dt, l1c)
                    wqr = Ring([sbc("wq%d" % i, [128, 4, 128], BF16) for i in range(2)])
                    wkr = Ring([sbc("wk%d" % i, [128, 256], BF16) for i in range(2)])
                    wvr = Ring([sbc("wv%d" % i, [128, 2, 128], BF16) for i in range(2)])
                    qTr = Ring([sbc("qT%d" % i, [128, T], BF16) for i in range(2)])
                    qlr = Ring([sbc("ql%d" % i, [128, 2, T], BF16) for i in range(2)])
                    er = Ring([sbc("e%d" % i, [128, 512], BF16) for i in range(4)])
                    pTr = Ring([sbc("pT%d" % i, [128, 512], BF16) for i in range(4)])
                    rden = sbc("rden", [128, 512])
                    evr = Ring([sbc("ev%d" % i, [128, 3, 512]) for i in range(2)])
                    b_rden = Buf()
                    olr = Ring([sbc("oln%d" % i, [128, 2, 512], BF16) for i in range(2)])
                    gtr = Ring([sbc("gt%d" % i, [128, 512], BF16) for i in range(2)])
                    ogr = Ring([sbc("og%d" % i, [128, 512], BF16) for i in range(2)])
                    pring = Ring([psb[0], psb[1]])
                    pring.bufs = [psbuf[0], psbuf[1]]
                    plog = Ring([psb[2], psb[3], psb[7]])
                    plog.bufs = [psbuf[2], psbuf[3], psbuf[7]]
                    PA = [psb[4], psb[5]]
                    PD = psb[6]
                    SCALE = 128.0 ** -0.5
                    LAG = 3

                    def head_proj(hd):
                        wq_, bwq = wqr.next()
                        dma("pool", wq_[:, :, :], w_uq[:, hd * 128:(hd + 1) * 128].rearrange("(c p) n -> p c n", p=128), [], [bwq])
                        wk_, bwk = wkr.next()
                        dma("pool", wk_[:, :], w_ukT[hd * 128:(hd + 1) * 128, :], [], [bwk])
                        wv_, bwv = wvr.next()
                        dma("pool", wv_[:, :, :], w_uv[hd * 256:(hd + 1) * 256, :].rearrange("(c p) n -> p c n", p=128), [], [bwv])
                        qT, bqT = qTr.next()
                        ql, bql = qlr.next()
                        res = (ql, bql, wv_, bwv)
                        yield res
                        for R in range(4):
                            ps, pb = pring.next()
                            for k in range(4):
                                mm(ps[:, :], wq_[:, k, :], cqn[:, k, R * 512:(R + 1) * 512], k == 0, k == 3, [bwq, b_cqn], [pb])
                            acopy(qT[:, R * 512:(R + 1) * 512], ps[:, :], [pb], [bqT])
                            yield res
                        for cc in range(2):
                            for R in range(4):
                                ps, pb = pring.next()
                                mm(ps[:, :], wk_[:, cc * 128:(cc + 1) * 128], qT[:, R * 512:(R + 1) * 512], True, True, [bwk, bqT], [pb])
                                vcopy(ql[:, cc, R * 512:(R + 1) * 512], ps[:, :], [pb], [bql])
                                yield res

                    tuples = [(R, j) for R in range(4) for j in range(4 * R + 4)]
                    n = len(tuples)
                    nxt = None
                    deferred = []
                    for nxt in head_proj(0):
                        pass
                    for hd in range(32):
                        ql, bql, wv_, bwv = nxt
                        gen = head_proj(hd + 1) if hd + 1 < 32 else None
                        pts = {}

                        def stage_qk(t, ql=ql, bql=bql):
                            R, j = tuples[t]
                            r1 = (R + 1) * 512
                            tlo = max(j * 128, R * 512)
                            off = tlo - R * 512
                            pl, bpl = plog.next()
                            mm(pl[:, off:512], ckvT[:, 0, j * 128:(j + 1) * 128], ql[:, 0, tlo:r1], True, False, [b_ckvT, bql], [bpl])
                            mm(pl[:, off:512], ckvT[:, 1, j * 128:(j + 1) * 128], ql[:, 1, tlo:r1], False, True, [b_ckvT, bql], [bpl])
                            e_, be = er.next()
                            act(e_[:, off:512], pl[:, off:512], AF.Exp, [bpl], [be], scale=SCALE)
                            pT, bpT = pTr.next()
                            tt(pT[:, off:512], e_[:, off:512], maskT[:, j, tlo:r1], ALU.mult, [be, b_maskT], [bpT])
                            pts[t] = (pT, bpT, off)

                        def stage_pv(t, wv_=wv_, bwv=bwv, hd=hd):
                            R, j = tuples[t]
                            r1 = (R + 1) * 512
                            nj = 4 * R + 4
                            pT, bpT, off = pts.pop(t)
                            for cc in range(2):
                                mm(PA[cc][:, off:512], ckv_tok[:, j, cc * 128:(cc + 1) * 128], pT[:, off:512], j == 0, j == nj - 1,
                                   [b_ckvtok, bpT], [psbuf[4 + cc]])
                            mm(PD[:, off:512], onesb[:], pT[:, off:512], j == 0, j == nj - 1, [b_c, bpT], [psbuf[6]])
                            if j == nj - 1:
                                ev, bev = evr.next()
                                acopy(ev[:, 2, :], PD[:, :], [psbuf[6]], [bev])
                                vcopy(ev[:, 0, :], PA[0][:, :], [psbuf[4]], [bev])
                                acopy(ev[:, 1, :], PA[1][:, :], [psbuf[5]], [bev])
                                ol, bol = olr.next()
                                st_ = {}

                                def d_ln(ev=ev, bev=bev):
                                    act(rden[:], ev[:, 2, :], AF.Ln, [bev], [b_rden])

                                def d_exp():
                                    act(rden[:], rden[:], AF.Exp, [b_rden], [b_rden], scale=-1.0)

                                def d_ol(cc, ev=ev, bev=bev, ol=ol, bol=bol):
                                    tt(ol[:, cc, :], ev[:, cc, :], rden[:], ALU.mult, [bev, b_rden], [bol])

                                def d_oproj(ol=ol, bol=bol, wv_=wv_, bwv=bwv, hd=hd, R=R, r1=r1, st_=st_):
                                    ps, pb = pring.next()
                                    for cc in range(2):
                                        mm(ps[:, :], wv_[:, cc, :], ol[:, cc, :], cc == 0, cc == 1, [bwv, bol], [pb])
                                    gt_, bgt = gtr.next()
                                    dma("sp", gt_[:], gateT_d[hd * 128:(hd + 1) * 128, R * 512:r1], [b_gateT], [bgt])
                                    st_["v"] = (ps, pb, gt_, bgt)

                                def d_og(hd=hd, R=R, r1=r1, st_=st_):
                                    ps, pb, gt_, bgt = st_["v"]
                                    og, bog = ogr.next()
                                    tt(og[:], ps[:, :], gt_[:], ALU.mult, [pb, bgt], [bog])
                                    dma("sp", ygT_d[hd * 128:(hd + 1) * 128, R * 512:r1], og[:], [bog], [b_ygT])

                                deferred.extend([d_ln, d_exp, lambda: d_ol(0), lambda: d_ol(1), d_oproj, d_og])

                        for step in range(n + LAG):
                            if step < n:
                                stage_qk(step)
                            if step - LAG >= 0:
                                stage_pv(step - LAG)
                            if deferred:
                                deferred.pop(0)()
                            if gen is not None and step >= 6 and step % 2 == 0:
                                nxt = next(gen, nxt)
                        if gen is not None:
                            for nxt in gen:
                                pass
                    while deferred:
                        deferred.pop(0)()

            S.fence()
            with ExitStack() as l1d:
                outproj_ln(l1d, ygT_d, b_ygT, w_out1, x1T_d, b_x1T, 1, outT_d, b_outT, True, False)
        else:
            xstk.close()

        S.emit()
    return nc


_CACHE = {}


def _feat(v, nchunk):
    return np.ascontiguousarray(np.asarray(v, np.float32).reshape(nchunk, 128).T)


def kernel(x, ssd_w_in, ssd_conv_w, ssd_conv_b, ssd_dt_bias, ssd_a_log, ssd_d_skip, ssd_norm_g, ssd_w_out,
           dsa_w_in, dsa_q_norm_g, dsa_kv_norm_g, dsa_w_uq, dsa_w_uk, dsa_w_uv, dsa_w_idx_q, dsa_w_out,
           ln_g, ln_b, _upto="all"):
    f = lambda a: np.ascontiguousarray(np.asarray(a, np.float32))
    x = f(x)
    nb = x.shape[0]
    shared = {
        "w_in0": f(ssd_w_in[0]),
        "convw": np.ascontiguousarray(f(ssd_conv_w[0]).T.reshape(48, 128, 4).transpose(1, 0, 2).reshape(128, 192)),
        "convb": _feat(ssd_conv_b[0], 48),
        "dtb": np.ascontiguousarray(np.tile(f(ssd_dt_bias[0])[None, :], (128, 1))),
        "alog": np.ascontiguousarray(np.tile(f(ssd_a_log[0])[None, :], (128, 1))),
        "dskip": _feat(np.repeat(f(ssd_d_skip[0]), 64), 32),
        "normg": _feat(ssd_norm_g[0], 32),
        "w_out0": f(ssd_w_out[0]),
        "lng": np.ascontiguousarray(np.concatenate([_feat(ln_g[0], 16), _feat(ln_g[1], 16)], 1)),
        "lnb": np.ascontiguousarray(np.concatenate([_feat(ln_b[0], 16), _feat(ln_b[1], 16)], 1)),
        "w_in1": f(dsa_w_in[0]),
        "qng": _feat(dsa_q_norm_g[0], 4),
        "kvng": _feat(dsa_kv_norm_g[0], 2),
        "w_uq": f(dsa_w_uq[0]),
        "w_ukT": np.ascontiguousarray(f(dsa_w_uk[0]).transpose(0, 2, 1).reshape(32 * 128, 256)),
        "w_uv": np.ascontiguousarray(f(dsa_w_uv[0]).reshape(32 * 256, 128)),
        "w_idxq": f(dsa_w_idx_q[0]),
        "w_out1": f(dsa_w_out[0]),
    }
    if _upto not in _CACHE:
        _CACHE[_upto] = build(_upto)
    nc = _CACHE[_upto]
    in_maps = []
    for b in range(nb):
        m = dict(shared)
        m["xT"] = np.ascontiguousarray(x[b].T)
        in_maps.append(m)
    res = run_bass_kernel_spmd(nc, in_maps, core_ids=list(range(nb)))
    out = np.stack([np.ascontiguousarray(np.asarray(r["outT"], np.float32).T) for r in res.results], 0)
    return out
```

```python
import numpy as np
from contextlib import ExitStack
import concourse.bass as bass
import concourse.mybir as mybir
from concourse.bass_utils import run_bass_kernel_spmd

F32 = mybir.dt.float32
BF16 = mybir.dt.bfloat16
ALU = mybir.AluOpType
AF = mybir.ActivationFunctionType

T = 2048
D = 2048
ALPHA = 4.0 ** 0.25
LN_EPS = 1e-5
RMS_EPS = 1e-6
NEG = -30000.0
USE_ACT_BISECT = True


class Buf:
    __slots__ = ("name", "w", "r")

    def __init__(self, name=""):
        self.name = name
        self.w = None
        self.r = []


class Op:
    __slots__ = ("eng", "fn", "deps", "signal", "sem", "val", "is_dma", "ndma", "prev")

    def __init__(self, eng, fn, is_dma=False, ndma=1):
        self.eng = eng
        self.fn = fn
        self.deps = []
        self.signal = False
        self.sem = None
        self.val = None
        self.is_dma = is_dma
        self.ndma = ndma
        self.prev = 0


class Sched:
    ENGS = ("pe", "act", "dve", "pool", "sp")

    def __init__(self, nc, n_dma_sems=16):
        self.nc = nc
        self.ops = {e: [] for e in self.ENGS}
        self.n_dma_sems = n_dma_sems
        self.out_ops = []
        self.dmas = []

    def _add(self, op, reads, writes):
        deps = set()
        for b in reads:
            if b.w is not None:
                deps.add(b.w)
        for b in writes:
            if b.w is not None:
                deps.add(b.w)
            for r in b.r:
                deps.add(r)
        for d in deps:
            if d is op:
                continue
            if (not op.is_dma) and (not d.is_dma) and d.eng == op.eng:
                if op.eng == "pe":
                    continue
            d.signal = True
            op.deps.append(d)
        for b in reads:
            b.r.append(op)
        for b in writes:
            b.w = op
            b.r = []
        self.ops[op.eng].append(op)
        return op

    def op(self, eng, fn, reads=(), writes=()):
        return self._add(Op(eng, fn), list(reads), list(writes))

    def dma(self, eng, fn, reads=(), writes=(), n=1):
        o = Op(eng, fn, is_dma=True, ndma=n)
        o.signal = True
        self.dmas.append(o)
        return self._add(o, list(reads), list(writes))

    def fence(self):
        front = []
        for e in self.ENGS:
            for o in reversed(self.ops[e]):
                if o.fn is not None and not o.is_dma:
                    front.append(o)
                    break
        front += self.dmas
        self.dmas = []
        for e in self.ENGS:
            p = Op(e, None)
            for d in front:
                if d.eng == e and e == "pe" and not d.is_dma:
                    continue
                d.signal = True
                p.deps.append(d)
            self.ops[e].append(p)

    def emit(self):
        nc = self.nc
        with ExitStack() as es:
            esem = {e: es.enter_context(nc.semaphore("s_" + e)) for e in self.ENGS}
            dsems = {e: [es.enter_context(nc.semaphore("d_%s%d" % (e, i))) for i in range(self.n_dma_sems)]
                     for e in ("sp", "pool", "act")}
            for e in self.ENGS:
                cnt = 0
                dcnt = [0] * self.n_dma_sems
                k = 0
                for o in self.ops[e]:
                    if o.is_dma:
                        o.sem = dsems[e][k]
                        o.prev = dcnt[k]
                        dcnt[k] += 16 * o.ndma
                        o.val = dcnt[k]
                        k = (k + 1) % self.n_dma_sems
                    elif o.signal:
                        cnt += 1
                        o.sem = esem[e]
                        o.val = cnt
            handles = {"pe": "tensor", "act": "scalar", "dve": "vector", "pool": "gpsimd", "sp": "sync"}
            block = es.enter_context(nc.Block())
            out_ops = self.out_ops
            for e in self.ENGS:
                ops = self.ops[e]

                def body(h, ops=ops, e=e):
                    seen = {}
                    for o in ops:
                        deps = o.deps
                        if o.fn is None:
                            best = {}
                            for d in deps:
                                if id(d.sem) not in best or best[id(d.sem)].val < d.val:
                                    best[id(d.sem)] = d
                            deps = list(best.values())
                        for d in deps:
                            key = id(d.sem)
                            if seen.get(key, 0) >= d.val:
                                continue
                            h.wait_ge(d.sem, d.val)
                            seen[key] = d.val
                        if o.is_dma:
                            if o.prev > 0 and seen.get(id(o.sem), 0) < o.prev:
                                h.wait_ge(o.sem, o.prev)
                                seen[id(o.sem)] = o.prev
                            r = o.fn(h)
                            if not isinstance(r, (list, tuple)):
                                r = [r]
                            assert len(r) == o.ndma
                            for ins in r:
                                ins.then_inc(o.sem, 16)
                        elif o.fn is not None:
                            ins = o.fn(h)
                            if o.signal:
                                ins.then_inc(o.sem, 1)
                    if e == "sp":
                        for o in out_ops:
                            h.wait_ge(o.sem, o.val)

                getattr(block, handles[e])(body)


class Ring:
    def __init__(self, tiles):
        self.tiles = tiles
        self.bufs = [Buf() for _ in tiles]
        self.i = 0

    def next(self):
        k = self.i % len(self.tiles)
        self.i += 1
        return self.tiles[k], self.bufs[k]


def build(upto="all"):
    nc = bass.Bass("TRN2", target_bir_lowering=False)
    S = Sched(nc)

    def din(name, shape):
        return nc.dram_tensor(name, shape, F32, kind="ExternalInput").ap()

    xT_d = din("xT", [D, T])
    w_in0 = din("w_in0", [D, 10304])
    convw_d = din("convw", [128, 48 * 4])
    convb_d = din("convb", [128, 48])
    dtb_d = din("dtb", [128, 64])
    alog_d = din("alog", [128, 64])
    dskip_d = din("dskip", [128, 32])
    normg_d = din("normg", [128, 32])
    w_out0 = din("w_out0", [4096, D])
    lng_d = din("lng", [128, 32])
    lnb_d = din("lnb", [128, 32])
    w_in1 = din("w_in1", [D, 4944])
    qng_d = din("qng", [128, 4])
    kvng_d = din("kvng", [128, 2])
    w_uq = din("w_uq", [512, 4096])
    w_ukT = din("w_ukT", [32 * 128, 256])
    w_uv = din("w_uv", [32 * 256, 128])
    w_idxq = din("w_idxq", [512, 1024])
    w_out1 = din("w_out1", [4096, D])
    outT_d = nc.dram_tensor("outT", [D, T], F32, kind="ExternalOutput").ap()

    ygT_d = nc.dram_tensor("ygT", [4096, T], BF16).ap()
    x1T_d = nc.dram_tensor("x1T", [D, T], F32).ap()
    gateT_d = nc.dram_tensor("gateT", [4096, T], BF16).ap()
    wsc_h = [nc.dram_tensor("wsc%d" % l, [16 * 128, 16 * 256], BF16) for l in range(2)]
    b_wsc = [Buf("wsc0"), Buf("wsc1")]
    csT_h = nc.dram_tensor("csTd", [64, T], F32)
    csT_d = csT_h.ap()
    b_csTd = Buf("csTd")
    b_ygT, b_x1T, b_gateT, b_outT = Buf("ygT"), Buf("x1T"), Buf("gateT"), Buf("outT")

    with ExitStack() as top:
        _cnt = [0]

        def sb(name, shape, dt=F32, es=top):
            _cnt[0] += 1
            return es.enter_context(nc.sbuf_tensor("sb%d_%s" % (_cnt[0], name), shape, dt))

        def mm(out, lhsT, rhs, start, stop, rd, wr):
            S.op("pe", lambda h: h.matmul(out, lhsT=lhsT, rhs=rhs, start=start, stop=stop), rd, wr)

        def tr(out, in_, idn, rd, wr):
            S.op("pe", lambda h: h.transpose(out=out, in_=in_, identity=idn), rd, wr)

        def act(out, in_, func, rd, wr, bias=None, scale=None, eng="act"):
            kw = {}
            if bias is not None:
                kw["bias"] = bias
            if scale is not None:
                kw["scale"] = scale
            S.op(eng, lambda h: h.activation(out=out, in_=in_, func=func, **kw), rd, wr)

        def amul(out, in_, c, rd, wr):
            S.op("act", lambda h: h.mul(out, in_, c), rd, wr)

        def acopy(out, in_, rd, wr):
            S.op("act", lambda h: h.copy(out, in_), rd, wr)

        def tt(out, in0, in1, op, rd, wr, eng="dve"):
            S.op(eng, lambda h: h.tensor_tensor(out=out, in0=in0, in1=in1, op=op), rd, wr)

        def ts(out, in0, s1, s2, op0, op1, rd, wr, accum=None, eng="dve"):
            kw = {}
            if op1 is not None:
                kw["op1"] = op1
            if accum is not None:
                kw["accum_out"] = accum
            S.op(eng, lambda h: h.tensor_scalar(out=out, in0=in0, scalar1=s1, scalar2=s2, op0=op0, **kw), rd, wr)

        def stt(out, in0, scalar, in1, op0, op1, rd, wr, eng="dve"):
            S.op(eng, lambda h: h.scalar_tensor_tensor(out=out, in0=in0, scalar=scalar, in1=in1, op0=op0, op1=op1), rd, wr)

        def vcopy(out, in_, rd, wr, eng="dve"):
            S.op(eng, lambda h: h.tensor_copy(out, in_), rd, wr)

        def recip(out, in_, rd, wr):
            S.op("dve", lambda h: h.reciprocal(out, in_), rd, wr)

        def memset(ap, val, wr, eng="pool"):
            S.op(eng, lambda h: h.memset(ap, val), (), wr)

        def dma(eng, out, in_, rd, wr):
            return S.dma(eng, lambda h: h.dma_start(out=out, in_=in_), rd, wr)

        psb = [top.enter_context(nc.psum_tensor("ps%d" % i, [128, 512], F32)) for i in range(8)]
        psbuf = [Buf("ps%d" % i) for i in range(8)]

        b_c = Buf("consts")
        identf = sb("identf", [128, 128])
        ident = sb("ident", [128, 128], BF16)
        U128 = sb("U128", [128, 128])
        onesf = sb("onesf", [128, 128])
        onesb = sb("onesb", [128, 128], BF16)
        memset(identf[:], 0.0, [b_c])
        S.op("pool", lambda h: h.affine_select(out=identf[:], in_=identf[:], pattern=[[-1, 128]], compare_op=ALU.not_equal,
                                                fill=1.0, base=0, channel_multiplier=1), [b_c], [b_c])
        memset(onesf[:], 1.0, [b_c])
        memset(onesb[:], 1.0, [b_c])
        memset(U128[:], 1.0, [b_c])
        S.op("pool", lambda h: h.affine_select(out=U128[:], in_=U128[:], pattern=[[1, 128]], compare_op=ALU.is_ge,
                                                fill=0.0, base=0, channel_multiplier=-1), [b_c], [b_c])
        vcopy(ident[:], identf[:], [b_c], [b_c], eng="pool")

        b_par = Buf("params")
        convw = sb("convw", [128, 48 * 4])
        convb = sb("convb", [128, 48])
        dtb = sb("dtb", [128, 64])
        alog = sb("alog", [128, 64])
        dskip = sb("dskip", [128, 32])
        normg = sb("normg", [128, 32])
        lng = sb("lng", [128, 32])
        lnb = sb("lnb", [128, 32])
        qng = sb("qng", [128, 4])
        kvng = sb("kvng", [128, 2])
        for t_, d_ in ((convw, convw_d), (convb, convb_d), (dtb, dtb_d), (alog, alog_d), (dskip, dskip_d),
                       (normg, normg_d), (lng, lng_d), (lnb, lnb_d), (qng, qng_d), (kvng, kvng_d)):
            dma("sp", t_[:], d_[:, :], [], [b_par])

        NW = 2
        wring = Ring([sb("wt%d" % i, [128, 16, 256], BF16) for i in range(NW)])

        xstk = ExitStack()
        xTb = sb("xTb", [128, 16, T], BF16, xstk)
        b_xTb = Buf("xTb")
        for c in range(16):
            dma("pool", xTb[:, c, :], xT_d[c * 128:(c + 1) * 128, :], [], [b_xTb])

        def load_w(src, kc, ncols):
            wt, wb = wring.next()
            dma("pool", wt[:, 0:kc, 0:ncols], src.rearrange("(c p) n -> p c n", p=128), [], [wb])
            return wt, wb

        precast_jobs = []
        for l_, wsrc in ((0, w_out0), (1, w_out1)):
            for dp in range(8):
                for kh in range(2):
                    precast_jobs.append((l_, wsrc, dp, kh))
        precast_pos = [0]

        def precast_step(layer_limit):
            if precast_pos[0] >= len(precast_jobs):
                return
            l_, wsrc, dp, kh = precast_jobs[precast_pos[0]]
            if l_ > layer_limit:
                return
            precast_pos[0] += 1
            ti = dp * 2 + kh
            dst = wsc_h[l_].ap()[ti * 128:(ti + 1) * 128, :].rearrange("p (c n) -> p c n", c=16)
            src = wsrc[kh * 2048:(kh + 1) * 2048, dp * 256:(dp + 1) * 256].rearrange("(c p) n -> p c n", p=128)
            dma("pool", dst, src, [], [b_wsc[l_]])

        def load_wsc(l_, dp, kh):
            wt, wb = wring.next()
            ti = dp * 2 + kh
            dma("pool", wt[:, :, :], wsc_h[l_].ap()[ti * 128:(ti + 1) * 128, :].rearrange("p (c n) -> p c n", c=16),
                [b_wsc[l_]], [wb])
            return wt, wb

        def outproj_ln(es, actT_d, b_act, w_d, res_d, b_res, layer, out_d, b_out, is_final, write_bf):
            nbuf = 2 if layer == 1 else 1
            Ar = Ring([sb("opA%d" % i, [128, 32, 512], BF16, es) for i in range(nbuf)])
            xrr = Ring([sb("opX%d" % i, [128, 16, 512], F32, es) for i in range(nbuf)])

            def preload_A(R):
                rs_ = slice(R * 512, (R + 1) * 512)
                A_, bA_ = Ar.next()
                for hh in range(2):
                    dma("sp", A_[:, hh * 16:(hh + 1) * 16, :],
                        actT_d[hh * 2048:(hh + 1) * 2048, rs_].rearrange("(c p) t -> p c t", p=128), [b_act], [bA_])
                return A_, bA_

            def preload_x(R):
                rs_ = slice(R * 512, (R + 1) * 512)
                x_, bx_ = xrr.next()
                dma("sp", x_[:, :, :], res_d[:, rs_].rearrange("(c p) t -> p c t", p=128), [b_res], [bx_])
                return x_, bx_

            def preload(R):
                return preload_A(R) + preload_x(R)
            sqr = Ring([sb("opsq%d" % i, [128, 512], F32, es) for i in range(2)])
            o32r = Ring([sb("opo%d" % i, [128, 512], F32, es) for i in range(2)])
            xnr = Ring([sb("opxn%d" % i, [128, 512], F32, es) for i in range(2)])
            mean = sb("opmean", [128, 512], F32, es)
            msq = sb("opmsq", [128, 512], F32, es)
            var = sb("opvar", [128, 512], F32, es)
            rstd = sb("oprstd", [128, 512], F32, es)
            nb = sb("opnb", [128, 512], F32, es)
            b_st = Buf()
            for _ in range(32):
                precast_step(layer)
            pring = Ring([psb[0], psb[1]])
            pring.bufs = [psbuf[0], psbuf[1]]
            P1, P2 = psb[2], psb[3]
            nxt_ld = preload(0)
            for R in range(4):
                rs = slice(R * 512, (R + 1) * 512)
                A, b_A, xr, b_xr = nxt_ld
                if nbuf == 2 and R < 3:
                    nxt_ld = preload(R + 1)
                for dp in range(8):
                    w0, wb0 = load_wsc(layer, dp, 0)
                    w1, wb1 = load_wsc(layer, dp, 1)
                    for dq in range(2):
                        dch = dp * 2 + dq
                        ps, pb = pring.next()
                        for kk in range(32):
                            wt, wb = (w0, wb0) if kk < 16 else (w1, wb1)
                            mm(ps[:, :], wt[:, kk % 16, dq * 128:(dq + 1) * 128], A[:, kk, :], kk == 0, kk == 31,
                               [wb, b_A], [pb])
                        stt(xr[:, dch, :], xr[:, dch, :], ALPHA, ps[:, :], ALU.mult, ALU.add, [b_xr, pb], [b_xr])
                        sq, bsq = sqr.next()
                        act(sq[:], xr[:, dch, :], AF.Square, [b_xr], [bsq])
                        mm(P1[:, :], onesf[:], xr[:, dch, :], dch == 0, dch == 15, [b_c, b_xr], [psbuf[2]])
                        mm(P2[:, :], onesf[:], sq[:], dch == 0, dch == 15, [b_c, bsq], [psbuf[3]])
                nA = preload_A(R + 1) if (nbuf == 1 and R < 3) else None
                amul(mean[:], P1[:, :], 1.0 / D, [psbuf[2]], [b_st])
                tt(msq[:], mean[:], mean[:], ALU.mult, [b_st], [b_st])
                stt(var[:], P2[:, :], 1.0 / D, msq[:], ALU.mult, ALU.subtract, [psbuf[3], b_st], [b_st])
                ts(var[:], var[:], LN_EPS, None, ALU.add, None, [b_st], [b_st])
                recip(var[:], var[:], [b_st], [b_st])
                act(rstd[:], var[:], AF.Sqrt, [b_st], [b_st])
                stt(nb[:], mean[:], -1.0, rstd[:], ALU.mult, ALU.mult, [b_st], [b_st])
                for dch in range(16):
                    xn, bxn = xnr.next()
                    tt(xn[:], xr[:, dch, :], rstd[:], ALU.mult, [b_xr, b_st], [bxn])
                    tt(xn[:], xn[:], nb[:], ALU.add, [bxn, b_st], [bxn])
                    o32, bo = o32r.next()
                    col = layer * 16 + dch
                    act(o32[:], xn[:], AF.Identity, [bxn, b_par], [bo], bias=lnb[:, col:col + 1], scale=lng[:, col:col + 1])
                    if write_bf:
                        act(xTb[:, dch, rs], xn[:], AF.Identity, [bxn, b_par], [b_xTb], bias=lnb[:, col:col + 1], scale=lng[:, col:col + 1])
                    o = dma("sp", out_d[dch * 128:(dch + 1) * 128, rs], o32[:], [bo], [b_out])
                    if is_final:
                        S.out_ops.append(o)
                if nbuf == 1 and R < 3:
                    nxt_ld = nA + preload_x(R + 1)

        with ExitStack() as l0:
            def sb0(name, shape, dt=F32):
                return sb(name, shape, dt, l0)

            dt_tok = sb0("dt_tok", [128, 16, 64])
            negcs = sb0("negcs", [128, 16, 64])
            dtw = sb0("dtw", [128, 16, 64])
            etot = sb0("etot", [128, 8, 64])
            b_dt = Buf("dt")
            with ExitStack() as l0a:
                wdt = sb("wdt", [128, 16, 64], BF16, l0a)
                adt = sb("adt", [128, 16, 64], F32, l0a)
                cstr = Ring([sb("cst%d" % i, [64, 256], F32, l0a) for i in range(2)])
                b_wdt = Buf()
                tmpA = sb("tmpA", [128, 16, 64], F32, l0a)
                ea = sb("ea", [128, 64], F32, l0a)
                b_tmp = Buf()
                dma("pool", wdt[:, :, :], w_in0[:, 10240:10304].rearrange("(c p) n -> p c n", p=128), [], [b_wdt])
                for j in range(16):
                    bank = psb[j // 8]
                    jj = j % 8
                    for k in range(16):
                        mm(bank[:, jj * 64:(jj + 1) * 64], xTb[:, k, j * 128:(j + 1) * 128], wdt[:, k, :], k == 0, k == 15,
                           [b_xTb, b_wdt], [psbuf[j // 8]])
                for hf in range(2):
                    tt(tmpA[:, hf * 8:(hf + 1) * 8, :], psb[hf][:, :].rearrange("p (a b) -> p a b", a=8),
                       dtb[:].unsqueeze(1).to_broadcast([128, 8, 64]), ALU.add, [psbuf[hf], b_par], [b_tmp])
                act(tmpA[:], tmpA[:], AF.Exp, [b_tmp], [b_tmp])
                act(dt_tok[:], tmpA[:], AF.Ln, [b_tmp], [b_dt], bias=1.0)
                act(ea[:], alog[:], AF.Exp, [b_par], [b_tmp])
                stt(adt[:], dt_tok[:], -1.0, ea[:].unsqueeze(1).to_broadcast([128, 16, 64]), ALU.mult, ALU.mult,
                    [b_dt, b_tmp], [b_dt])
                for b in range(16):
                    bank = psb[2 + b // 8]
                    bb = b % 8
                    o_ = bank[:, bb * 64:(bb + 1) * 64]
                    if b % 2 == 0:
                        mm(o_, U128[:], adt[:, b, :], True, True, [b_c, b_dt], [psbuf[2 + b // 8]])
                    else:
                        mm(o_, onesf[:], adt[:, b - 1, :], True, False, [b_c, b_dt], [psbuf[2 + b // 8]])
                        mm(o_, U128[:], adt[:, b, :], False, True, [b_c, b_dt], [psbuf[2 + b // 8]])
                for c in range(8):
                    o_ = psb[4][:, c * 64:(c + 1) * 64]
                    mm(o_, onesf[:], adt[:, 2 * c, :], True, False, [b_c, b_dt], [psbuf[4]])
                    mm(o_, onesf[:], adt[:, 2 * c + 1, :], False, True, [b_c, b_dt], [psbuf[4]])
                for hf in range(2):
                    amul(negcs[:, hf * 8:(hf + 1) * 8, :], psb[2 + hf][:, :].rearrange("p (a b) -> p a b", a=8), -1.0,
                         [psbuf[2 + hf]], [b_dt])
                tt(tmpA[:].rearrange("p (c e) h -> p c e h", e=2), negcs[:].rearrange("p (c e) h -> p c e h", e=2),
                   psb[4][:, :].rearrange("p (c h) -> p c h", c=8).unsqueeze(2).to_broadcast([128, 8, 2, 64]),
                   ALU.add, [b_dt, psbuf[4]], [b_tmp])
                act(tmpA[:], tmpA[:], AF.Exp, [b_tmp], [b_tmp])
                tt(dtw[:], dt_tok[:], tmpA[:], ALU.mult, [b_dt, b_tmp], [b_dt])
                act(etot[:], psb[4][:, :].rearrange("p (c h) -> p c h", c=8), AF.Exp, [psbuf[4]], [b_dt])
                for c in range(8):
                    bk = 5 + c % 2
                    ps = psb[bk]
                    mm(ps[0:64, 0:128], adt[:, 2 * c, :], U128[:], True, True, [b_dt, b_c], [psbuf[bk]])
                    mm(ps[0:64, 128:256], adt[:, 2 * c, :], onesf[:], True, False, [b_dt, b_c], [psbuf[bk]])
                    mm(ps[0:64, 128:256], adt[:, 2 * c + 1, :], U128[:], False, True, [b_dt, b_c], [psbuf[bk]])
                    cst, bcst = cstr.next()
                    amul(cst[:], ps[0:64, 0:256], -1.0, [psbuf[bk]], [bcst])
                    dma("sp", csT_d[:, c * 256:(c + 1) * 256], cst[:], [bcst], [b_csTd])

            S.fence()
            with ExitStack() as l0b:
                def sbb(name, shape, dt=F32):
                    return sb(name, shape, dt, l0b)

                zs = sbb("zs", [128, 4, T], BF16)
                xsT = sbb("xsT", [128, 4, T], BF16)
                BT = sbb("BT", [128, T], BF16)
                CT = sbb("CT", [128, T], BF16)
                b_zs, b_xs, b_BT, b_CT = Buf(), Buf(), Buf(), Buf()
                stg = [sbb("stg%d" % i, [128, 515]) for i in range(2)]
                b_stg = [Buf(), Buf()]
                accr = Ring([sbb("cacc%d" % i, [128, 512]) for i in range(1)])
                xdt_pad = [sbb("xdtp%d" % i, [128, 2, 8, 128], BF16) for i in range(2)]
                b_xdt = [Buf(), Buf()]
                xdtw = [sbb("xdtw%d" % i, [128, 2, 512], BF16) for i in range(2)]
                b_xdtw = [Buf(), Buf()]
                Btok = [sbb("Btok%d" % i, [128, 256], BF16) for i in range(2)]
                b_Btok = [Buf(), Buf()]
                cbT = [sbb("cbT%d" % i, [128, 384]) for i in range(2)]
                b_cbT = [Buf(), Buf()]
                csbr = Ring([sbb("csb%d" % i, [128, 8, 256]) for i in range(2)])
                x3 = sbb("x3", [128, 8, 3, 128])
                b_x3 = Buf()
                ecs4 = sbb("ecs4", [128, 4, 256], BF16)
                b_ecs4 = Buf()
                diagD = sbb("diagD", [128, 8, 128], BF16)
                b_diagD = Buf()
                dhi_b = sbb("dhi_b", [128, 32], BF16)
                dhi = sbb("dhi", [128, 32])
                dlo = sbb("dlo", [128, 32])
                b_dsp = Buf()
                vcopy(dhi_b[:], dskip[:], [b_par], [b_dsp], eng="pool")
                vcopy(dhi[:], dhi_b[:], [b_dsp], [b_dsp], eng="pool")
                tt(dlo[:], dskip[:], dhi[:], ALU.subtract, [b_par, b_dsp], [b_dsp], eng="pool")
                GTr = Ring([sbb("GT%d" % i, [128, 384], BF16) for i in range(4)])
                yoffr = Ring([sbb("yoff%d" % i, [128, 256]) for i in range(1)])
                ysb = sbb("ysb", [128, 4, 256])
                b_y = [Buf() for _ in range(4)]
                sqr0 = Ring([sbb("sq0_%d" % i, [128, 256], BF16) for i in range(2)])
                rs0 = sbb("rs0", [128, 256])
                b_rs0 = Buf()
                ygr = Ring([sbb("yg%d" % i, [128, 4, 256], BF16) for i in range(1)])
                S32 = sbb("S32", [128, 512])
                Sbf = sbb("Sbf", [128, 512], BF16)
                b_S, b_Sbf = Buf(), Buf()
                for i in range(2):
                    memset(xdt_pad[i][:], 0.0, [b_xdt[i]])

                pring = Ring([psb[0], psb[1]])
                pring.bufs = [psbuf[0], psbuf[1]]
                pzr = Ring([psb[3][:, 0:256], psb[6][:, 0:256]])
                pzr.bufs = [psbuf[3], psbuf[6]]
                pyr = Ring([psb[4][:, 0:256], psb[5][:, 0:256]])
                pyr.bufs = [psbuf[4], psbuf[5]]
                P_ssq = psb[2]
                b_Pssq = psbuf[2]

                def conv_evac(cc, R, ps, pb, dest, b_dest):
                    st, bst = stg[R % 2], b_stg[R % 2]
                    if R == 0:
                        memset(st[:, 0:3], 0.0, [bst], eng="dve")
                    else:
                        acopy(st[:, 0:3], stg[(R - 1) % 2][:, 512:515], [b_stg[(R - 1) % 2]], [bst])
                    acopy(st[:, 3:515], ps, [pb], [bst])
                    acc, bacc = accr.next()
                    ts(acc[:], st[:, 0:512], convw[:, cc * 4:cc * 4 + 1], None, ALU.mult, None, [bst, b_par], [bacc])
                    for k in range(1, 4):
                        stt(acc[:], st[:, k:k + 512], convw[:, cc * 4 + k:cc * 4 + k + 1], acc[:], ALU.mult, ALU.add,
                            [bst, b_par, bacc], [bacc])
                    act(dest, acc[:], AF.Silu, [bacc, b_par], [b_dest], bias=convb[:, cc:cc + 1])

                def inproj_cols(col0, ncols, evac):
                    wt, wb = load_w(w_in0[:, col0:col0 + ncols], 16, ncols)
                    if g >= 1 and ncols == 256:
                        precast_step(0)
                    for qq in range(ncols // 128):
                        for R in range(4):
                            ps, pb = pring.next()
                            for k in range(16):
                                mm(ps[:, :], wt[:, k, qq * 128:(qq + 1) * 128], xTb[:, k, R * 512:(R + 1) * 512],
                                   k == 0, k == 15, [wb, b_xTb], [pb])
                            evac(qq, R, ps[:, :], pb)

                csb_next = None

                def load_csb(g_, c_):
                    t_, b_ = csbr.next()
                    src = bass.AP(csT_h, 8 * g_ * T + c_ * 256, [[0, 128], [T, 8], [1, 256]])
                    dma("sp", t_[:, :, :], src, [b_csTd], [b_])
                    return t_, b_

                for g in range(8):
                    for pr in range(4):
                        for hl, dsrc in ((0, dhi), (1, dlo)):
                            S.op("dve", lambda h, pr=pr, hl=hl, dsrc=dsrc, g=g: h.tensor_scalar(
                                out=diagD[:, 2 * pr + hl, :], in0=identf[:], scalar1=dsrc[:, 4 * g + pr:4 * g + pr + 1],
                                scalar2=None, op0=ALU.mult), [b_c, b_dsp], [b_diagD])
                    for half in range(2):
                        def ev_z(qq, R, ps, pb, half=half):
                            act(zs[:, half * 2 + qq, R * 512:(R + 1) * 512], ps, AF.Silu, [pb], [b_zs])
                        inproj_cols(g * 512 + half * 256, 256, ev_z)
                    for half in range(2):
                        def ev_x(qq, R, ps, pb, half=half):
                            q4 = half * 2 + qq
                            conv_evac(4 * g + q4, R, ps, pb, xsT[:, q4, R * 512:(R + 1) * 512], b_xs)
                        inproj_cols(4096 + g * 512 + half * 256, 256, ev_x)
                    inproj_cols(8192 + g * 128, 128,
                                lambda qq, R, ps, pb: conv_evac(32 + g, R, ps, pb, BT[:, R * 512:(R + 1) * 512], b_BT))
                    inproj_cols(9216 + g * 128, 128,
                                lambda qq, R, ps, pb: conv_evac(40 + g, R, ps, pb, CT[:, R * 512:(R + 1) * 512], b_CT))
                    hs = slice(8 * g, 8 * g + 8)

                    def prologue_pe(c):
                        nonlocal csb_next
                        t0 = c * 256
                        if g == 0 and c == 0:
                            csb_next = load_csb(0, 0)
                        csb, bcsb = csb_next
                        if c < 7:
                            csb_next = load_csb(g, c + 1)
                        elif g < 7:
                            csb_next = load_csb(g + 1, 0)
                        pxs_, b_Pxs = pring.next()
                        P_xstok = pxs_[:, :].bitcast(BF16)
                        pbt_, b_Pbt = pring.next()
                        P_btok = pbt_[:, 0:128].bitcast(BF16)
                        for j in range(2):
                            for q in range(4):
                                tr(P_xstok[:, j * 512 + q * 128: j * 512 + (q + 1) * 128],
                                   xsT[:, q, t0 + j * 128: t0 + (j + 1) * 128], ident[:], [b_xs, b_c], [b_Pxs])
                            tr(P_btok[:, j * 128:(j + 1) * 128], BT[:, t0 + j * 128: t0 + (j + 1) * 128], ident[:],
                               [b_BT, b_c], [b_Pbt])
                        return dict(csb=csb, bcsb=bcsb, P_xstok=P_xstok, b_Pxs=b_Pxs, P_btok=P_btok, b_Pbt=b_Pbt)

                    def prologue_rest1(c, h):
                        t0 = c * 256
                        pp = c % 2
                        P_xstok, b_Pxs, P_btok, b_Pbt = h["P_xstok"], h["b_Pxs"], h["P_btok"], h["b_Pbt"]
                        for j in range(2):
                            blk = P_xstok[:, j * 512:(j + 1) * 512]
                            xs4 = blk.rearrange("p (a e d) -> p a e d", a=4, e=2)
                            for par in range(2):
                                tt(xdt_pad[pp][:, j, par::2, par * 64:(par + 1) * 64], xs4[:, :, par, :],
                                   dt_tok[:, 2 * c + j, 8 * g + par:8 * g + 8:2].unsqueeze(2).to_broadcast([128, 4, 64]),
                                   ALU.mult, [b_Pxs, b_dt], [b_xdt[pp]])
                            tt(xdtw[pp][:, j, :].rearrange("p (a d) -> p a d", a=8), blk.rearrange("p (a d) -> p a d", a=8),
                               dtw[:, 2 * c + j, hs].unsqueeze(2).to_broadcast([128, 8, 64]), ALU.mult,
                               [b_Pxs, b_dt], [b_xdtw[pp]])
                        acopy(Btok[pp][:], P_btok, [b_Pbt], [b_Btok[pp]])
                        pcb_, b_Pcb = pring.next()
                        P_cbT = pcb_[:, 0:384]
                        mm(P_cbT[:, 0:256], BT[:, t0:t0 + 128], CT[:, t0:t0 + 256], True, True, [b_BT, b_CT], [b_Pcb])
                        mm(P_cbT[:, 256:384], BT[:, t0 + 128:t0 + 256], CT[:, t0 + 128:t0 + 256], True, True,
                           [b_BT, b_CT], [b_Pcb])
                        tt(cbT[pp][:].rearrange("p (a b) -> p a b", a=3)[:, 0::2, :], P_cbT.rearrange("p (a b) -> p a b", a=3)[:, 0::2, :],
                           U128[:].unsqueeze(1).to_broadcast([128, 2, 128]), ALU.mult, [b_Pcb, b_c], [b_cbT[pp]])
                        acopy(cbT[pp][:, 128:256], P_cbT[:, 128:256], [b_Pcb], [b_cbT[pp]])
                        h["pst"] = h["pbst"] = None
                        if c < 7:
                            pst, pbst = psb[7], psbuf[7]
                            mm(pst[:, :], Btok[pp][:, 0:128], xdtw[pp][:, 0, :], True, False, [b_Btok[pp], b_xdtw[pp]], [pbst])
                            mm(pst[:, :], Btok[pp][:, 128:256], xdtw[pp][:, 1, :], False, True, [b_Btok[pp], b_xdtw[pp]], [pbst])
                            h["pst"], h["pbst"] = pst, pbst

                    def prologue_rest2(c, h):
                        csb, bcsb = h["csb"], h["bcsb"]
                        if c > 0:
                            for par in range(2):
                                act(ecs4[par * 64:(par + 1) * 64, :, :], csb[par * 64:(par + 1) * 64, par::2, :], AF.Exp,
                                    [bcsb], [b_ecs4], scale=-1.0)
                        nb0 = negcs[:, 2 * c, hs].unsqueeze(2).to_broadcast([128, 8, 128])
                        nb1 = negcs[:, 2 * c + 1, hs].unsqueeze(2).to_broadcast([128, 8, 128])
                        tt(x3[:, :, 0, :], csb[:, :, 0:128], nb0, ALU.max, [bcsb, b_dt], [b_x3])
                        tt(x3[:, :, 2, :], csb[:, :, 128:256], nb1, ALU.max, [bcsb, b_dt], [b_x3])
                        tt(x3[:, :, 1, :], csb[:, :, 128:256], nb0, ALU.subtract, [bcsb, b_dt], [b_x3])
                        tt(x3[:, :, 0, :], x3[:, :, 0, :], nb0, ALU.subtract, [b_x3, b_dt], [b_x3])
                        tt(x3[:, :, 2, :], x3[:, :, 2, :], nb1, ALU.subtract, [b_x3, b_dt], [b_x3])
                        act(x3[:], x3[:], AF.Exp, [b_x3], [b_x3], scale=-1.0)

                    def pairs(c, hook):
                        t0 = c * 256
                        pp = c % 2

                        def pair_heads(pr):
                            yps, byp = pyr.next()
                            pz = None
                            if c > 0:
                                pz = pzr.next()
                                mm(pz[0], Sbf[:, pr * 128:(pr + 1) * 128], CT[:, t0:t0 + 256], True, True, [b_Sbf, b_CT], [pz[1]])
                            for par in range(2):
                                r = 2 * pr + par
                                hh = 8 * g + r
                                GT, bGT = GTr.next()
                                tt(GT[:], x3[:, r, :, :].rearrange("p a b -> p (a b)"), cbT[pp][:], ALU.mult, [b_x3, b_cbT[pp]], [bGT])
                                mm(yps[:, 0:256], xdt_pad[pp][:, 0, r, :], GT[:, 0:256], par == 0, False,
                                   [b_xdt[pp], bGT], [byp])
                                mm(yps[:, 128:256], xdt_pad[pp][:, 1, r, :], GT[:, 256:384], False, False,
                                   [b_xdt[pp], bGT], [byp])
                            mm(yps[:, 0:256], diagD[:, 2 * pr, :], xsT[:, pr, t0:t0 + 256], False, False, [b_diagD, b_xs], [byp])
                            mm(yps[:, 0:256], diagD[:, 2 * pr + 1, :], xsT[:, pr, t0:t0 + 256], False, True, [b_diagD, b_xs], [byp])
                            return yps, byp, pz

                        def pair_tail(pr, yps, byp, pz):
                            yv = ysb[:, pr, :]
                            if c > 0:
                                yoff, byo = yoffr.next()
                                tt(yoff[:], pz[0], ecs4[:, pr, :], ALU.mult, [pz[1], b_ecs4], [byo])
                                tt(yv, yps, yoff[:], ALU.add, [byp, byo], [b_y[pr]])
                            else:
                                acopy(yv, yps, [byp], [b_y[pr]])
                            tt(yv, yv, zs[:, pr, t0:t0 + 256], ALU.mult, [b_y[pr], b_zs], [b_y[pr]], eng="pool")
                            sq, bsq = sqr0.next()
                            tt(sq[:], yv, yv, ALU.mult, [b_y[pr]], [bsq], eng="pool")
                            mm(P_ssq[:, 0:256], onesb[:], sq[:], pr == 0, pr == 3, [b_c, bsq], [b_Pssq])

                        cur = pair_heads(0)
                        res = None
                        for pr in range(4):
                            nxtp = pair_heads(pr + 1) if pr < 3 else None
                            if pr == 3:
                                res = hook()
                            pair_tail(pr, *cur)
                            cur = nxtp
                        return res

                    def state_update(c, pst, pbst):
                        if c < 7:
                            if c == 0:
                                vcopy(S32[:], pst[:, :], [pbst], [b_S])
                            else:
                                tt(S32[:].rearrange("p (a d) -> p a d", a=8), S32[:].rearrange("p (a d) -> p a d", a=8),
                                   etot[:, c, hs].unsqueeze(2).to_broadcast([128, 8, 64]), ALU.mult, [b_S, b_dt], [b_S])
                                tt(S32[:], S32[:], pst[:, :], ALU.add, [b_S, pbst], [b_S])
                            acopy(Sbf[:], S32[:], [b_S], [b_Sbf])

                    def epilogue_a(c):
                        ts(rs0[:], P_ssq[:, 0:256], 1.0 / 512, RMS_EPS, ALU.mult, ALU.add, [b_Pssq], [b_rs0])
                        act(rs0[:], rs0[:], AF.Ln, [b_rs0], [b_rs0])
                        act(rs0[:], rs0[:], AF.Exp, [b_rs0], [b_rs0], scale=-0.5)

                    def epilogue_b(c):
                        t0 = c * 256
                        yg, byg = ygr.next()
                        for pr in range(4):
                            stt(yg[:, pr, :], ysb[:, pr, :], normg[:, 4 * g + pr:4 * g + pr + 1], rs0[:], ALU.mult, ALU.mult,
                                [b_y[pr], b_par, b_rs0], [byg])
                        dma("sp", ygT_d[g * 512:(g + 1) * 512, t0:t0 + 256].rearrange("(q p) t -> p q t", p=128), yg[:],
                            [byg], [b_ygT])

                    pro = prologue_pe(0)
                    prologue_rest1(0, pro)
                    prologue_rest2(0, pro)
                    for c in range(8):
                        nxt_pro = pairs(c, (lambda c=c: prologue_pe(c + 1)) if c < 7 else (lambda: None))
                        state_update(c, pro["pst"], pro["pbst"])
                        if c < 7:
                            prologue_rest1(c + 1, nxt_pro)
                        epilogue_a(c)
                        if c < 7:
                            prologue_rest2(c + 1, nxt_pro)
                        epilogue_b(c)
                        pro = nxt_pro

            S.fence()
            with ExitStack() as l0c:
                if upto == "l0":
                    outproj_ln(l0c, ygT_d, b_ygT, w_out0, xT_d, Buf(), 0, outT_d, b_outT, True, False)
                else:
                    outproj_ln(l0c, ygT_d, b_ygT, w_out0, xT_d, Buf(), 0, x1T_d, b_x1T, False, True)


        if upto != "l0":
            cqnT_d = nc.dram_tensor("cqnT", [512, T], BF16).ap()
            ckvT_d = nc.dram_tensor("ckvT", [256, T], BF16).ap()
            kdup_d = nc.dram_tensor("kdup", [128, T], BF16).ap()
            widx_d = nc.dram_tensor("widx", [128, 256], F32).ap()
            b_cqnT_d, b_ckvT_d, b_kdup_d, b_widx_d = Buf(), Buf(), Buf(), Buf()
            S.fence()
            with ExitStack() as l1a:
                def sba(name, shape, dt=F32):
                    return sb(name, shape, dt, l1a)
                cq32 = sba("cq32", [128, 4, T])
                ckv32 = sba("ckv32", [128, 2, T])
                b_cq32, b_ckv32 = Buf(), Buf()
                kst = sba("kst", [128, T], BF16)
                b_kst = Buf()
                wst = sba("wst", [128, 256])
                b_wst = Buf()
                gstr = Ring([sba("gst%d" % i, [128, 512], BF16) for i in range(3)])
                sq1r = Ring([sba("sq1_%d" % i, [128, 512]) for i in range(2)])
                rs1 = sba("rs1", [128, 512])
                b_rs1 = Buf()
                nrmr = Ring([sba("nrm%d" % i, [128, 512], BF16) for i in range(2)])
                pring = Ring([psb[0], psb[1], psb[2]])
                pring.bufs = [psbuf[0], psbuf[1], psbuf[2]]

                def inproj1(wt, wb, nsub, evac):
                    for qq in range(nsub):
                        for R in range(4):
                            ps, pb = pring.next()
                            for k in range(16):
                                mm(ps[:, :], wt[:, k, qq * 128:(qq + 1) * 128], xTb[:, k, R * 512:(R + 1) * 512],
                                   k == 0, k == 15, [wb, b_xTb], [pb])
                            evac(qq, R, ps[:, :], pb)

                for half in range(2):
                    wt, wb = load_w(w_in1[:, half * 256:(half + 1) * 256], 16, 256)
                    inproj1(wt, wb, 2, lambda qq, R, ps, pb, half=half:
                            acopy(cq32[:, half * 2 + qq, R * 512:(R + 1) * 512], ps, [pb], [b_cq32]))
                wt, wb = load_w(w_in1[:, 512:768], 16, 256)
                inproj1(wt, wb, 2, lambda qq, R, ps, pb: acopy(ckv32[:, qq, R * 512:(R + 1) * 512], ps, [pb], [b_ckv32]))
                wt, wb = wring.next()
                ksrc = w_in1[:, 768:832].rearrange("(c p) n -> p c n", p=128)
                S.dma("pool", lambda h, wt=wt: [h.dma_start(out=wt[:, 0:16, 0:64], in_=ksrc),
                                                h.dma_start(out=wt[:, 0:16, 64:128], in_=ksrc)], [], [wb], n=2)
                inproj1(wt, wb, 1, lambda qq, R, ps, pb: acopy(kst[:, R * 512:(R + 1) * 512], ps, [pb], [b_kst]))
                dma("sp", kdup_d[:, :], kst[:], [b_kst], [b_kdup_d])
                wt, wb = load_w(w_in1[:, 832:848], 16, 16)
                ps, pb = pring.next()
                for i in range(16):
                    for k in range(16):
                        mm(ps[:, i * 16:(i + 1) * 16], xTb[:, k, i * 128:(i + 1) * 128], wt[:, k, 0:16], k == 0, k == 15,
                           [wb, b_xTb], [pb])
                amul(wst[:], ps[:, 0:256], 0.25 * 0.125, [pb], [b_wst])
                dma("sp", widx_d[:, :], wst[:], [b_wst], [b_widx_d])
                for gt in range(16):
                    wt, wb = load_w(w_in1[:, 848 + gt * 256: 848 + (gt + 1) * 256], 16, 256)
                    precast_step(1)

                    def ev_g(qq, R, ps, pb, gt=gt):
                        g_, bg_ = gstr.next()
                        act(g_[:], ps, AF.Silu, [pb], [bg_])
                        fc = gt * 2 + qq
                        dma("sp", gateT_d[fc * 128:(fc + 1) * 128, R * 512:(R + 1) * 512], g_[:], [bg_], [b_gateT])
                    inproj1(wt, wb, 2, ev_g)
                for (src, bsrc, nch, gtile, dst_d, bdst) in ((cq32, b_cq32, 4, qng, cqnT_d, b_cqnT_d),
                                                            (ckv32, b_ckv32, 2, kvng, ckvT_d, b_ckvT_d)):
                    for R in range(4):
                        rs = slice(R * 512, (R + 1) * 512)
                        ps, pb = pring.next()
                        for q in range(nch):
                            sq, bsq = sq1r.next()
                            act(sq[:], src[:, q, rs], AF.Square, [bsrc], [bsq])
                            mm(ps[:, :], onesf[:], sq[:], q == 0, q == nch - 1, [b_c, bsq], [pb])
                        ts(rs1[:], ps[:, :], 1.0 / (128 * nch), RMS_EPS, ALU.mult, ALU.add, [pb], [b_rs1])
                        recip(rs1[:], rs1[:], [b_rs1], [b_rs1])
                        act(rs1[:], rs1[:], AF.Sqrt, [b_rs1], [b_rs1])
                        for q in range(nch):
                            nr, bnr = nrmr.next()
                            stt(nr[:], src[:, q, rs], gtile[:, q:q + 1], rs1[:], ALU.mult, ALU.mult, [bsrc, b_par, b_rs1], [bnr])
                            dma("sp", dst_d[q * 128:(q + 1) * 128, rs], nr[:], [bnr], [bdst])
            xstk.close()

            S.fence()
            with ExitStack() as l1:
                def sb1(name, shape, dt=F32):
                    return sb(name, shape, dt, l1)
                cqn = sb1("cqn", [128, 4, T], BF16)
                ckvT = sb1("ckvT", [128, 2, T], BF16)
                ckv_tok = sb1("ckv_tok", [128, 16, 256], BF16)
                kdup = sb1("kdup", [128, T], BF16)
                widx = sb1("widx", [128, 256])
                maskT = sb1("maskT", [128, 16, T], BF16)
                b_cqn, b_ckvT, b_ckvtok, b_kdup, b_widx, b_maskT = Buf(), Buf(), Buf(), Buf(), Buf(), Buf()
                dma("sp", cqn[:, :, :], cqnT_d[:, :].rearrange("(c p) t -> p c t", p=128), [b_cqnT_d], [b_cqn])
                dma("sp", ckvT[:, :, :], ckvT_d[:, :].rearrange("(c p) t -> p c t", p=128), [b_ckvT_d], [b_ckvT])
                dma("sp", kdup[:, :], kdup_d[:, :], [b_kdup_d], [b_kdup])
                dma("sp", widx[:, :], widx_d[:, :], [b_widx_d], [b_widx])
                pring = Ring([psb[0], psb[1], psb[7]])
                pring.bufs = [psbuf[0], psbuf[1], psbuf[7]]
                for j in range(16):
                    ps, pb = pring.next()
                    pv = ps[:, 0:128].bitcast(BF16)
                    for cc in range(2):
                        tr(pv[:, cc * 128:(cc + 1) * 128], ckvT[:, cc, j * 128:(j + 1) * 128], ident[:], [b_ckvT, b_c], [pb])
                    acopy(ckv_tok[:, j, :], pv, [pb], [b_ckvtok])

                with ExitStack() as l1b:
                    def sbi(name, shape, dt=F32):
                        return sb(name, shape, dt, l1b)
                    qidxT = sbi("qidxT", [128, 8, T], BF16)
                    b_qidx = Buf()
                    scr = Ring([sbi("sc%d" % i, [128, T]) for i in range(3)])
                    rlr = Ring([sbi("rl%d" % i, [128, 512]) for i in range(2)])
                    mkr = Ring([sbi("mk%d" % i, [128, T], BF16) for i in range(2)])
                    junk = sbi("junk", [128, T], BF16)
                    b_junk = Buf()
                    cneg = sbi("cneg", [128, 128])
                    b_cneg = Buf()
                    NIT = 26
                    bis = Ring([sbi("bis%d" % i, [128, 72]) for i in range(2)])
                    memset(cneg[:], 0.0, [b_cneg])
                    S.op("pool", lambda h: h.affine_select(out=cneg[:], in_=cneg[:], pattern=[[-1, 128]], compare_op=ALU.is_ge,
                                                            fill=-1e30, base=0, channel_multiplier=1), [b_cneg], [b_cneg])
                    for cp in range(4):
                        wt, wb = load_w(w_idxq[:, cp * 256:(cp + 1) * 256], 4, 256)
                        for qq in range(2):
                            for R in range(4):
                                ps, pb = pring.next()
                                for k in range(4):
                                    mm(ps[:, :], wt[:, k, qq * 128:(qq + 1) * 128], cqn[:, k, R * 512:(R + 1) * 512],
                                       k == 0, k == 3, [wb, b_cqn], [pb])
                                acopy(qidxT[:, cp * 2 + qq, R * 512:(R + 1) * 512], ps[:, :], [pb], [b_qidx])
                    bconst = sbi("bconst", [128, 64])
                    b_bconst = Buf()
                    for k in range(NIT):
                        memset(bconst[:, k:k + 1], -(2.0 ** -(k + 2)), [b_bconst])
                    for i in range(16):
                        memset(bconst[:, 32 + i:33 + i], 0.5 - (512.0 - (i + 1) * 128), [b_bconst])
                    ftab = sbi("ftab", [128, 32])
                    for k in range(NIT + 1):
                        memset(ftab[:, k:k + 1], 2.0 ** -(k + 1), [b_bconst])
                    junk2 = sbi("junk2", [128, T], BF16)
                    b_junk2 = Buf()

                    def scores(i):
                        Wi = (i + 1) * 128
                        sc, bsc = scr.next()
                        for Q in range((Wi + 511) // 512):
                            wq = min(512, Wi - Q * 512)
                            qs = slice(Q * 512, Q * 512 + wq)
                            for hd in range(16):
                                ip, hf = hd // 2, hd % 2
                                ps, pb = pring.next()
                                mm(ps[:, 0:wq], qidxT[hf * 64:(hf + 1) * 64, ip, i * 128:(i + 1) * 128],
                                   kdup[hf * 64:(hf + 1) * 64, qs], True, True, [b_qidx, b_kdup], [pb])
                                if hd == 0:
                                    ts(sc[:, qs], ps[:, 0:wq], 0.0, widx[:, i * 16:i * 16 + 1], ALU.max, ALU.mult, [pb, b_widx], [bsc])
                                else:
                                    rl, brl = rlr.next()
                                    ts(rl[:, 0:wq], ps[:, 0:wq], 0.0, widx[:, i * 16 + hd:i * 16 + hd + 1], ALU.max, ALU.mult,
                                       [pb, b_widx], [brl])
                                    tt(sc[:, qs], sc[:, qs], rl[:, 0:wq], ALU.add, [brl, bsc], [bsc])
                        tt(sc[:, i * 128:Wi], sc[:, i * 128:Wi], cneg[:], ALU.add, [bsc, b_cneg], [bsc])
                        return sc, bsc

                    def bisect(i, sc, bsc, use_act):
                        Wi = (i + 1) * 128
                        bt, bbt = bis.next()
                        lo = bt[:, 0:1]
                        if i < 2:
                            memset(lo, -1e29, [bbt], eng="dve")
                            return lambda: (lo, bbt)
                        hi = bt[:, 1:2]
                        w0 = bt[:, 2:3]
                        mid = bt[:, 3:4]
                        stp = bt[:, 4:5]
                        tab = bt[:, 40:40 + NIT + 1]
                        S.op("dve", lambda h: h.tensor_reduce(out=hi, in_=sc[:, 0:Wi], axis=mybir.AxisListType.X, op=ALU.max), [bsc], [bbt])
                        S.op("dve", lambda h: h.tensor_reduce(out=lo, in_=sc[:, 0:i * 128], axis=mybir.AxisListType.X, op=ALU.min), [bsc], [bbt])
                        tt(w0, hi, lo, ALU.subtract, [bbt], [bbt])
                        memset(bt[:, 8:8 + NIT], 0.0, [bbt], eng="dve")
                        if use_act:
                            ts(tab, ftab[:, 0:NIT + 1], w0, -1.0, ALU.mult, ALU.mult, [bbt, b_bconst], [bbt])
                            stt(mid, lo, -1.0, tab[:, 0:1], ALU.mult, ALU.add, [bbt], [bbt])
                        else:
                            ts(tab, ftab[:, 0:NIT + 1], w0, None, ALU.mult, None, [bbt, b_bconst], [bbt])
                            tt(mid, lo, tab[:, 0:1], ALU.add, [bbt], [bbt])
                        return lambda: bisect_loop(i, sc, bsc, use_act, bt, bbt)

                    def bisect_loop(i, sc, bsc, use_act, bt, bbt):
                        Wi = (i + 1) * 128
                        lo = bt[:, 0:1]
                        mid = bt[:, 3:4]
                        stp = bt[:, 4:5]
                        tab = bt[:, 40:40 + NIT + 1]
                        if not use_act:
                            for k in range(NIT):
                                ts(junk[:, 0:Wi], sc[:, 0:Wi], mid, 0.0, ALU.is_ge, ALU.add, [bsc, bbt], [b_junk, bbt],
                                   accum=bt[:, 8 + k:9 + k])
                                ts(stp, bt[:, 8 + k:9 + k], 256.0, 0.5, ALU.is_ge, ALU.subtract, [bbt], [bbt])
                                stt(mid, stp, tab[:, k:k + 1], mid, ALU.mult, ALU.add, [bbt], [bbt])
                            tt(lo, mid, tab[:, NIT:NIT + 1], ALU.subtract, [bbt], [bbt])
                        else:
                            for k in range(NIT):
                                S.op("act", lambda h, k=k: h.activation(out=junk2[:, 0:Wi], in_=sc[:, 0:Wi], func=AF.Sign, bias=mid, scale=1.0,
                                                                       accum_out=bt[:, 8 + k:9 + k]), [bsc, bbt], [b_junk2, bbt])
                                act(stp, bt[:, 8 + k:9 + k], AF.Sign, [bbt, b_bconst], [bbt], bias=bconst[:, 32 + i:33 + i])
                                act(mid, stp, AF.Identity, [bbt], [bbt], bias=mid, scale=tab[:, k + 1:k + 2])
                            stt(lo, mid, -1.0, tab[:, NIT:NIT + 1], ALU.mult, ALU.add, [bbt], [bbt])
                        return lo, bbt

                    def finish(i, sc, bsc, lo, bbt):
                        Wi = (i + 1) * 128
                        mk, bmk = mkr.next()
                        ts(mk[:, 0:Wi], sc[:, 0:Wi], lo, None, ALU.is_ge, None, [bsc, bbt], [bmk])
                        for j0 in range(0, i + 1, 4):
                            n = min(4, i + 1 - j0)
                            ps, pb = pring.next()
                            pv = ps[:, 0:256].bitcast(BF16)
                            for jj in range(n):
                                tr(pv[:, jj * 128:(jj + 1) * 128], mk[:, (j0 + jj) * 128:(j0 + jj + 1) * 128], ident[:], [bmk, b_c], [pb])
                            acopy(maskT[:, j0:j0 + n, i * 128:(i + 1) * 128], pv[:, 0:n * 128].rearrange("p (a b) -> p a b", a=n),
                                  [pb], [b_maskT])

                    scq = {0: scores(0), 1: scores(1)}
                    for i in range(16):
                        sa = scq.pop(i)
                        fa = bisect(i, sa[0], sa[1], True)
                        if i + 2 < 16:
                            scq[i + 2] = scores(i + 2)
                        la = fa()
                        finish(i, sa[0], sa[1], la[0], la[1])

                S.fence()
                with ExitStack() as l1c:
                    def sbc(name, shape, dt=F32):
                        return sb(name, shape, dt, l1c)
                    wqr = Ring([sbc("wq%d" % i, [128, 4, 128], BF16) for i in range(2)])
                    wkr = Ring([sbc("wk%d" % i, [128, 256], BF16) for i in range(2)])
                    wvr = Ring([sbc("wv%d" % i, [128, 2, 128], BF16) for i in range(2)])
                    qTr = Ring([sbc("qT%d" % i, [128, T], BF16) for i in range(2)])
                    qlr = Ring([sbc("ql%d" % i, [128, 2, T], BF16) for i in range(2)])
                    er = Ring([sbc("e%d" % i, [128, 512], BF16) for i in range(4)])
                    pTr = Ring([sbc("pT%d" % i, [128, 512], BF16) for i in range(4)])
                    rden = sbc("rden", [128, 512])
                    evr = Ring([sbc("ev%d" % i, [128, 3, 512]) for i in range(2)])
                    b_rden = Buf()
                    olr = Ring([sbc("oln%d" % i, [128, 2, 512], BF16) for i in range(2)])
                    gtr = Ring([sbc("gt%d" % i, [128, 512], BF16) for i in range(2)])
                    ogr = Ring([sbc("og%d" % i, [128, 512], BF16) for i in range(2)])
                    pring = Ring([psb[0], psb[1]])
                    pring.bufs = [psbuf[0], psbuf[1]]
                    plog = Ring([psb[2], psb[3], psb[7]])
                    plog.bufs = [psbuf[2], psbuf[3], psbuf[7]]
                    PA = [psb[4], psb[5]]
                    PD = psb[6]
                    SCALE = 128.0 ** -0.5
                    LAG = 2

                    def head_proj(hd):
                        wq_, bwq = wqr.next()
                        dma("pool", wq_[:, :, :], w_uq[:, hd * 128:(hd + 1) * 128].rearrange("(c p) n -> p c n", p=128), [], [bwq])
                        wk_, bwk = wkr.next()
                        dma("pool", wk_[:, :], w_ukT[hd * 128:(hd + 1) * 128, :], [], [bwk])
                        wv_, bwv = wvr.next()
                        dma("pool", wv_[:, :, :], w_uv[hd * 256:(hd + 1) * 256, :].rearrange("(c p) n -> p c n", p=128), [], [bwv])
                        qT, bqT = qTr.next()
                        ql, bql = qlr.next()
                        res = (ql, bql, wv_, bwv)
                        yield res
                        for R in range(4):
                            ps, pb = pring.next()
                            for k in range(4):
                                mm(ps[:, :], wq_[:, k, :], cqn[:, k, R * 512:(R + 1) * 512], k == 0, k == 3, [bwq, b_cqn], [pb])
                            acopy(qT[:, R * 512:(R + 1) * 512], ps[:, :], [pb], [bqT])
                            yield res
                        for cc in range(2):
                            for R in range(4):
                                ps, pb = pring.next()
                                mm(ps[:, :], wk_[:, cc * 128:(cc + 1) * 128], qT[:, R * 512:(R + 1) * 512], True, True, [bwk, bqT], [pb])
                                vcopy(ql[:, cc, R * 512:(R + 1) * 512], ps[:, :], [pb], [bql])
                                yield res

                    tuples = [(R, j) for R in range(4) for j in range(4 * R + 4)]
                    n = len(tuples)
                    nxt = None
                    deferred = []
                    for nxt in head_proj(0):
                        pass
                    for hd in range(32):
                        ql, bql, wv_, bwv = nxt
                        gen = head_proj(hd + 1) if hd + 1 < 32 else None
                        pts = {}

                        def stage_qk(t, ql=ql, bql=bql):
                            R, j = tuples[t]
                            r1 = (R + 1) * 512
                            tlo = max(j * 128, R * 512)
                            off = tlo - R * 512
                            pl, bpl = plog.next()
                            mm(pl[:, off:512], ckvT[:, 0, j * 128:(j + 1) * 128], ql[:, 0, tlo:r1], True, False, [b_ckvT, bql], [bpl])
                            mm(pl[:, off:512], ckvT[:, 1, j * 128:(j + 1) * 128], ql[:, 1, tlo:r1], False, True, [b_ckvT, bql], [bpl])
                            e_, be = er.next()
                            act(e_[:, off:512], pl[:, off:512], AF.Exp, [bpl], [be], scale=SCALE)
                            pT, bpT = pTr.next()
                            tt(pT[:, off:512], e_[:, off:512], maskT[:, j, tlo:r1], ALU.mult, [be, b_maskT], [bpT])
                            pts[t] = (pT, bpT, off)

                        def stage_pv(t, wv_=wv_, bwv=bwv, hd=hd):
                            R, j = tuples[t]
                            r1 = (R + 1) * 512
                            nj = 4 * R + 4
                            pT, bpT, off = pts.pop(t)
                            for cc in range(2):
                                mm(PA[cc][:, off:512], ckv_tok[:, j, cc * 128:(cc + 1) * 128], pT[:, off:512], j == 0, j == nj - 1,
                                   [b_ckvtok, bpT], [psbuf[4 + cc]])
                            mm(PD[:, off:512], onesb[:], pT[:, off:512], j == 0, j == nj - 1, [b_c, bpT], [psbuf[6]])
                            if j == nj - 1:
                                ev, bev = evr.next()
                                acopy(ev[:, 2, :], PD[:, :], [psbuf[6]], [bev])
                                vcopy(ev[:, 0, :], PA[0][:, :], [psbuf[4]], [bev])
                                acopy(ev[:, 1, :], PA[1][:, :], [psbuf[5]], [bev])
                                ol, bol = olr.next()
                                st_ = {}

                                def d_ln(ev=ev, bev=bev):
                                    act(rden[:], ev[:, 2, :], AF.Ln, [bev], [b_rden])

                                def d_exp():
                                    act(rden[:], rden[:], AF.Exp, [b_rden], [b_rden], scale=-1.0)

                                def d_ol(cc, ev=ev, bev=bev, ol=ol, bol=bol):
                                    tt(ol[:, cc, :], ev[:, cc, :], rden[:], ALU.mult, [bev, b_rden], [bol])

                                def d_oproj(ol=ol, bol=bol, wv_=wv_, bwv=bwv, hd=hd, R=R, r1=r1, st_=st_):
                                    ps, pb = pring.next()
                                    for cc in range(2):
                                        mm(ps[:, :], wv_[:, cc, :], ol[:, cc, :], cc == 0, cc == 1, [bwv, bol], [pb])
                                    gt_, bgt = gtr.next()
                                    dma("sp", gt_[:], gateT_d[hd * 128:(hd + 1) * 128, R * 512:r1], [b_gateT], [bgt])
                                    st_["v"] = (ps, pb, gt_, bgt)

                                def d_og(hd=hd, R=R, r1=r1, st_=st_):
                                    ps, pb, gt_, bgt = st_["v"]
                                    og, bog = ogr.next()
                                    tt(og[:], ps[:, :], gt_[:], ALU.mult, [pb, bgt], [bog])
                                    dma("sp", ygT_d[hd * 128:(hd + 1) * 128, R * 512:r1], og[:], [bog], [b_ygT])

                                deferred.extend([d_ln, d_exp, lambda: d_ol(0), lambda: d_ol(1), d_oproj, d_og])

                        for step in range(n + LAG):
                            if step < n:
                                stage_qk(step)
                            if step - LAG >= 0:
                                stage_pv(step - LAG)
                            if deferred:
                                deferred.pop(0)()
                            if gen is not None and step >= 6 and step % 2 == 0:
                                nxt = next(gen, nxt)
                        if gen is not None:
                            for nxt in gen:
                                pass
                    while deferred:
                        deferred.pop(0)()

            S.fence()
            with ExitStack() as l1d:
                outproj_ln(l1d, ygT_d, b_ygT, w_out1, x1T_d, b_x1T, 1, outT_d, b_outT, True, False)
        else:
            xstk.close()

        S.emit()
    return nc


_CACHE = {}


def _feat(v, nchunk):
    return np.ascontiguousarray(np.asarray(v, np.float32).reshape(nchunk, 128).T)


def kernel(x, ssd_w_in, ssd_conv_w, ssd_conv_b, ssd_dt_bias, ssd_a_log, ssd_d_skip, ssd_norm_g, ssd_w_out,
           dsa_w_in, dsa_q_norm_g, dsa_kv_norm_g, dsa_w_uq, dsa_w_uk, dsa_w_uv, dsa_w_idx_q, dsa_w_out,
           ln_g, ln_b, _upto="all"):
    f = lambda a: np.ascontiguousarray(np.asarray(a, np.float32))
    x = f(x)
    nb = x.shape[0]
    shared = {
        "w_in0": f(ssd_w_in[0]),
        "convw": np.ascontiguousarray(f(ssd_conv_w[0]).T.reshape(48, 128, 4).transpose(1, 0, 2).reshape(128, 192)),
        "convb": _feat(ssd_conv_b[0], 48),
        "dtb": np.ascontiguousarray(np.tile(f(ssd_dt_bias[0])[None, :], (128, 1))),
        "alog": np.ascontiguousarray(np.tile(f(ssd_a_log[0])[None, :], (128, 1))),
        "dskip": _feat(np.repeat(f(ssd_d_skip[0]), 64), 32),
        "normg": _feat(ssd_norm_g[0], 32),
        "w_out0": f(ssd_w_out[0]),
        "lng": np.ascontiguousarray(np.concatenate([_feat(ln_g[0], 16), _feat(ln_g[1], 16)], 1)),
        "lnb": np.ascontiguousarray(np.concatenate([_feat(ln_b[0], 16), _feat(ln_b[1], 16)], 1)),
        "w_in1": f(dsa_w_in[0]),
        "qng": _feat(dsa_q_norm_g[0], 4),
        "kvng": _feat(dsa_kv_norm_g[0], 2),
        "w_uq": f(dsa_w_uq[0]),
        "w_ukT": np.ascontiguousarray(f(dsa_w_uk[0]).transpose(0, 2, 1).reshape(32 * 128, 256)),
        "w_uv": np.ascontiguousarray(f(dsa_w_uv[0]).reshape(32 * 256, 128)),
        "w_idxq": f(dsa_w_idx_q[0]),
        "w_out1": f(dsa_w_out[0]),
    }
    if _upto not in _CACHE:
        _CACHE[_upto] = build(_upto)
    nc = _CACHE[_upto]
    in_maps = []
    for b in range(nb):
        m = dict(shared)
        m["xT"] = np.ascontiguousarray(x[b].T)
        in_maps.append(m)
    res = run_bass_kernel_spmd(nc, in_maps, core_ids=list(range(nb)))
    out = np.stack([np.ascontiguousarray(np.asarray(r["outT"], np.float32).T) for r in res.results], 0)
    return out
```
